# Optimizing a Trainium2 kernel written in Bass

```python
import math
import jax, jax.numpy as jnp
from jax import lax
import numpy as np

D_MODEL = 1024
BATCH = 2
SEQ = 8192
DEPTH = 2

EXPAND = 2
D_INNER = EXPAND * D_MODEL
S5_WIDTH = D_INNER // 2
S5_GROUP = 16
S5_GROUPS = S5_WIDTH // S5_GROUP
S5_STATE = 64
RET_WIDTH = D_INNER - S5_WIDTH
RET_HEADS = 4
RET_DK = RET_WIDTH // RET_HEADS
RET_DV = RET_WIDTH // RET_HEADS
RET_CHUNK = 128
ROPE_BASE = 10000.0
EVEN_SPLITS = [S5_WIDTH, S5_WIDTH, RET_HEADS * RET_DK, RET_HEADS * RET_DK, RET_HEADS * RET_DV, RET_HEADS * RET_DV]
EVEN_IN = sum(EVEN_SPLITS)
SGU_WIDTH = D_INNER
SGU_GROUPS = 4
SGU_GROUP_DIM = SGU_WIDTH // SGU_GROUPS
SGU_CHUNK = 128
ODD_IN = 3 * SGU_WIDTH
N_EVEN = (DEPTH + 1) // 2
N_ODD = DEPTH // 2
NORM_EPS = 1e-6

kernel_name = "hybrid_s5_retention_sgu_block"

F32 = jnp.float32


def rms_norm(x, g):
    xf = x.astype(F32)
    y = xf * lax.rsqrt(jnp.mean(xf * xf, axis=-1, keepdims=True) + NORM_EPS)
    return (y * g.astype(F32)).astype(x.dtype)


def rotary(x, pos):
    half = x.shape[-1] // 2
    inv = ROPE_BASE ** (-jnp.arange(half, dtype=F32) / half)
    ang = pos[:, None] * inv[None, :]
    cos = jnp.cos(ang)[None, :, None, :]
    sin = jnp.sin(ang)[None, :, None, :]
    x1, x2 = x[..., :half], x[..., half:]
    return jnp.concatenate([x1 * cos - x2 * sin, x1 * sin + x2 * cos], axis=-1)


def s5_branch(u, lam_re, lam_im, log_dt, b_re, b_im, c_re, c_im, d_skip, w_glu, b_glu):
    bsz, seq, _ = u.shape
    uf = u.astype(F32).reshape(bsz, seq, S5_GROUPS, S5_GROUP)
    lr = jnp.minimum(lam_re.astype(F32), -1e-4)
    li = lam_im.astype(F32)
    dt = jnp.exp(log_dt.astype(F32))[:, None]
    mag = jnp.exp(lr * dt)
    ab_re = mag * jnp.cos(li * dt)
    ab_im = mag * jnp.sin(li * dt)
    den = lr * lr + li * li
    n_re = ab_re - 1.0
    n_im = ab_im
    z_re = (n_re * lr + n_im * li) / den
    z_im = (n_im * lr - n_re * li) / den
    br = b_re.astype(F32)
    bi = b_im.astype(F32)
    bb_re = z_re[..., None] * br - z_im[..., None] * bi
    bb_im = z_re[..., None] * bi + z_im[..., None] * br
    bu_re = jnp.einsum('gph,blgh->blgp', bb_re, uf)
    bu_im = jnp.einsum('gph,blgh->blgp', bb_im, uf)
    a_re = jnp.broadcast_to(ab_re, bu_re.shape)
    a_im = jnp.broadcast_to(ab_im, bu_im.shape)

    def combine(left, right):
        a1r, a1i, b1r, b1i = left
        a2r, a2i, b2r, b2i = right
        return (a2r * a1r - a2i * a1i,
                a2r * a1i + a2i * a1r,
                a2r * b1r - a2i * b1i + b2r,
                a2r * b1i + a2i * b1r + b2i)

    _, _, s_re, s_im = lax.associative_scan(combine, (a_re, a_im, bu_re, bu_im), axis=1)
    y = (jnp.einsum('ghp,blgp->blgh', c_re.astype(F32), s_re)
         - jnp.einsum('ghp,blgp->blgh', c_im.astype(F32), s_im))
    y = y + d_skip.astype(F32).reshape(S5_GROUPS, S5_GROUP) * uf
    y = jax.nn.gelu(y.reshape(bsz, seq, S5_WIDTH))
    y = y * jax.nn.sigmoid(y @ w_glu.astype(F32) + b_glu.astype(F32))
    return y.astype(u.dtype)


def retention_branch(q, k, v, gn_gain):
    bsz, seq, _ = q.shape
    nc = seq // RET_CHUNK
    pos = jnp.arange(seq, dtype=F32)
    qh = rotary(q.astype(F32).reshape(bsz, seq, RET_HEADS, RET_DK), pos)
    kh = rotary(k.astype(F32).reshape(bsz, seq, RET_HEADS, RET_DK), pos) * (RET_DK ** -0.5)
    vh = v.astype(F32).reshape(bsz, seq, RET_HEADS, RET_DV)
    log_g = jnp.log1p(-jnp.exp2(-5.0 - jnp.arange(RET_HEADS, dtype=F32)))
    idx = jnp.arange(RET_CHUNK, dtype=F32)
    diff = idx[:, None] - idx[None, :]
    decay = jnp.where(diff >= 0, jnp.exp(log_g[:, None, None] * jnp.maximum(diff, 0.0)), 0.0)
    xi = jnp.exp(log_g[None, :] * (idx[:, None] + 1.0))
    zeta = jnp.exp(log_g[None, :] * (RET_CHUNK - 1.0 - idx[:, None]))
    chunk_decay = jnp.exp(log_g * RET_CHUNK)
    qc = qh.reshape(bsz, nc, RET_CHUNK, RET_HEADS, RET_DK)
    kc = kh.reshape(bsz, nc, RET_CHUNK, RET_HEADS, RET_DK)
    vc = vh.reshape(bsz, nc, RET_CHUNK, RET_HEADS, RET_DV)
    scores = jnp.einsum('bcnhk,bcmhk->bchnm', qc, kc) * decay[None, None]
    inner = jnp.einsum('bchnm,bcmhv->bcnhv', scores, vc)
    local = jnp.einsum('bcmhk,bcmhv->bchkv', kc * zeta[None, None, :, :, None], vc)

    def step(state, s_chunk):
        return state * chunk_decay[None, :, None, None] + s_chunk, state

    init = jnp.zeros((bsz, RET_HEADS, RET_DK, RET_DV), F32)
    _, prev = lax.scan(step, init, jnp.moveaxis(local, 1, 0))
    prev = jnp.moveaxis(prev, 0, 1)
    cross = jnp.einsum('bcnhk,bchkv->bcnhv', qc * xi[None, None, :, :, None], prev)
    o = (inner + cross).reshape(bsz, seq, RET_HEADS, RET_DV)
    mu = jnp.mean(o, axis=-1, keepdims=True)
    var = jnp.mean(jnp.square(o - mu), axis=-1, keepdims=True)
    o = (o - mu) * lax.rsqrt(var + NORM_EPS)
    o = o.reshape(bsz, seq, RET_HEADS * RET_DV) * gn_gain.astype(F32)
    return o.astype(q.dtype)


def spatial_gating_branch(h, v_gain, w_s, b_s):
    bsz, seq, _ = h.shape
    nc = seq // SGU_CHUNK
    u, v = h[..., :SGU_WIDTH], h[..., SGU_WIDTH:]
    vf = v.astype(F32)
    mu = jnp.mean(vf, axis=-1, keepdims=True)
    var = jnp.mean(jnp.square(vf - mu), axis=-1, keepdims=True)
    vf = (vf - mu) * lax.rsqrt(var + NORM_EPS) * v_gain.astype(F32)
    vc = vf.reshape(bsz, nc, SGU_CHUNK, SGU_GROUPS, SGU_GROUP_DIM)
    mask = jnp.tril(jnp.ones((SGU_CHUNK, SGU_CHUNK), dtype=bool))
    w = jnp.where(mask[None], w_s.astype(F32), 0.0)
    s = jnp.einsum('gts,bcsgd->bctgd', w, vc) + b_s.astype(F32).T[None, None, :, :, None]
    return (u.astype(F32) * s.reshape(bsz, seq, SGU_WIDTH)).astype(h.dtype)


def setup_inputs(seed: int = 0) -> dict:
    key = jax.random.key(seed)
    ks = jax.random.split(key, 24)
    nrm = lambda k, shape, scale: jax.random.normal(k, shape, F32) * scale
    x = jax.random.normal(ks[0], (BATCH, SEQ, D_MODEL), F32)
    norm_even = 1.0 + nrm(ks[1], (N_EVEN, D_MODEL), 0.02)
    w_in_even = nrm(ks[2], (N_EVEN, D_MODEL, EVEN_IN), D_MODEL ** -0.5)
    s5_lam_re = -0.5 + nrm(ks[3], (N_EVEN, S5_GROUPS, S5_STATE), 0.01)
    s5_lam_im = math.pi * jnp.arange(S5_STATE, dtype=F32)[None, None, :] + nrm(ks[4], (N_EVEN, S5_GROUPS, S5_STATE), 0.01)
    s5_log_dt = jax.random.uniform(ks[5], (N_EVEN, S5_GROUPS), F32, math.log(0.001), math.log(0.1))
    s5_b_re = nrm(ks[6], (N_EVEN, S5_GROUPS, S5_STATE, S5_GROUP), (2 * S5_GROUP) ** -0.5)
    s5_b_im = nrm(ks[7], (N_EVEN, S5_GROUPS, S5_STATE, S5_GROUP), (2 * S5_GROUP) ** -0.5)
    s5_c_re = nrm(ks[8], (N_EVEN, S5_GROUPS, S5_GROUP, S5_STATE), (2 * S5_STATE) ** -0.5)
    s5_c_im = nrm(ks[9], (N_EVEN, S5_GROUPS, S5_GROUP, S5_STATE), (2 * S5_STATE) ** -0.5)
    s5_d = nrm(ks[10], (N_EVEN, S5_WIDTH), 1.0)
    s5_w_glu = nrm(ks[11], (N_EVEN, S5_WIDTH, S5_WIDTH), S5_WIDTH ** -0.5)
    s5_b_glu = nrm(ks[12], (N_EVEN, S5_WIDTH), 0.01)
    ret_gn_gain = 1.0 + nrm(ks[13], (N_EVEN, RET_HEADS * RET_DV), 0.02)
    w_out_even = nrm(ks[14], (N_EVEN, D_INNER, D_MODEL), D_INNER ** -0.5)
    norm_odd = 1.0 + nrm(ks[15], (N_ODD, D_MODEL), 0.02)
    w_in_odd = nrm(ks[16], (N_ODD, D_MODEL, ODD_IN), D_MODEL ** -0.5)
    sgu_norm_gain = 1.0 + nrm(ks[17], (N_ODD, SGU_WIDTH), 0.02)
    sgu_w_spatial = nrm(ks[18], (N_ODD, SGU_GROUPS, SGU_CHUNK, SGU_CHUNK), SGU_CHUNK ** -0.5)
    sgu_b_spatial = 1.0 + nrm(ks[19], (N_ODD, SGU_GROUPS, SGU_CHUNK), 0.02)
    w_out_odd = nrm(ks[20], (N_ODD, SGU_WIDTH, D_MODEL), SGU_WIDTH ** -0.5)
    final_norm = 1.0 + nrm(ks[21], (D_MODEL,), 0.02)
    return {"x": x, "norm_even": norm_even, "w_in_even": w_in_even,
            "s5_lam_re": s5_lam_re, "s5_lam_im": s5_lam_im, "s5_log_dt": s5_log_dt,
            "s5_b_re": s5_b_re, "s5_b_im": s5_b_im, "s5_c_re": s5_c_re, "s5_c_im": s5_c_im,
            "s5_d": s5_d, "s5_w_glu": s5_w_glu, "s5_b_glu": s5_b_glu,
            "ret_gn_gain": ret_gn_gain, "w_out_even": w_out_even,
            "norm_odd": norm_odd, "w_in_odd": w_in_odd, "sgu_norm_gain": sgu_norm_gain,
            "sgu_w_spatial": sgu_w_spatial, "sgu_b_spatial": sgu_b_spatial,
            "w_out_odd": w_out_odd, "final_norm": final_norm}


def reference(x, norm_even, w_in_even, s5_lam_re, s5_lam_im, s5_log_dt, s5_b_re, s5_b_im,
              s5_c_re, s5_c_im, s5_d, s5_w_glu, s5_b_glu, ret_gn_gain, w_out_even,
              norm_odd, w_in_odd, sgu_norm_gain, sgu_w_spatial, sgu_b_spatial, w_out_odd,
              final_norm):
    split_pts = [int(p) for p in np.cumsum(EVEN_SPLITS)[:-1]]
    for layer in range(DEPTH):
        i = layer // 2
        if layer % 2 == 0:
            h = rms_norm(x, norm_even[i])
            p = h @ w_in_even[i]
            a_u, a_z, q, k, v, b_z = jnp.split(p, split_pts, axis=-1)
            ya = s5_branch(a_u, s5_lam_re[i], s5_lam_im[i], s5_log_dt[i], s5_b_re[i], s5_b_im[i],
                           s5_c_re[i], s5_c_im[i], s5_d[i], s5_w_glu[i], s5_b_glu[i]) * jax.nn.silu(a_z)
            yb = retention_branch(q, k, v, ret_gn_gain[i]) * jax.nn.silu(b_z)
            x = x + jnp.concatenate([ya, yb], axis=-1) @ w_out_even[i]
        else:
            h = rms_norm(x, norm_odd[i])
            p = h @ w_in_odd[i]
            hz = jax.nn.gelu(p[..., :2 * SGU_WIDTH])
            z = p[..., 2 * SGU_WIDTH:]
            y = spatial_gating_branch(hz, sgu_norm_gain[i], sgu_w_spatial[i], sgu_b_spatial[i]) * jax.nn.silu(z)
            x = x + y @ w_out_odd[i]
    return rms_norm(x, final_norm)
```

```python
import math
import os
import numpy as np
SKIP = {k_: True for k_ in os.environ.get('KSKIP', '').split(',') if k_}
import ml_dtypes
import concourse.bass as bass
import concourse.mybir as mybir
from concourse.bass_utils import run_bass_kernel_spmd

F32 = mybir.dt.float32
BF16 = mybir.dt.bfloat16
AF = mybir.ActivationFunctionType
ALU = mybir.AluOpType
AX = mybir.AxisListType

D = 1024
SEG = 2048
NCH = SEG // 128
NSEG = 4
EPS = 1e-6
PE, ACT, DVE, POOL, SP = 0, 1, 2, 3, 4


class Buf:
    __slots__ = ("name", "w", "r", "dsem", "dcnt")

    def __init__(self, name):
        self.name = name
        self.w = None
        self.r = []
        self.dsem = None
        self.dcnt = 0


class Rec:
    def __init__(self, nc):
        self.nc = nc
        self.ops = []
        self.engs = [nc.tensor, nc.scalar, nc.vector, nc.gpsimd, nc.sync]

    def op(self, eng, fn, reads=(), writes=(), dma=False):
        idx = len(self.ops)
        deps = set()
        raw = set()
        for b in reads:
            if b.w is not None:
                deps.add(b.w)
                raw.add(b.w)
        for b in writes:
            if b.w is not None:
                deps.add(b.w)
            for r in b.r:
                deps.add(r)
        self.ops.append(dict(eng=eng, fn=fn, deps=deps, raw=raw, dma=dma, sig=False, dbuf=None))
        for b in reads:
            if not dma:
                b.r = [r for r in b.r if self.ops[r]["dma"] or self.ops[r]["eng"] != eng]
            b.r.append(idx)
        for b in writes:
            b.w = idx
            b.r = []
        return idx

    def barrier(self):
        n = len(self.ops)
        lb = getattr(self, "_lb", 0)
        deps = set()
        last = {}
        for j in range(lb, n):
            o = self.ops[j]
            if o["dma"]:
                deps.add(j)
            else:
                last[o["eng"]] = j
        deps.update(last.values())
        for e in range(5):
            eng = self.engs[e]
            self.ops.append(dict(eng=e, fn=(lambda eng=eng: eng.nop()), deps=set(deps), raw=set(), dma=False, sig=False, dbuf=None))
        self._lb = n

    def dma(self, fn, sbuf_side, reads=(), writes=(), eng=SP):
        idx = self.op(eng, fn, reads, writes, dma=True)
        self.ops[idx]["dbuf"] = sbuf_side
        return idx

    def emit(self):
        nc = self.nc
        ops = self.ops
        for i, o in enumerate(ops):
            for d in o["deps"]:
                p = ops[d]
                if p["dma"] or p["eng"] != o["eng"] or o["dma"] or o["eng"] != PE:
                    p["sig"] = True
        esem = [nc.alloc_semaphore("es%d" % i) for i in range(5)]
        ecnt = [0] * 5
        tok = [None] * len(ops)
        waited = [dict() for _ in range(5)]
        final = {}
        for i, o in enumerate(ops):
            e = o["eng"]
            eng = self.engs[e]
            need = {}
            for d in o["deps"]:
                p = ops[d]
                if (not p["dma"]) and p["eng"] == e and not o["dma"] and e == PE:
                    continue
                if (not p["dma"]) and p["eng"] == e and o["dma"] and e == SP:
                    continue
                s, v = tok[d]
                k = id(s)
                if k not in need or need[k][1] < v:
                    need[k] = (s, v)
            for k, (s, v) in need.items():
                if waited[e].get(k, 0) >= v:
                    continue
                eng.wait_ge(s, v)
                waited[e][k] = v
            ins = o["fn"]()
            if o["dma"]:
                b = o["dbuf"]
                if b.dsem is None or b.dcnt >= 800:
                    b.dcnt = 0
                    self._nsem = getattr(self, "_nsem", 0) + 1
                    b.dsem = nc.alloc_semaphore("d%d_%s" % (self._nsem, b.name))
                b.dcnt += 16
                ins.then_inc(b.dsem, 16)
                tok[i] = (b.dsem, b.dcnt)
                final[id(b.dsem)] = (b.dsem, b.dcnt)
            elif o["sig"]:
                if ecnt[e] >= 3000:
                    self._nsem = getattr(self, "_nsem", 0) + 1
                    esem[e] = nc.alloc_semaphore("es%d_%d" % (e, self._nsem))
                    ecnt[e] = 0
                ecnt[e] += 1
                ins.then_inc(esem[e], 1)
                tok[i] = (esem[e], ecnt[e])
            o["fn"] = None
        for k, (s, v) in final.items():
            if waited[SP].get(k, 0) < v:
                nc.sync.wait_ge(s, v)


class K:
    def __init__(self, nc, rec):
        self.nc = nc
        self.rec = rec
        self.nb = 0

    def sb(self, name, shape, dt):
        t = self.nc.alloc_sbuf_tensor(name, list(shape), dt).ap()
        return t

    def ps(self, name, shape, dt=F32):
        return self.nc.alloc_psum_tensor(name, list(shape), dt).ap()

    def buf(self, name=None):
        self.nb += 1
        return Buf(name or ("b%d" % self.nb))


def bcast_rows(ap_1d_dram, n):
    return ap_1d_dram.partition_broadcast(128)


def build(nc, mode="full"):
    rec = Rec(nc)
    k = K(nc, rec)
    dr = lambda name, shape, dt=F32, kind="ExternalInput": nc.dram_tensor(name, list(shape), dt, kind=kind).ap()
    x_seq = dr("x_seq", [NSEG * SEG, D])
    w_in1 = dr("w_in_odd", [D, 6144])
    w_out1 = dr("w_out_odd", [2048, D])
    norm_odd = dr("norm_odd", [D])
    sgu_gain = dr("sgu_norm_gain", [2048])
    sgu_w = dr("sgu_w_spatial", [4, 128, 128])
    sgu_b = dr("sgu_b_spatial", [4, 128])
    final_norm = dr("final_norm", [D])
    tril = dr("c_tril", [128, 128])
    ident_d = dr("c_ident", [128, 128])
    out = dr("out", [SEG, D], F32, kind="ExternalOutput")
    x1 = dr("x1_scratch", [SEG, D], F32, kind="Internal")
    x2 = dr("x2_scratch", [SEG, D], F32, kind="Internal")

    nct = nc
    R = rec

    ident_f = k.sb("ident_f", [128, 128], F32)
    ident_b = k.sb("ident_b", [128, 128], BF16)
    b_ident = k.buf("ident")
    R.dma(lambda: nc.sync.dma_start(out=ident_f, in_=ident_d), b_ident, writes=[b_ident])
    R.op(DVE, lambda: nc.vector.tensor_copy(out=ident_b, in_=ident_f), reads=[b_ident], writes=[b_ident])

    ARENA = 53248
    arena = k.sb("arena", [128, ARENA], BF16)
    apos = [0]

    def carve(shape, dt):
        n = 1
        for d_ in shape[1:]:
            n *= d_
        nb16 = n * (2 if dt == F32 else 1)
        a = apos[0]
        apos[0] += nb16
        assert apos[0] <= ARENA, apos[0]
        v = arena[:, a:a + nb16]
        if dt == F32:
            v = v.bitcast(F32)
        if len(shape) == 3:
            v = v.rearrange("p (a b) -> p a b", b=shape[2])
        return v

    def carve_at(off, shape, dt):
        n = 1
        for d_ in shape[1:]:
            n *= d_
        nb16 = n * (2 if dt == F32 else 1)
        assert off + nb16 <= ARENA, (off, nb16)
        v = arena[:, off:off + nb16]
        if dt == F32:
            v = v.bitcast(F32)
        if len(shape) == 3:
            v = v.rearrange("p (a b) -> p a b", b=shape[2])
        elif len(shape) == 4:
            v = v.rearrange("p (a b c) -> p a b c", b=shape[2], c=shape[3])
        return v

    xnT = k.sb("xnT", [128, 8, SEG], BF16)
    b_xnT = k.buf("xnT")
    pg = k.ps("pg", [128, 4, 512], F32)
    b_pg = [k.buf("pg%d" % i) for i in range(4)]
    yT = k.sb("yT", [128, 8, SEG], BF16)
    b_yT = k.buf("yT")

    xc = [k.sb("xc%d" % i, [128, D], F32) for i in range(2)]
    b_xc = [k.buf("xc%d" % i) for i in range(2)]
    xb = [k.sb("xb0", [128, D], BF16)] * 2
    b_xb = [k.buf("xb0")] * 2
    t1 = k.sb("t1", [128, 1024], F32)
    b_t1 = k.buf("t1")
    junk = t1[:, :D]
    b_junk = b_t1
    stat = [k.sb("stat%d" % i, [128, 8], F32) for i in range(2)]
    b_stat = [k.buf("stat%d" % i) for i in range(2)]
    pT = [k.ps("pT0", [128, 8, 128], BF16)] * 2
    pT2 = k.ps("pT2", [128, 8, 128], BF16)
    b_pT2 = k.buf("pT2")
    b_pT = [k.buf("pT0")] * 2
    b_ptq_fix = [b_pT[0], b_pT2]

    def norm_transpose(src_dram, row0):
        for c in range(NCH):
            i = c % 2
            rows = src_dram[row0 + c * 128: row0 + (c + 1) * 128, :]
            skey = (id(src_dram), c)
            R.dma(lambda i=i, rows=rows: nc.sync.dma_start(out=xc[i], in_=rows), b_xc[i], reads=[b_x[skey]] if (row0 == 0 and skey in b_x) else [], writes=[b_xc[i]])
            R.op(ACT, lambda i=i: nc.scalar.activation(out=junk, in_=xc[i], func=AF.Square, accum_out=stat[i][:, 0:1]),
                 reads=[b_xc[i]], writes=[b_junk, b_stat[i]])
            R.op(DVE, lambda i=i: nc.vector.tensor_scalar(out=stat[i][:, 1:2], in0=stat[i][:, 0:1], scalar1=1.0 / D, scalar2=EPS,
                                                          op0=ALU.mult, op1=ALU.add), reads=[b_stat[i]], writes=[b_stat[i]])
            R.op(ACT, lambda i=i: nc.scalar.activation(out=stat[i][:, 3:4], in_=stat[i][:, 1:2], func=AF.Sqrt), reads=[b_stat[i]], writes=[b_stat[i]])
            R.op(DVE, lambda i=i: nc.vector.reciprocal(out=stat[i][:, 2:3], in_=stat[i][:, 3:4]), reads=[b_stat[i]], writes=[b_stat[i]])
            R.op(DVE, lambda i=i: nc.vector.scalar_tensor_tensor(out=xb[i], in0=xc[i], scalar=stat[i][:, 2:3], in1=gbc, op0=ALU.mult, op1=ALU.mult),
                 reads=[b_xc[i], b_stat[i], b_gbc], writes=[b_xb[i]])
            for kt in range(8):
                R.op(PE, lambda i=i, kt=kt: nc.tensor.transpose(out=pT[i][:, kt, :], in_=xb[i][:, kt * 128:(kt + 1) * 128], identity=ident_b),
                     reads=[b_xb[i], b_ident], writes=[b_pT[i]])
            R.op(ACT, lambda i=i, c=c: nc.scalar.copy(out=xnT[:, :, c * 128:(c + 1) * 128], in_=pT[i]),
                 reads=[b_pT[i]], writes=[b_xnT])

    wbf = [k.sb("wbf%d" % i, [128, 8, 256], BF16) for i in range(3)]
    b_wbf = [k.buf("wbf%d" % i) for i in range(3)]
    gbc = k.sb("gbc", [128, D], F32)
    b_gbc = k.buf("gbc")
    wctr = [0]

    def load_gain(g_dram):
        R.dma(lambda: nc.sync.dma_start(out=gbc, in_=g_dram.partition_broadcast(128)), b_gbc, writes=[b_gbc])

    def load_wblock(w_dram, c0, ncols=256, gain=True):
        i = wctr[0] % 3
        wctr[0] += 1
        src = w_dram.rearrange("(kt p) n -> p kt n", p=128)[:, :, c0:c0 + ncols]
        R.dma(lambda: nc.gpsimd.dma_start(out=wbf[i][:, :, :ncols], in_=src), b_wbf[i], writes=[b_wbf[i]], eng=POOL)
        return wbf[i], b_wbf[i]

    pacc = [k.ps("pacc%d" % i, [128, 512], F32) for i in range(2)]
    b_pacc = [k.buf("pacc%d" % i) for i in range(2)]
    pctr = [0]

    wide = [True]

    def next_pacc():
        if wide[0]:
            i = pctr[0] % 6
            pctr[0] += 1
            if i < 2:
                return pacc[i], b_pacc[i]
            return pg[:, i - 2, :], b_pg[i - 2]
        i = pctr[0] % 2
        pctr[0] += 1
        return pacc[i], b_pacc[i]

    def layer1(src_dram, dst_dram):
        load_gain(norm_odd)
        norm_transpose(src_dram, 0)
        R.barrier()
        apos[0] = 0
        vgain = carve([128, 2048], F32)
        b_vgain = k.buf("vgain")
        R.dma(lambda: nc.sync.dma_start(out=vgain, in_=sgu_gain.partition_broadcast(128)), b_vgain, writes=[b_vgain])
        bsp = k.sb("bsp", [128, 4, 128], F32)
        b_bsp = k.buf("bsp")
        R.dma(lambda: nc.sync.dma_start(out=bsp, in_=sgu_b.partition_broadcast(128)), b_bsp, writes=[b_bsp])
        wraw = k.sb("wraw", [128, 4, 128], F32)
        b_wraw = k.buf("wraw")
        R.dma(lambda: nc.sync.dma_start(out=wraw, in_=sgu_w.rearrange("g t s -> t g s")), b_wraw, writes=[b_wraw])
        trl = k.sb("trl", [128, 128], F32)
        b_trl = k.buf("trl")
        R.dma(lambda: nc.sync.dma_start(out=trl, in_=tril), b_trl, writes=[b_trl])
        wmT = k.sb("wmT", [128, 4, 128], BF16)
        b_wmT = k.buf("wmT")
        ptw = pacc[0].rearrange("p (g t) -> p g t", t=128)
        b_ptw = b_pacc[0]
        for g in range(4):
            R.op(PE, lambda g=g: nc.tensor.transpose(out=ptw[:, g, :], in_=wraw[:, g, :], identity=ident_f),
                 reads=[b_wraw, b_ident], writes=[b_ptw])
        R.op(DVE, lambda: nc.vector.tensor_tensor(out=wmT, in0=ptw, in1=trl.unsqueeze(1).to_broadcast([128, 4, 128]), op=ALU.mult),
             reads=[b_ptw, b_trl], writes=[b_wmT])

        vn = carve([128, NCH, 2048], BF16)
        b_vn = k.buf("vn")
        vf = carve([128, 2048], F32)
        b_vf = k.buf("vf")
        bst = k.sb("bst", [128, 4, 6], F32)
        mv = k.sb("mv", [128, 4], F32)
        b_bst = k.buf("bst")

        def gelu_from(psrc, b_psrc, dst, b_dst, n):
            R.op(ACT, lambda: nc.scalar.activation(out=dst, in_=psrc, func=AF.Gelu_apprx_tanh), reads=[b_psrc], writes=[b_dst])

        uT1 = carve([128, SEG], BF16)
        b_uT1 = k.buf("uT1")
        zT1 = carve([128, SEG], BF16)
        b_zT1 = k.buf("zT1")
        vf2 = arena[:, apos[0] - 4096:apos[0]].bitcast(F32)
        vfs = [vf, vf2]
        b_vfs = [[b_vf], [b_uT1, b_zT1]]
        for blk in range(8):
            src = w_in1.rearrange("(kt p) n -> p kt n", p=128)[:, :, 2048 + blk * 256:2048 + (blk + 1) * 256]
            R.dma(lambda blk=blk, src=src: nc.gpsimd.dma_start(out=yT[:, :, blk * 256:(blk + 1) * 256], in_=src), b_yT, writes=[b_yT], eng=POOL)
        for c in range(NCH):
            vfc, bvf = vfs[c % 2], b_vfs[c % 2]
            for blk in range(8):
                pa, b_pa = next_pacc()
                for kt in range(8):
                    R.op(PE, lambda pa=pa, kt=kt, c=c, blk=blk: nc.tensor.matmul(pa[:, :256], lhsT=xnT[:, kt, c * 128:(c + 1) * 128], rhs=yT[:, kt, blk * 256:(blk + 1) * 256],
                                                                                 start=(kt == 0), stop=(kt == 7)),
                         reads=[b_xnT, b_yT], writes=[b_pa])
                R.op(ACT, lambda pa=pa, vfc=vfc, blk=blk: nc.scalar.activation(out=vfc[:, blk * 256:(blk + 1) * 256], in_=pa[:, :256], func=AF.Gelu_apprx_tanh), reads=[b_pa], writes=bvf)
            for j in range(4):
                R.op(DVE, lambda j=j, vfc=vfc: nc.vector.bn_stats(out=bst[:, j, :], in_=vfc[:, j * 512:(j + 1) * 512]), reads=bvf, writes=[b_bst])
            R.op(DVE, lambda: nc.vector.bn_aggr(out=mv[:, 0:2], in_=bst), reads=[b_bst], writes=[b_bst])
            R.op(DVE, lambda: nc.vector.tensor_scalar(out=mv[:, 3:4], in0=mv[:, 1:2], scalar1=EPS, scalar2=None, op0=ALU.add),
                 reads=[b_bst], writes=[b_bst])
            R.op(ACT, lambda: nc.scalar.activation(out=mv[:, 3:4], in_=mv[:, 3:4], func=AF.Sqrt), reads=[b_bst], writes=[b_bst])
            R.op(DVE, lambda: nc.vector.reciprocal(out=mv[:, 2:3], in_=mv[:, 3:4]), reads=[b_bst], writes=[b_bst])
            R.op(DVE, lambda vfc=vfc: nc.vector.tensor_scalar(out=vfc, in0=vfc, scalar1=mv[:, 0:1], scalar2=mv[:, 2:3], op0=ALU.subtract, op1=ALU.mult),
                 reads=bvf + [b_bst], writes=bvf)
            R.op(DVE, lambda c=c, vfc=vfc: nc.vector.tensor_tensor(out=vn[:, c, :], in0=vfc, in1=vgain, op=ALU.mult), reads=bvf + [b_vgain], writes=[b_vn])

        psT = pg
        b_psT = k.buf("psT")
        wo = carve([128, 8, D], BF16)
        b_wo = k.buf("wo")
        for half in range(2):
            for jt in range(8):
                j = half * 8 + jt
                g = j // 4
                if jt % 2 == 0:
                    wu, b_wu = load_wblock(w_in1, j * 128)
                    wz, b_wz = load_wblock(w_in1, 4096 + j * 128)
                    off = 0
                else:
                    off = 128
                for nb in range(4):
                    pa, b_pa = next_pacc()
                    for kt in range(8):
                        R.op(PE, lambda pa=pa, wu=wu, kt=kt, nb=nb, off=off: nc.tensor.matmul(pa, lhsT=wu[:, kt, off:off + 128], rhs=xnT[:, kt, nb * 512:(nb + 1) * 512],
                                                                                              start=(kt == 0), stop=(kt == 7)),
                             reads=[b_xnT, b_wu], writes=[b_pa])
                    gelu_from(pa, b_pa, uT1[:, nb * 512:(nb + 1) * 512], b_uT1, 512)
                    pz, b_pz = next_pacc()
                    for kt in range(8):
                        R.op(PE, lambda pz=pz, wz=wz, kt=kt, nb=nb, off=off: nc.tensor.matmul(pz, lhsT=wz[:, kt, off:off + 128], rhs=xnT[:, kt, nb * 512:(nb + 1) * 512],
                                                                                              start=(kt == 0), stop=(kt == 7)),
                             reads=[b_xnT, b_wz], writes=[b_pz])
                    R.op(ACT, lambda pz=pz, nb=nb: nc.scalar.activation(out=zT1[:, nb * 512:(nb + 1) * 512], in_=pz, func=AF.Silu),
                         reads=[b_pz], writes=[b_zT1])
                for c in range(NCH):
                    R.op(PE, lambda c=c, j=j, g=g: nc.tensor.matmul(psT[:, c // 4, (c % 4) * 128:(c % 4 + 1) * 128], lhsT=vn[:, c, j * 128:(j + 1) * 128],
                                                                     rhs=wmT[:, g, :], start=True, stop=True),
                         reads=[b_vn, b_wmT], writes=[b_pg[c // 4]])
                R.op(DVE, lambda: nc.vector.tensor_tensor(out=uT1, in0=uT1, in1=zT1, op=ALU.mult), reads=[b_uT1, b_zT1], writes=[b_uT1])
                R.op(DVE, lambda g=g: nc.vector.tensor_tensor(out=zT1.rearrange("p (c t) -> p c t", t=128), in0=psT.rearrange("p a (b t) -> p (a b) t", t=128),
                                                              in1=bsp[:, g:g + 1, :].to_broadcast([128, NCH, 128]), op=ALU.add),
                     reads=b_pg + [b_bsp], writes=[b_zT1])
                R.op(DVE, lambda jt=jt: nc.vector.tensor_tensor(out=yT[:, jt, :], in0=uT1, in1=zT1, op=ALU.mult), reads=[b_uT1, b_zT1], writes=[b_yT])
            out_proj_half(w_out1, half * 1024, src_dram if half == 0 else dst_dram, dst_dram, wo, b_wo)

    b_x = {}

    def out_proj_half(w_dram, row0, srcd, dstd, wo, b_wo):
        for ct2 in range(2):
            r0 = row0 + ct2 * 512
            R.dma(lambda r0=r0, ct2=ct2: nc.gpsimd.dma_start(out=wo[:, 4 * ct2:4 * ct2 + 4, :], in_=w_dram[r0:r0 + 512, :].rearrange("(c p) n -> p c n", p=128)), b_wo, writes=[b_wo], eng=POOL)
        for c in range(NCH):
            i = c % 2
            skey = (id(srcd), c)
            R.dma(lambda i=i, c=c: nc.sync.dma_start(out=xc[i], in_=srcd[c * 128:(c + 1) * 128, :]), b_xc[i],
                  reads=[b_x[skey]] if skey in b_x else [], writes=[b_xc[i]])
            for nb in range(2):
                pa, b_pa = next_pacc()
                for ct in range(8):
                    R.op(PE, lambda pa=pa, ct=ct, c=c, nb=nb: nc.tensor.matmul(pa, lhsT=yT[:, ct, c * 128:(c + 1) * 128], rhs=wo[:, ct, nb * 512:(nb + 1) * 512],
                                                                               start=(ct == 0), stop=(ct == 7)),
                         reads=[b_yT, b_wo], writes=[b_pa])
                R.op(DVE, lambda pa=pa, i=i, nb=nb: nc.vector.tensor_tensor(out=xc[i][:, nb * 512:(nb + 1) * 512], in0=xc[i][:, nb * 512:(nb + 1) * 512], in1=pa, op=ALU.add),
                     reads=[b_pa, b_xc[i]], writes=[b_xc[i]])
            key = (id(dstd), c)
            if key not in b_x:
                b_x[key] = k.buf("xd")
            R.dma(lambda i=i, c=c: nc.sync.dma_start(out=dstd[c * 128:(c + 1) * 128, :], in_=xc[i]), b_xc[i],
                  reads=[b_xc[i]], writes=[b_x[key]])

    norm_even = dr("norm_even", [D])
    w_in0 = dr("w_in_even", [D, 6144])
    w_out0 = dr("w_out_even", [2048, D])
    lam_re_d = dr("s5_lam_re", [64, 64])
    lam_im_d = dr("s5_lam_im", [64, 64])
    log_dt_d = dr("s5_log_dt", [64])
    b_re_d = dr("s5_b_re", [64, 64, 16])
    b_im_d = dr("s5_b_im", [64, 64, 16])
    c_re_d = dr("s5_c_re", [64, 16, 64])
    c_im_d = dr("s5_c_im", [64, 16, 64])
    s5_d_d = dr("s5_d", [1024])
    w_glu_d = dr("s5_w_glu", [1024, 1024])
    b_glu_d = dr("s5_b_glu", [1024])
    gn_gain_d = dr("ret_gn_gain", [1024])
    rot_d = dr("c_rot", [NSEG, 2, 128, SEG])
    dt_d = dr("c_dt", [128, 4, 128])
    xizs_d = dr("c_xizs", [128, 8])
    zsc_d = dr("c_zsc", [128, 64])
    tabB = dr("tabB", [8, 16, 128, 512], BF16, kind="Internal")
    tabCL = dr("tabCL", [8, 16, 128, 512], BF16, kind="Internal")
    tabK = dr("tabK", [8, 128, 1024], BF16, kind="Internal")
    b_tabK = k.buf("tabK_d")
    b_ptq = [None, None]
    b_tabB = k.buf("tabB_d")
    b_tabC = k.buf("tabC_d")

    OFF_UT, OFF_X, OFF_XB, OFF_E, OFF_PW, OFF_TBC, OFF_SRET, OFF_SBF, OFF_QK = 0, 16384, 24576, 28672, 31744, 34816, 38912, 43008, 45056
    uT = carve_at(OFF_UT, [128, 8, SEG], BF16)
    b_uT = k.buf("uT")
    Xre = carve_at(OFF_X, [128, SEG], F32)
    Xim = carve_at(OFF_X + 4096, [128, SEG], F32)
    b_X = k.buf("X")
    Xbre = carve_at(OFF_XB, [128, SEG], BF16)
    Xbim = carve_at(OFF_XB + 2048, [128, SEG], BF16)
    b_Xb = k.buf("Xb")
    EA = [carve_at(OFF_E + 768 * j, [128, 384], F32) for j in range(2)]
    EB = [carve_at(OFF_E + 768 * (2 + j), [128, 384], F32) for j in range(2)]
    b_E = k.buf("E")
    pw = carve_at(OFF_PW, [128, 32, 16, 3], F32)
    b_pw = k.buf("pw")
    tB = [carve_at(OFF_TBC + 1024 * j, [128, 2, 512], BF16) for j in range(2)]
    b_tB = [k.buf("tB%d" % j) for j in range(2)]
    tC = [carve_at(OFF_TBC + 2048 + 1024 * j, [128, 2, 512], BF16) for j in range(2)]
    b_tC = [k.buf("tC%d" % j) for j in range(2)]
    Sret = carve_at(OFF_SRET, [128, 4, 512], F32)
    b_Sret = k.buf("Sret")
    Sbf = carve_at(OFF_SBF, [128, 4, 512], BF16)
    b_Sbf = k.buf("Sbf")
    qrT = carve_at(OFF_QK, [128, 2, SEG], BF16)
    krT = carve_at(OFF_QK + 4096, [128, 2, SEG], BF16)
    b_qrT = k.buf("qrT")
    b_krT = k.buf("krT")
    Ss5 = k.sb("Ss5", [128, 32, 2], F32)
    b_Ss5 = k.buf("Ss5")
    dcol = k.sb("dcol", [128, 16], F32)
    b_dcol = k.buf("dcol")

    def gelu_sb(src, b_src, dst, b_dst, scr, b_scr):
        R.op(ACT, lambda: nc.scalar.activation(out=scr, in_=src, func=AF.Square), reads=[b_src], writes=[b_scr])
        R.op(DVE, lambda: nc.vector.tensor_scalar(out=scr, in0=scr, scalar1=0.044715 * 1.5957691216, scalar2=1.5957691216,
                                                  op0=ALU.mult, op1=ALU.add), reads=[b_scr], writes=[b_scr])
        R.op(DVE, lambda: nc.vector.tensor_tensor(out=scr, in0=scr, in1=src, op=ALU.mult), reads=[b_scr, b_src], writes=[b_scr])
        R.op(ACT, lambda: nc.scalar.activation(out=scr, in_=scr, func=AF.Sigmoid), reads=[b_scr], writes=[b_scr])
        R.op(DVE, lambda: nc.vector.tensor_tensor(out=dst, in0=scr, in1=src, op=ALU.mult), reads=[b_scr, b_src], writes=[b_dst])

    def s5_precompute():
        R.barrier()
        bp = k.buf("pre")
        pos = [OFF_UT]

        def tmp(shape, dt=F32):
            n = 1
            for d_ in shape[1:]:
                n *= d_
            n16 = n * (2 if dt == F32 else 1)
            v = carve_at(pos[0], shape, dt)
            pos[0] += n16
            assert pos[0] <= OFF_PW, pos[0]
            return v

        def VT(out, a, b, op):
            R.op(DVE, lambda: nc.vector.tensor_tensor(out=out, in0=a, in1=b, op=op), reads=[bp], writes=[bp])

        def VS(out, a, s1, op0, s2=None, op1=None):
            if op1 is None:
                R.op(DVE, lambda: nc.vector.tensor_scalar(out=out, in0=a, scalar1=s1, scalar2=None, op0=op0), reads=[bp], writes=[bp])
            else:
                R.op(DVE, lambda: nc.vector.tensor_scalar(out=out, in0=a, scalar1=s1, scalar2=s2, op0=op0, op1=op1), reads=[bp], writes=[bp])

        def AC(out, a, func):
            R.op(ACT, lambda: nc.scalar.activation(out=out, in_=a, func=func), reads=[bp], writes=[bp])

        def LD(out, src):
            R.dma(lambda: nc.sync.dma_start(out=out, in_=src, allow_slow_non_contiguous=True), bp, writes=[bp])

        S = [128, 32]
        lr, li, ldt, dtv, x1, mag, ang, r, sn, cs, are, aim, den, nre, zre, zim, ta, tb = [tmp(S) for _ in range(18)]
        LD(lr, lam_re_d.rearrange("(i gg) p -> (gg p) i", gg=2))
        LD(li, lam_im_d.rearrange("(i gg) p -> (gg p) i", gg=2))
        for gg in range(2):
            LD(ldt[gg * 64:(gg + 1) * 64, :], log_dt_d.rearrange("(i gg) -> gg i", gg=2)[gg].partition_broadcast(64))
        VS(lr, lr, -1e-4, ALU.min)
        AC(dtv, ldt, AF.Exp)
        VT(x1, lr, dtv, ALU.mult)
        AC(mag, x1, AF.Exp)
        VT(ang, li, dtv, ALU.mult)
        MAGIC = 12582912.0

        def reduce_sin(dst, a_in):
            VS(r, a_in, 1.0 / (2 * math.pi), ALU.mult)
            VS(ta, r, MAGIC, ALU.add)
            VS(ta, ta, -MAGIC, ALU.add)
            VS(tb, ta, -2 * math.pi, ALU.mult)
            VT(r, a_in, tb, ALU.add)
            VS(r, r, 3.14159, ALU.min, -3.14159, ALU.max)
            AC(dst, r, AF.Sin)

        reduce_sin(sn, ang)
        VS(x1, ang, 0.5 * math.pi, ALU.add)
        reduce_sin(cs, x1)
        VT(are, mag, cs, ALU.mult)
        VT(aim, mag, sn, ALU.mult)
        VT(den, lr, lr, ALU.mult)
        VT(ta, li, li, ALU.mult)
        VT(den, den, ta, ALU.add)
        R.op(DVE, lambda: nc.vector.reciprocal(out=den, in_=den), reads=[bp], writes=[bp])
        VS(nre, are, -1.0, ALU.add)
        VT(ta, nre, lr, ALU.mult)
        VT(tb, aim, li, ALU.mult)
        VT(ta, ta, tb, ALU.add)
        VT(zre, ta, den, ALU.mult)
        VT(ta, aim, lr, ALU.mult)
        VT(tb, nre, li, ALU.mult)
        VT(ta, ta, tb, ALU.subtract)
        VT(zim, ta, den, ALU.mult)
        def P(kk, j):
            return pw[:, :, kk, j]

        def setp(kk, re_ap, im_ap):
            R.op(DVE, lambda: nc.vector.tensor_copy(out=P(kk, 0), in_=re_ap), reads=[bp], writes=[bp, b_pw])
            R.op(DVE, lambda: nc.vector.tensor_copy(out=P(kk, 1), in_=im_ap), reads=[bp], writes=[bp, b_pw])
            R.op(DVE, lambda: nc.vector.tensor_scalar(out=P(kk, 2), in0=im_ap, scalar1=-1.0, scalar2=None, op0=ALU.mult), reads=[bp], writes=[bp, b_pw])

        def cmul(ore, oim, a_re, a_im, b_re_, b_im_):
            VT(ta, a_re, b_re_, ALU.mult)
            VT(tb, a_im, b_im_, ALU.mult)
            VT(ore, ta, tb, ALU.subtract)
            VT(ta, a_re, b_im_, ALU.mult)
            VT(tb, a_im, b_re_, ALU.mult)
            VT(oim, ta, tb, ALU.add)

        cr, ci, nr, ni = [tmp(S) for _ in range(4)]
        setp(0, are, aim)
        R.op(DVE, lambda: nc.vector.tensor_copy(out=cr, in_=are), reads=[bp], writes=[bp])
        R.op(DVE, lambda: nc.vector.tensor_copy(out=ci, in_=aim), reads=[bp], writes=[bp])
        for kk in range(1, 8):
            cmul(nr, ni, cr, ci, are, aim)
            R.op(DVE, lambda: nc.vector.tensor_copy(out=cr, in_=nr), reads=[bp], writes=[bp])
            R.op(DVE, lambda: nc.vector.tensor_copy(out=ci, in_=ni), reads=[bp], writes=[bp])
            setp(kk, cr, ci)
        setp(8, cr, ci)
        for m in range(1, 8):
            cmul(nr, ni, cr, ci, cr, ci)
            R.op(DVE, lambda: nc.vector.tensor_copy(out=cr, in_=nr), reads=[bp], writes=[bp])
            R.op(DVE, lambda: nc.vector.tensor_copy(out=ci, in_=ni), reads=[bp], writes=[bp])
            setp(8 + m, cr, ci)
        S3 = [128, 32, 16]
        bre, bim, Bre, Bim, t3a, t3b = [tmp(S3) for _ in range(6)]
        LD(bre, b_re_d.rearrange("(i gg) p h -> (gg p) i h", gg=2))
        LD(bim, b_im_d.rearrange("(i gg) p h -> (gg p) i h", gg=2))
        zre3 = zre.unsqueeze(2).to_broadcast(S3)
        zim3 = zim.unsqueeze(2).to_broadcast(S3)
        VT(t3a, bre, zre3, ALU.mult)
        VT(t3b, bim, zim3, ALU.mult)
        VT(Bre, t3a, t3b, ALU.subtract)
        VT(t3a, bim, zre3, ALU.mult)
        VT(t3b, bre, zim3, ALU.mult)
        VT(Bim, t3a, t3b, ALU.add)
        Blre, Blim = tmp(S3), tmp(S3)
        Wb = [tmp([128, 32, 128], BF16) for _ in range(2)]
        b_wp = [k.buf("wpad%d" % j) for j in range(2)]
        stages = [tmp([128, 4, 128], BF16) for _ in range(2)]
        b_stage = [k.buf("stg%d" % j) for j in range(2)]
        Cpad0 = tmp([128, 8, 2, 512], BF16)
        b_c0 = k.buf("cpad0")
        cin = [tmp([128, 128]) for _ in range(2)]
        b_cin = [k.buf("cin%d" % j) for j in range(2)]
        b_trs = k.buf("trs")
        Kst = [tmp([128, 128], BF16) for _ in range(2)]
        ptmp = tmp([128, 16])
        b_kst = [k.buf("kst%d" % j) for j in range(2)]
        for j in range(2):
            R.op(DVE, lambda j=j: nc.vector.memset(Wb[j], 0.0), reads=[], writes=[b_wp[j]])
        R.op(DVE, lambda: nc.vector.memset(Cpad0, 0.0), reads=[], writes=[b_c0])
        TrsAll = [tmp([128, 8, 128], BF16) for _ in range(2)]
        for t in range(8):
            for ri, cd in enumerate((c_re_d, c_im_d)):
                src = cd.rearrange("(t gl) h p -> t (gl h) p", gl=8)[t]
                R.dma(lambda ri=ri, src=src: nc.sync.dma_start(out=cin[ri][:, 0:64], in_=src), b_cin[ri], writes=[b_cin[ri]])
                R.dma(lambda ri=ri, src=src: nc.sync.dma_start(out=cin[ri][:, 64:128], in_=src), b_cin[ri], writes=[b_cin[ri]])
                ptc = pacc[ri][:, 0:128]
                R.op(PE, lambda ri=ri, ptc=ptc: nc.tensor.transpose(out=ptc, in_=cin[ri], identity=ident_f), reads=[b_cin[ri], b_ident], writes=[b_pacc[ri]])
                R.op(ACT, lambda ri=ri, ptc=ptc, t=t: nc.scalar.copy(out=TrsAll[ri][:, t, :], in_=ptc), reads=[b_pacc[ri]], writes=[b_trs])
        for kk in range(4):
            for gg in range(2):
                rs = slice(gg * 64, (gg + 1) * 64)
                c0 = 32 * kk + 16 * gg
                R.op(DVE, lambda kk=kk, rs=rs, c0=c0: nc.vector.tensor_copy(out=Cpad0[rs, :, 0, kk * 128 + c0:kk * 128 + c0 + 16], in_=TrsAll[0][rs, :, c0:c0 + 16]), reads=[b_trs], writes=[b_c0])
                R.op(DVE, lambda kk=kk, rs=rs, c0=c0: nc.vector.tensor_scalar(out=Cpad0[rs, :, 1, kk * 128 + c0:kk * 128 + c0 + 16], in0=TrsAll[1][rs, :, c0:c0 + 16], scalar1=-1.0, scalar2=None, op0=ALU.mult),
                     reads=[b_trs], writes=[b_c0])
        nst = [0]
        for lag in range(8):
            if lag == 0:
                srcs = (Bre, Bim)
            else:
                ar3 = pw[:, :, lag - 1, 0].unsqueeze(2).to_broadcast(S3)
                ai3 = pw[:, :, lag - 1, 1].unsqueeze(2).to_broadcast(S3)
                VT(t3a, Bre, ar3, ALU.mult)
                VT(t3b, Bim, ai3, ALU.mult)
                VT(Blre, t3a, t3b, ALU.subtract)
                VT(t3a, Bim, ar3, ALU.mult)
                VT(t3b, Bre, ai3, ALU.mult)
                VT(Blim, t3a, t3b, ALU.add)
                srcs = (Blre, Blim)
            for ri, Bt in enumerate(srcs):
                for kk in range(4):
                    R.op(DVE, lambda Bt=Bt, kk=kk, ri=ri: nc.vector.tensor_copy(out=Wb[ri][0:64, kk::4, 32 * kk:32 * kk + 16], in_=Bt[0:64, kk::4, :]), reads=[bp, b_wp[ri]], writes=[b_wp[ri]])
                    R.op(DVE, lambda Bt=Bt, kk=kk, ri=ri: nc.vector.tensor_copy(out=Wb[ri][64:128, kk::4, 32 * kk + 16:32 * kk + 32], in_=Bt[64:128, kk::4, :]), reads=[bp, b_wp[ri]], writes=[b_wp[ri]])
                for t in range(8):
                    j = nst[0] % 2
                    nst[0] += 1
                    ptr = (pT[0] if j == 0 else pT2)[:, 0:4, :]
                    for kk in range(4):
                        R.op(PE, lambda kk=kk, t=t, ptr=ptr, ri=ri: nc.tensor.transpose(out=ptr[:, kk, :], in_=Wb[ri][:, 4 * t + kk, :], identity=ident_b), reads=[b_wp[ri], b_ident], writes=[b_ptq_fix[j]])
                    R.op(ACT, lambda j=j, ptr=ptr: nc.scalar.copy(out=stages[j], in_=ptr), reads=[b_ptq_fix[j]], writes=[b_stage[j]])
                    R.dma(lambda t=t, ri=ri, lag=lag, j=j: nc.sync.dma_start(out=tabB[t, 2 * lag + ri], in_=stages[j].rearrange("p a b -> p (a b)")), b_stage[j], reads=[b_stage[j]], writes=[b_tabB])
            for t in range(8 if not SKIP.get('K') else 0):
                j = t % 2
                pk = pacc[j][:, 0:128]
                for kk in range(4):
                    for ri in range(2):
                        R.op(PE, lambda pk=pk, kk=kk, ri=ri, t=t: nc.tensor.matmul(pk, lhsT=Wb[ri][:, 4 * t + kk, :], rhs=Cpad0[:, t, ri, kk * 128:(kk + 1) * 128],
                                                                                   start=(kk == 0 and ri == 0), stop=(kk == 3 and ri == 1)), reads=[b_wp[ri], b_c0], writes=[b_pacc[j]])
                R.op(ACT, lambda j=j, pk=pk: nc.scalar.copy(out=Kst[j], in_=pk), reads=[b_pacc[j]], writes=[b_kst[j]])
                R.dma(lambda t=t, lag=lag, j=j: nc.sync.dma_start(out=tabK[t, :, lag * 128:(lag + 1) * 128], in_=Kst[j]), b_kst[j], reads=[b_kst[j]], writes=[b_tabK])
        CpadAll = [Wb[ri].rearrange("p a b -> p (a b)").rearrange("p (t n) -> p t n", n=512) for ri in range(2)]
        tc1, tc2 = tmp([128, 8, 16]), tmp([128, 8, 16])
        for ri in range(2):
            R.op(DVE, lambda ri=ri: nc.vector.memset(Wb[ri], 0.0), reads=[b_wp[ri]], writes=[b_wp[ri]])
        for s_ in range(8):
            for kk in range(4):
                for gg in range(2):
                    rs = slice(gg * 64, (gg + 1) * 64)
                    c0 = 32 * kk + 16 * gg
                    S8 = [64, 8, 16]
                    Ar = pw[rs, kk::4, s_, 0].unsqueeze(2).to_broadcast(S8)
                    Ai = pw[rs, kk::4, s_, 1].unsqueeze(2).to_broadcast(S8)
                    Tr_, Ti_ = TrsAll[0][rs, :, c0:c0 + 16], TrsAll[1][rs, :, c0:c0 + 16]
                    o_re = CpadAll[0][rs, :, kk * 128 + c0:kk * 128 + c0 + 16]
                    o_im = CpadAll[1][rs, :, kk * 128 + c0:kk * 128 + c0 + 16]
                    a_, b_ = tc1[rs], tc2[rs]
                    rd = [b_trs, b_pw, bp]
                    R.op(DVE, lambda a_=a_, Tr_=Tr_, Ar=Ar: nc.vector.tensor_tensor(out=a_, in0=Tr_, in1=Ar, op=ALU.mult), reads=rd, writes=[bp])
                    R.op(DVE, lambda b_=b_, Ti_=Ti_, Ai=Ai: nc.vector.tensor_tensor(out=b_, in0=Ti_, in1=Ai, op=ALU.mult), reads=rd, writes=[bp])
                    R.op(DVE, lambda o_re=o_re, a_=a_, b_=b_: nc.vector.tensor_tensor(out=o_re, in0=a_, in1=b_, op=ALU.subtract), reads=[bp, b_wp[0]], writes=[b_wp[0]])
                    R.op(DVE, lambda a_=a_, Tr_=Tr_, Ai=Ai: nc.vector.tensor_tensor(out=a_, in0=Tr_, in1=Ai, op=ALU.mult), reads=rd, writes=[bp])
                    R.op(DVE, lambda b_=b_, Ti_=Ti_, Ar=Ar: nc.vector.tensor_tensor(out=b_, in0=Ti_, in1=Ar, op=ALU.mult), reads=rd, writes=[bp])
                    R.op(DVE, lambda o_im=o_im, a_=a_, b_=b_: nc.vector.tensor_tensor(out=o_im, in0=a_, in1=b_, op=ALU.add), reads=[bp, b_wp[1]], writes=[b_wp[1]])
            for ri in range(2):
                R.dma(lambda s_=s_, ri=ri: nc.sync.dma_start(out=tabCL[:, 2 * s_ + ri].rearrange("t p n -> p t n"), in_=CpadAll[ri]), b_wp[ri], reads=[b_wp[ri]], writes=[b_tabC])
        R.dma(lambda: nc.sync.dma_start(out=dcol[:, 0:8], in_=s5_d_d.rearrange("(t p) -> p t", p=128), allow_slow_non_contiguous=True), b_dcol, writes=[b_dcol])
        R.dma(lambda: nc.sync.dma_start(out=dcol[:, 8:16], in_=b_glu_d.rearrange("(t p) -> p t", p=128), allow_slow_non_contiguous=True), b_dcol, writes=[b_dcol])
        R.barrier()

    def stt(eng_id, out, in0, scalar, in1, reads, writes):
        e = nc.vector if eng_id == DVE else nc.gpsimd
        R.op(eng_id, lambda: e.scalar_tensor_tensor(out=out, in0=in0, scalar=scalar, in1=in1, op0=ALU.mult, op1=ALU.add), reads=reads, writes=writes)

    tBL = [yT.rearrange("p a b -> p (a b)")[:, j * 8192:(j + 1) * 8192].rearrange("p (l n) -> p l n", n=512) for j in range(2)]
    b_tBL = [k.buf("tBL%d" % j) for j in range(2)]
    tCL = carve_at(OFF_X, [128, 16, 512], BF16)
    b_tCL = k.buf("tCL")
    tK = [carve_at(OFF_TBC + 1024 * j, [128, 8, 128], BF16) for j in range(2)]
    b_tK = [k.buf("tK%d" % j) for j in range(2)]
    carry_b = [carve_at(OFF_XB + 6144, [128, 4, 256], BF16), carve_at(OFF_TBC + 2048, [128, 4, 256], BF16)]
    b_carry = k.buf("carry")
    Epre = [[carve_at(base + 2048 * j, [128, 4, 256], F32) for j in range(4)] for base in (OFF_X, OFF_QK)]
    b_Epre = [k.buf("Epre%d" % j) for j in range(2)]
    Tpre = carve_at(OFF_XB, [128, 4, 128], F32)
    b_Tpre = k.buf("Tpre")
    HA = [carve_at(OFF_QK + 3072 * j, [128, 4, 384], F32) for j in range(2)]
    HB = [carve_at(OFF_XB + 3072 * j, [128, 4, 384], F32) for j in range(2)]
    Ths = carve_at(OFF_QK + 6144, [128, 4, 256], F32)
    b_H = k.buf("H")
    pgv = pg.rearrange("p a b -> p (a b)").rearrange("p (s j) -> p s j", j=256)

    def VTT(out, a, b, op, reads, writes):
        R.op(DVE, lambda: nc.vector.tensor_tensor(out=out, in0=a, in1=b, op=op), reads=reads, writes=writes)

    def pwb(ct, idx, comp, w):
        return pw[:, 4 * ct:4 * ct + 4, idx, comp].unsqueeze(2).to_broadcast([128, 4, w])

    def inject(ct, e_re, e_im, tmp4, reads, writes):
        sre, sim = Ss5[:, 4 * ct:4 * ct + 4, 0], Ss5[:, 4 * ct:4 * ct + 4, 1]
        p8r, p8i = pw[:, 4 * ct:4 * ct + 4, 7, 0], pw[:, 4 * ct:4 * ct + 4, 7, 1]
        rd = reads + [b_pw, b_Ss5]
        VTT(tmp4, sre, p8r, ALU.mult, rd, writes)
        VTT(e_re, e_re, tmp4, ALU.add, rd, writes)
        VTT(tmp4, sim, p8i, ALU.mult, rd, writes)
        VTT(e_re, e_re, tmp4, ALU.subtract, rd, writes)
        VTT(tmp4, sim, p8r, ALU.mult, rd, writes)
        VTT(e_im, e_im, tmp4, ALU.add, rd, writes)
        VTT(tmp4, sre, p8i, ALU.mult, rd, writes)
        VTT(e_im, e_im, tmp4, ALU.add, rd, writes)

    uS3 = t1.bitcast(BF16).rearrange("p (s j) -> p s j", j=256)

    def deinterleave(ct):
        uv_ = uT[:, ct, :].rearrange("p (j s) -> p s j", s=8)
        R.op(POOL, lambda: nc.gpsimd.tensor_copy(out=uS3, in_=uv_), reads=[b_uT], writes=[b_t1])

    def e_matmuls(ct, sl, dst_re, dst_im, bdst, col0):
        uv = uS3
        for kk in range(4):
            for ri, dst in enumerate((dst_re, dst_im)):
                pe_ = pacc[ri][:, 0:256]
                for s_ in range(8):
                    lag = 7 - s_
                    R.op(PE, lambda pe_=pe_, lag=lag, ri=ri, kk=kk, s_=s_: nc.tensor.matmul(pe_, lhsT=tBL[sl][:, 2 * lag + ri, kk * 128:(kk + 1) * 128], rhs=uv[:, s_, :],
                                                                                            start=(s_ == 0), stop=(s_ == 7)), reads=[b_tBL[sl], b_t1], writes=[b_pacc[ri]])
                R.op(ACT, lambda pe_=pe_, dst=dst, kk=kk: nc.scalar.copy(out=dst[:, kk, col0:col0 + 256], in_=pe_), reads=[b_pacc[ri]], writes=[bdst])

    def s5_prefix_tile(ct):
        sl = ct % 2
        R.dma(lambda: nc.sync.dma_start(out=tBL[sl], in_=tabB[ct].rearrange("r p n -> p r n")), b_tBL[sl], reads=[b_tabB], writes=[b_tBL[sl]])
        E = Epre[sl]
        bE = b_Epre[sl]
        deinterleave(ct)
        e_matmuls(ct, sl, E[0], E[1], bE, 0)
        inject(ct, E[0][:, :, 0], E[1][:, :, 0], Tpre[:, :, 0], [bE, b_Tpre], [bE, b_Tpre])
        rw = [bE, b_pw, b_Tpre]
        for m in range(8):
            w = 128 >> m
            src = (E[0], E[1]) if m % 2 == 0 else (E[2], E[3])
            dst = (E[2], E[3]) if m % 2 == 0 else (E[0], E[1])
            ev = [x_[:, :, 0:2 * w].rearrange("p a (k two) -> p a k two", two=2)[:, :, :, 0] for x_ in src]
            od = [x_[:, :, 0:2 * w].rearrange("p a (k two) -> p a k two", two=2)[:, :, :, 1] for x_ in src]
            dr_, di_ = dst[0][:, :, 0:w], dst[1][:, :, 0:w]
            t_ = Tpre[:, :, 0:w]
            Pr, Pi = pwb(ct, 8 + m, 0, w), pwb(ct, 8 + m, 1, w)
            VTT(t_, ev[0], Pr, ALU.mult, rw, [b_Tpre])
            VTT(dr_, t_, od[0], ALU.add, rw, [bE])
            VTT(t_, ev[1], Pi, ALU.mult, rw, [b_Tpre])
            VTT(dr_, dr_, t_, ALU.subtract, rw, [bE])
            VTT(t_, ev[1], Pr, ALU.mult, rw, [b_Tpre])
            VTT(di_, t_, od[1], ALU.add, rw, [bE])
            VTT(t_, ev[0], Pi, ALU.mult, rw, [b_Tpre])
            VTT(di_, di_, t_, ALU.add, rw, [bE])
        R.op(DVE, lambda: nc.vector.tensor_copy(out=Ss5[:, 4 * ct:4 * ct + 4, 0], in_=E[0][:, :, 0]), reads=[bE], writes=[b_Ss5])
        R.op(DVE, lambda: nc.vector.tensor_copy(out=Ss5[:, 4 * ct:4 * ct + 4, 1], in_=E[1][:, :, 0]), reads=[bE], writes=[b_Ss5])

    def s5_own_tile(ct):
        sl = ct % 2
        R.dma(lambda: nc.sync.dma_start(out=tBL[sl], in_=tabB[ct].rearrange("r p n -> p r n")), b_tBL[sl], reads=[b_tabB], writes=[b_tBL[sl]])
        R.dma(lambda: nc.sync.dma_start(out=tCL, in_=tabCL[ct].rearrange("r p n -> p r n")), b_tCL, reads=[b_tabC], writes=[b_tCL])
        R.dma(lambda: nc.sync.dma_start(out=tK[sl], in_=tabK[ct].rearrange("p (l n) -> p l n", n=128)), b_tK[sl], reads=[b_tabK], writes=[b_tK[sl]])
        uv = uT[:, ct, :].rearrange("p (j s) -> p s j", s=8)
        deinterleave(ct)
        for s_ in range(8):
            for lag in range(s_ + 1):
                R.op(PE, lambda s_=s_, lag=lag: nc.tensor.matmul(pgv[:, s_, :], lhsT=tK[sl][:, lag, :], rhs=uS3[:, s_ - lag, :], start=(lag == 0 and s_ % 2 == 0), stop=False),
                     reads=[b_tK[sl], b_t1], writes=[b_pg[s_ // 2]])
        e_matmuls(ct, sl, HA[0], HA[1], b_H, 128)
        inject(ct, HA[0][:, :, 128], HA[1][:, :, 128], Ths[:, :, 0], [b_H], [b_H])
        rw = [b_H, b_pw]

        def cmac(dre, dim_, sre_, sim_, m, w):
            Pr, Pi = pwb(ct, 8 + m, 0, w), pwb(ct, 8 + m, 1, w)
            t_ = Ths[:, :, 0:w]
            VTT(t_, sre_, Pr, ALU.mult, rw, [b_H])
            VTT(dre, dre, t_, ALU.add, rw, [b_H])
            VTT(t_, sim_, Pi, ALU.mult, rw, [b_H])
            VTT(dre, dre, t_, ALU.subtract, rw, [b_H])
            VTT(t_, sim_, Pr, ALU.mult, rw, [b_H])
            VTT(dim_, dim_, t_, ALU.add, rw, [b_H])
            VTT(t_, sre_, Pi, ALU.mult, rw, [b_H])
            VTT(dim_, dim_, t_, ALU.add, rw, [b_H])

        def sview(buf_, first, cnt, step):
            return buf_[:, :, 128 + first:128 + first + (cnt - 1) * step + 1:step]

        for m in range(8):
            d_ = 1 << m
            n_ = 256 // (2 * d_)
            cmac(sview(HA[0], 2 * d_ - 1, n_, 2 * d_), sview(HA[1], 2 * d_ - 1, n_, 2 * d_), sview(HA[0], d_ - 1, n_, 2 * d_), sview(HA[1], d_ - 1, n_, 2 * d_), m, n_)
        for m in range(6, -1, -1):
            d_ = 1 << m
            n_ = 256 // (2 * d_) - 1
            cmac(sview(HA[0], 3 * d_ - 1, n_, 2 * d_), sview(HA[1], 3 * d_ - 1, n_, 2 * d_), sview(HA[0], 2 * d_ - 1, n_, 2 * d_), sview(HA[1], 2 * d_ - 1, n_, 2 * d_), m, n_)
        for ri in range(2):
            R.op(DVE, lambda ri=ri: nc.vector.tensor_copy(out=HA[ri][:, :, 127], in_=Ss5[:, 4 * ct:4 * ct + 4, ri]), reads=[b_Ss5, b_H], writes=[b_H])
        for ri in range(2):
            R.op(DVE, lambda ri=ri: nc.vector.tensor_copy(out=Ss5[:, 4 * ct:4 * ct + 4, ri], in_=HA[ri][:, :, 383]), reads=[b_H], writes=[b_Ss5])
        R.op(ACT, lambda: nc.scalar.copy(out=carry_b[0], in_=HA[0][:, :, 127:383]), reads=[b_H], writes=[b_carry])
        R.op(ACT, lambda: nc.scalar.activation(out=carry_b[1], in_=HA[1][:, :, 127:383], func=AF.Copy, scale=-1.0), reads=[b_H], writes=[b_carry])
        for ri in range(2):
            R.op(DVE, lambda ri=ri: nc.vector.memset(HA[ri][:, :, 127:128], 0.0), reads=[b_carry], writes=[b_H])
        for kk in range(4):
            for s_ in range(8):
                for ri in range(2):
                    R.op(PE, lambda kk=kk, s_=s_, ri=ri: nc.tensor.matmul(pgv[:, s_, :], lhsT=tCL[:, 2 * s_ + ri, kk * 128:(kk + 1) * 128], rhs=carry_b[ri][:, kk, :],
                                                                         start=False, stop=(kk == 3 and ri == 1 and s_ % 2 == 1)), reads=[b_tCL, b_carry], writes=[b_pg[s_ // 2]])
        for nb in range(4):
            ya, sc = t1[:, 0:512], t1[:, 512:1024]
            ya3 = ya.rearrange("p (s j) -> p s j", j=256)
            uvs = uv[:, 2 * nb:2 * nb + 2, :]
            R.op(DVE, lambda nb=nb, ya3=ya3, uvs=uvs: nc.vector.scalar_tensor_tensor(out=ya3, in0=uvs, scalar=dcol[:, ct:ct + 1], in1=pg[:, nb, :].rearrange("p (s j) -> p s j", j=256),
                                                                                    op0=ALU.mult, op1=ALU.add), reads=[b_uT, b_dcol, b_pg[nb]], writes=[b_t1])
            R.op(ACT, lambda uvs=uvs, ya3=ya3: nc.scalar.activation(out=uvs, in_=ya3, func=AF.Gelu_apprx_tanh), reads=[b_t1], writes=[b_uT])

    def s5_segment(own):
        for ct2 in range(4):
            wb, b_wb = load_wblock(w_in0, ct2 * 256)
            for hf in range(2):
                ct = 2 * ct2 + hf
                for nb in range(4):
                    pa, b_pa = next_pacc()
                    for kt in range(8):
                        R.op(PE, lambda pa=pa, wb=wb, kt=kt, nb=nb, hf=hf: nc.tensor.matmul(pa, lhsT=wb[:, kt, hf * 128:(hf + 1) * 128], rhs=xnT[:, kt, nb * 512:(nb + 1) * 512],
                                                                                            start=(kt == 0), stop=(kt == 7)), reads=[b_xnT, b_wb], writes=[b_pa])
                    R.op(ACT, lambda pa=pa, ct=ct, nb=nb: nc.scalar.copy(out=uT[:, ct, nb * 512:(nb + 1) * 512], in_=pa), reads=[b_pa], writes=[b_uT])
        if own:
            for buf_ in HA + HB:
                R.op(DVE, lambda buf_=buf_: nc.vector.memset(buf_[:, :, 0:128], 0.0), reads=[], writes=[b_H])
        for ct in range(8):
            if own:
                s5_own_tile(ct)
            else:
                s5_prefix_tile(ct)

    def glu():
        gsb = carve_at(OFF_X, [128, 512], F32)
        zsb = carve_at(OFF_X + 1024, [128, 512], F32)
        b_g = k.buf("gsb")
        for jt2 in range(4):
            wg, b_wg = load_wblock(w_glu_d, jt2 * 256, gain=False)
            wa, b_wa = load_wblock(w_in0, 1024 + jt2 * 256)
            for hf in range(2):
                jt = 2 * jt2 + hf
                for nb in range(4):
                    pa, b_pa = next_pacc()
                    for kt in range(8):
                        R.op(PE, lambda pa=pa, wg=wg, kt=kt, nb=nb, hf=hf: nc.tensor.matmul(pa, lhsT=wg[:, kt, hf * 128:(hf + 1) * 128], rhs=uT[:, kt, nb * 512:(nb + 1) * 512],
                                                                                            start=(kt == 0), stop=(kt == 7)), reads=[b_uT, b_wg], writes=[b_pa])
                    R.op(ACT, lambda pa=pa, jt=jt: nc.scalar.activation(out=gsb, in_=pa, func=AF.Sigmoid, bias=dcol[:, 8 + jt:9 + jt]), reads=[b_pa, b_dcol], writes=[b_g])
                    pz, b_pz = next_pacc()
                    for kt in range(8):
                        R.op(PE, lambda pz=pz, wa=wa, kt=kt, nb=nb, hf=hf: nc.tensor.matmul(pz, lhsT=wa[:, kt, hf * 128:(hf + 1) * 128], rhs=xnT[:, kt, nb * 512:(nb + 1) * 512],
                                                                                            start=(kt == 0), stop=(kt == 7)), reads=[b_xnT, b_wa], writes=[b_pz])
                    R.op(ACT, lambda pz=pz: nc.scalar.activation(out=zsb, in_=pz, func=AF.Silu), reads=[b_pz], writes=[b_g])
                    R.op(DVE, lambda: nc.vector.tensor_tensor(out=gsb, in0=gsb, in1=zsb, op=ALU.mult), reads=[b_g], writes=[b_g])
                    R.op(DVE, lambda jt=jt, nb=nb: nc.vector.tensor_tensor(out=yT[:, jt, nb * 512:(nb + 1) * 512], in0=uT[:, jt, nb * 512:(nb + 1) * 512], in1=gsb, op=ALU.mult),
                         reads=[b_g, b_uT], writes=[b_yT])

    G128 = [(1.0 - 2.0 ** (-5 - h)) ** 128 for h in range(4)]

    def ret_segment(seg, own):
        cosT = carve_at(OFF_X, [128, SEG], F32)
        sinT = carve_at(OFF_X + 4096, [128, SEG], F32)
        b_rot = k.buf("rot")
        R.dma(lambda: nc.sync.dma_start(out=cosT, in_=rot_d[seg, 0]), b_rot, writes=[b_rot])
        R.dma(lambda: nc.sync.dma_start(out=sinT, in_=rot_d[seg, 1]), b_rot, writes=[b_rot])
        DTt = carve_at(OFF_XB, [128, 4, 128], F32)
        xz = carve_at(OFF_XB + 1024, [128, 8], F32)
        gng = carve_at(OFF_XB + 1040, [128, 1024], F32)
        b_cst = k.buf("rcst")
        R.dma(lambda: nc.sync.dma_start(out=DTt, in_=dt_d), b_cst, writes=[b_cst])
        R.dma(lambda: nc.sync.dma_start(out=xz, in_=xizs_d), b_cst, writes=[b_cst])
        zsc = carve_at(OFF_XB + 1040 + 2048, [128, 64], F32)
        R.dma(lambda: nc.sync.dma_start(out=zsc, in_=zsc_d), b_cst, writes=[b_cst])
        R.dma(lambda: nc.sync.dma_start(out=gng, in_=gn_gain_d.partition_broadcast(128)), b_cst, writes=[b_cst])
        eo = [OFF_UT]

        def et(shape, dt):
            n = 1
            for d_ in shape[1:]:
                n *= d_
            n16 = n * (2 if dt == F32 else 1)
            v = carve_at(eo[0], shape, dt)
            eo[0] += n16
            assert eo[0] <= OFF_UT + 16384
            return v

        TT = []
        for pp in range(2):
            d_ = dict(PT=et([128, 128], BF16), o_sb=et([128, 256], F32), in_sb=et([128, 256], F32), ktok=et([128, 256], BF16), v_bf=et([128, 256], BF16),
                      vz_bf=et([128, 256], BF16), yb=et([128, 256], BF16), sbz=et([128, 256], F32), bstt=et([128, 6], F32), mvv=et([128, 4], F32))
            for nm in ("b_PT", "b_o", "b_in", "b_ktok", "b_v", "b_yb", "b_sbz", "b_bs"):
                d_[nm] = k.buf()
            d_["b_psc"], d_["b_pin"], d_["b_pcr"] = b_pg[0], b_pg[1], b_pg[2]
            tb_, btb_ = (pT[0], b_pT[0]) if pp == 0 else (pT2, b_pT2)
            d_["b_ptk"], d_["b_pty"] = btb_, btb_
            d_["ptk"] = tb_[:, 0:2, :]
            d_["pty"] = tb_[:, 2:4, :]
            d_["psc"] = pg[:, 0, 0:128]
            d_["pin"] = pg[:, 1, 0:256]
            d_["pcr"] = pg[:, 2, 0:256]
            if own:
                d_["b_pv"], d_["b_pz"] = b_pacc[0], b_pacc[1]
                d_["pv"] = pacc[0][:, 0:256]
                d_["pz"] = pacc[1][:, 0:256]
                d_["plc"], d_["b_plc"] = pg[:, 3, :], b_pg[3]
            else:
                d_["b_pv"], d_["b_pz"] = b_pacc[pp], None
                d_["pv"] = pacc[pp][:, 0:256]
                d_["pz"] = None
                d_["plc"], d_["b_plc"] = pg[:, 2 + pp, :], b_pg[2 + pp]
            TT.append(d_)
        hb = [[b_pacc[0]], [b_pacc[1]]]

        rbanks = [((pacc[0], b_pacc[0]), (pacc[1], b_pacc[1])), ((pg[:, 0, :], b_pg[0]), (pg[:, 1, :], b_pg[1])), ((pg[:, 2, :], b_pg[2]), (pg[:, 3, :], b_pg[3]))]
        rctr = [0]

        def rotary_proj(w, b_w, dstT, b_dst):
            for nb in range(4):
                blk = slice(nb * 512, (nb + 1) * 512)
                bk = rbanks[rctr[0] % 3]
                rctr[0] += 1
                for hf in range(2):
                    ps_, bps_ = bk[hf]
                    for kt in range(8):
                        R.op(PE, lambda hf=hf, kt=kt, blk=blk, ps_=ps_: nc.tensor.matmul(ps_, lhsT=w[:, kt, hf * 128:(hf + 1) * 128], rhs=xnT[:, kt, blk], start=(kt == 0), stop=(kt == 7)),
                             reads=[b_xnT, b_w], writes=[bps_])
                (p0, bp0), (p1, bp1) = bk
                ta, tb = t1[:, 0:512], t1[:, 512:1024]
                R.op(DVE, lambda blk=blk, p0=p0: nc.vector.tensor_tensor(out=ta, in0=p0, in1=cosT[:, blk], op=ALU.mult), reads=[bp0, b_rot], writes=[b_t1])
                R.op(DVE, lambda blk=blk, p1=p1: nc.vector.tensor_tensor(out=tb, in0=p1, in1=sinT[:, blk], op=ALU.mult), reads=[bp1, b_rot], writes=[b_t1])
                R.op(DVE, lambda blk=blk: nc.vector.tensor_tensor(out=dstT[:, 0, blk], in0=ta, in1=tb, op=ALU.subtract), reads=[b_t1], writes=[b_dst])
                R.op(DVE, lambda blk=blk, p0=p0: nc.vector.tensor_tensor(out=ta, in0=p0, in1=sinT[:, blk], op=ALU.mult), reads=[bp0, b_rot, b_dst], writes=[b_t1])
                R.op(DVE, lambda blk=blk, p1=p1: nc.vector.tensor_tensor(out=tb, in0=p1, in1=cosT[:, blk], op=ALU.mult), reads=[bp1, b_rot], writes=[b_t1])
                R.op(DVE, lambda blk=blk: nc.vector.tensor_tensor(out=dstT[:, 1, blk], in0=ta, in1=tb, op=ALU.add), reads=[b_t1], writes=[b_dst])

        for h in range(4):
            wk, b_wk = load_wblock(w_in0, 3072 + h * 256)
            rotary_proj(wk, b_wk, krT, b_krT)
            if own:
                wq, b_wq = load_wblock(w_in0, 2048 + h * 256)
                rotary_proj(wq, b_wq, qrT, b_qrT)
            wv, b_wv = load_wblock(w_in0, 4096 + h * 256)
            if own:
                wz, b_wz = load_wblock(w_in0, 5120 + h * 256)
            for c in range(NCH):
                ch = slice(c * 128, (c + 1) * 128)
                T = TT[c % 2]
                for kt in range(8):
                    R.op(PE, lambda kt=kt, ch=ch, T=T, wv=wv: nc.tensor.matmul(T["pv"], lhsT=xnT[:, kt, ch], rhs=wv[:, kt, :], start=(kt == 0), stop=(kt == 7)),
                         reads=[b_xnT, b_wv], writes=[T["b_pv"]])
                R.op(ACT, lambda T=T: nc.scalar.copy(out=T["v_bf"], in_=T["pv"]), reads=[T["b_pv"]], writes=[T["b_v"]])
                vsc = xz[:, 4 + h:5 + h] if own else zsc[:, 16 * h + c:16 * h + c + 1]
                R.op(ACT, lambda vsc=vsc, T=T: nc.scalar.activation(out=T["vz_bf"], in_=T["pv"], func=AF.Copy, scale=vsc), reads=[T["b_pv"], b_cst], writes=[T["b_v"]])
                for tl in range(2):
                    R.op(PE, lambda tl=tl, ch=ch, T=T: nc.tensor.transpose(out=T["ptk"][:, tl, :], in_=krT[:, tl, ch], identity=ident_b), reads=[b_krT, b_ident], writes=[T["b_ptk"]])
                R.op(DVE, lambda T=T: nc.vector.tensor_copy(out=T["ktok"].rearrange("p (a b) -> p a b", b=128), in_=T["ptk"]), reads=[T["b_ptk"]], writes=[T["b_ktok"]])
                if own:
                    for tl in range(2):
                        R.op(PE, lambda tl=tl, ch=ch, T=T: nc.tensor.matmul(T["psc"], lhsT=krT[:, tl, ch], rhs=qrT[:, tl, ch], start=(tl == 0), stop=(tl == 1)),
                             reads=[b_krT, b_qrT], writes=[T["b_psc"]])
                    R.op(DVE, lambda h=h, T=T: nc.vector.tensor_tensor(out=T["PT"], in0=T["psc"], in1=DTt[:, h, :], op=ALU.mult), reads=[T["b_psc"], b_cst], writes=[T["b_PT"]])
                    R.op(PE, lambda T=T: nc.tensor.matmul(T["pin"], lhsT=T["PT"], rhs=T["v_bf"], start=True, stop=True), reads=[T["b_PT"], T["b_v"]], writes=[T["b_pin"]])
                    for tl in range(2):
                        R.op(PE, lambda tl=tl, ch=ch, h=h, T=T: nc.tensor.matmul(T["pcr"], lhsT=qrT[:, tl, ch], rhs=Sbf[:, h, tl * 256:(tl + 1) * 256], start=(tl == 0), stop=(tl == 1)),
                             reads=[b_qrT, b_Sbf], writes=[T["b_pcr"]])
                    R.op(ACT, lambda T=T: nc.scalar.copy(out=T["in_sb"], in_=T["pin"]), reads=[T["b_pin"]], writes=[T["b_in"]])
                    R.op(DVE, lambda h=h, T=T: nc.vector.scalar_tensor_tensor(out=T["o_sb"], in0=T["pcr"], scalar=xz[:, h:h + 1], in1=T["in_sb"], op0=ALU.mult, op1=ALU.add),
                         reads=[T["b_pcr"], b_cst, T["b_in"]], writes=[T["b_o"]])
                    R.op(DVE, lambda T=T: nc.vector.bn_stats(out=T["bstt"], in_=T["o_sb"]), reads=[T["b_o"]], writes=[T["b_bs"]])
                    R.op(DVE, lambda T=T: nc.vector.bn_aggr(out=T["mvv"][:, 0:2], in_=T["bstt"]), reads=[T["b_bs"]], writes=[T["b_bs"]])
                    R.op(DVE, lambda T=T: nc.vector.tensor_scalar(out=T["mvv"][:, 3:4], in0=T["mvv"][:, 1:2], scalar1=EPS, scalar2=None, op0=ALU.add), reads=[T["b_bs"]], writes=[T["b_bs"]])
                    R.op(ACT, lambda T=T: nc.scalar.activation(out=T["mvv"][:, 3:4], in_=T["mvv"][:, 3:4], func=AF.Sqrt), reads=[T["b_bs"]], writes=[T["b_bs"]])
                    R.op(DVE, lambda T=T: nc.vector.reciprocal(out=T["mvv"][:, 2:3], in_=T["mvv"][:, 3:4]), reads=[T["b_bs"]], writes=[T["b_bs"]])
                    R.op(DVE, lambda T=T: nc.vector.tensor_scalar(out=T["o_sb"], in0=T["o_sb"], scalar1=T["mvv"][:, 0:1], scalar2=T["mvv"][:, 2:3], op0=ALU.subtract, op1=ALU.mult),
                         reads=[T["b_o"], T["b_bs"]], writes=[T["b_o"]])
                    for kt in range(8):
                        R.op(PE, lambda kt=kt, ch=ch, T=T, wz=wz: nc.tensor.matmul(T["pz"], lhsT=xnT[:, kt, ch], rhs=wz[:, kt, :], start=(kt == 0), stop=(kt == 7)),
                             reads=[b_xnT, b_wz], writes=[T["b_pz"]])
                    R.op(ACT, lambda T=T: nc.scalar.activation(out=T["sbz"], in_=T["pz"], func=AF.Silu), reads=[T["b_pz"]], writes=[T["b_sbz"]])
                    R.op(DVE, lambda h=h, T=T: nc.vector.tensor_tensor(out=T["o_sb"], in0=T["o_sb"], in1=gng[:, h * 256:(h + 1) * 256], op=ALU.mult), reads=[T["b_o"], b_cst], writes=[T["b_o"]])
                    R.op(DVE, lambda T=T: nc.vector.tensor_tensor(out=T["yb"], in0=T["o_sb"], in1=T["sbz"], op=ALU.mult), reads=[T["b_o"], T["b_sbz"]], writes=[T["b_yb"]])
                    for tl in range(2):
                        R.op(PE, lambda tl=tl, T=T: nc.tensor.transpose(out=T["pty"][:, tl, :], in_=T["yb"][:, tl * 128:(tl + 1) * 128], identity=ident_b), reads=[T["b_yb"], b_ident], writes=[T["b_pty"]])
                    R.op(DVE, lambda h=h, ch=ch, T=T: nc.vector.tensor_copy(out=yT[:, 2 * h:2 * h + 2, ch], in_=T["pty"]), reads=[T["b_pty"]], writes=[b_yT])
                if own:
                    plc, b_plc = T["plc"], T["b_plc"]
                    for tl in range(2):
                        R.op(PE, lambda tl=tl, T=T, plc=plc: nc.tensor.matmul(plc[:, tl * 256:(tl + 1) * 256], lhsT=T["ktok"][:, tl * 128:(tl + 1) * 128], rhs=T["vz_bf"], start=(tl == 0), stop=(tl == 1)),
                             reads=[T["b_ktok"], T["b_v"]], writes=[b_plc])
                    R.op(DVE, lambda h=h, plc=plc: nc.vector.scalar_tensor_tensor(out=Sret[:, h, :], in0=Sret[:, h, :], scalar=G128[h], in1=plc, op0=ALU.mult, op1=ALU.add),
                         reads=[b_Sret, b_plc], writes=[b_Sret])
                    R.op(ACT, lambda h=h: nc.scalar.copy(out=Sbf[:, h, :], in_=Sret[:, h, :]), reads=[b_Sret], writes=[b_Sbf])
                else:
                    plc, b_plc = pg[:, 3, :], b_pg[3]
                    for tl in range(2):
                        R.op(PE, lambda tl=tl, T=T, plc=plc, c=c: nc.tensor.matmul(plc[:, tl * 256:(tl + 1) * 256], lhsT=T["ktok"][:, tl * 128:(tl + 1) * 128], rhs=T["vz_bf"],
                                                                                  start=(c == 0 and tl == 0), stop=(c == NCH - 1 and tl == 1)),
                             reads=[T["b_ktok"], T["b_v"]], writes=[b_plc])
                    if c == NCH - 1:
                        R.op(DVE, lambda h=h, plc=plc: nc.vector.scalar_tensor_tensor(out=Sret[:, h, :], in0=Sret[:, h, :], scalar=G128[h] ** 16, in1=plc, op0=ALU.mult, op1=ALU.add),
                             reads=[b_Sret, b_plc], writes=[b_Sret])
                        R.op(ACT, lambda h=h: nc.scalar.copy(out=Sbf[:, h, :], in_=Sret[:, h, :]), reads=[b_Sret], writes=[b_Sbf])

    def prefix_segment(seg):
        for ct2 in range(4):
            wb, b_wb = load_wblock(w_in0, ct2 * 256)
            for hf in range(2):
                ct = 2 * ct2 + hf
                for nb in range(4):
                    pa, b_pa = next_pacc()
                    for kt in range(8):
                        R.op(PE, lambda pa=pa, wb=wb, kt=kt, nb=nb, hf=hf: nc.tensor.matmul(pa, lhsT=wb[:, kt, hf * 128:(hf + 1) * 128], rhs=xnT[:, kt, nb * 512:(nb + 1) * 512],
                                                                                            start=(kt == 0), stop=(kt == 7)), reads=[b_xnT, b_wb], writes=[b_pa])
                    R.op(ACT, lambda pa=pa, ct=ct, nb=nb: nc.scalar.copy(out=uT[:, ct, nb * 512:(nb + 1) * 512], in_=pa), reads=[b_pa], writes=[b_uT])
        yTf = yT.rearrange("p a b -> p (a b)")
        tbl = yTf[:, 0:8192].rearrange("p (l n) -> p l n", n=512)
        b_tbl = k.buf("ptbl")
        cosT = yTf[:, 8192:12288].bitcast(F32)
        sinT = yTf[:, 12288:16384].bitcast(F32)
        b_rot = k.buf("prot")
        R.dma(lambda: nc.sync.dma_start(out=cosT, in_=rot_d[seg, 0]), b_rot, writes=[b_rot])
        R.dma(lambda: nc.sync.dma_start(out=sinT, in_=rot_d[seg, 1]), b_rot, writes=[b_rot])
        zsc = carve_at(OFF_XB + 1040 + 2048, [128, 64], F32)
        b_cst = k.buf("pcst")
        R.dma(lambda: nc.sync.dma_start(out=zsc, in_=zsc_d), b_cst, writes=[b_cst])
        E = [carve_at(OFF_X + 2048 * j, [128, 4, 256], F32) for j in range(4)]
        bE = k.buf("pE")
        TTp = []
        for pp in range(2):
            base = OFF_E + pp * 768
            d_ = dict(ktok=carve_at(base, [128, 256], BF16), v_bf=carve_at(base + 256, [128, 256], BF16), vz_bf=carve_at(base + 512, [128, 256], BF16),
                      b_ktok=k.buf(), b_v=k.buf())
            d_["pv"], d_["b_pv"] = pacc[pp][:, 0:256], b_pacc[pp]
            tb_, btb_ = (pT[0], b_pT[0]) if pp == 0 else (pT2, b_pT2)
            d_["ptk"], d_["b_ptk"] = tb_[:, 0:2, :], btb_
            TTp.append(d_)
        plc, b_plc = pg[:, 3, :], b_pg[3]
        pE = [pg[:, 0, 0:256], pg[:, 1, 0:256]]
        bpE = [b_pg[0], b_pg[1]]
        rb = [((pacc[0], b_pacc[0]), (pacc[1], b_pacc[1])), ((pg[:, 2, :], b_pg[2]), (pg[:, 3, :], b_pg[3]))]
        rc = [0]

        def rotary_k(w, b_w):
            for nb in range(4):
                blk = slice(nb * 512, (nb + 1) * 512)
                bk = rb[rc[0] % 2]
                rc[0] += 1
                for hf in range(2):
                    ps_, bps_ = bk[hf]
                    for kt in range(8):
                        R.op(PE, lambda hf=hf, kt=kt, blk=blk, ps_=ps_: nc.tensor.matmul(ps_, lhsT=w[:, kt, hf * 128:(hf + 1) * 128], rhs=xnT[:, kt, blk], start=(kt == 0), stop=(kt == 7)),
                             reads=[b_xnT, b_w], writes=[bps_])
                (p0, bp0), (p1, bp1) = bk
                ta, tb = Tpre.rearrange("p a b -> p (a b)"), carve_at(OFF_XB + 3216, [128, 512], F32) if False else t1[:, 512:1024]
                ta = t1[:, 0:512]
                R.op(DVE, lambda blk=blk, p0=p0: nc.vector.tensor_tensor(out=ta, in0=p0, in1=cosT[:, blk], op=ALU.mult), reads=[bp0, b_rot], writes=[b_t1])
                R.op(DVE, lambda blk=blk, p1=p1: nc.vector.tensor_tensor(out=tb, in0=p1, in1=sinT[:, blk], op=ALU.mult), reads=[bp1, b_rot], writes=[b_t1])
                R.op(DVE, lambda blk=blk: nc.vector.tensor_tensor(out=krT[:, 0, blk], in0=ta, in1=tb, op=ALU.subtract), reads=[b_t1], writes=[b_krT])
                R.op(DVE, lambda blk=blk, p0=p0: nc.vector.tensor_tensor(out=ta, in0=p0, in1=sinT[:, blk], op=ALU.mult), reads=[bp0, b_rot, b_krT], writes=[b_t1])
                R.op(DVE, lambda blk=blk, p1=p1: nc.vector.tensor_tensor(out=tb, in0=p1, in1=cosT[:, blk], op=ALU.mult), reads=[bp1, b_rot], writes=[b_t1])
                R.op(DVE, lambda blk=blk: nc.vector.tensor_tensor(out=krT[:, 1, blk], in0=ta, in1=tb, op=ALU.add), reads=[b_t1], writes=[b_krT])

        def tile_steps(ct):
            R.dma(lambda: nc.sync.dma_start(out=tbl, in_=tabB[ct].rearrange("r p n -> p r n")), b_tbl, reads=[b_tabB], writes=[b_tbl])
            deinterleave(ct)
            for kk in range(4):
                for ri in range(2):
                    for s_ in range(8):
                        lag = 7 - s_
                        R.op(PE, lambda lag=lag, ri=ri, kk=kk, s_=s_: nc.tensor.matmul(pE[ri], lhsT=tbl[:, 2 * lag + ri, kk * 128:(kk + 1) * 128], rhs=uS3[:, s_, :],
                                                                                       start=(s_ == 0), stop=(s_ == 7)), reads=[b_tbl, b_t1], writes=[bpE[ri]])
                    R.op(ACT, lambda ri=ri, kk=kk: nc.scalar.copy(out=E[ri][:, kk, :], in_=pE[ri]), reads=[bpE[ri]], writes=[bE])
                if kk % 2 == 1:
                    yield
            inject(ct, E[0][:, :, 0], E[1][:, :, 0], Tpre[:, :, 0], [bE, b_Tpre], [bE, b_Tpre])
            yield
            rw = [bE, b_pw, b_Tpre]
            for m in range(8):
                w = 128 >> m
                src = (E[0], E[1]) if m % 2 == 0 else (E[2], E[3])
                dst = (E[2], E[3]) if m % 2 == 0 else (E[0], E[1])
                ev = [x_[:, :, 0:2 * w].rearrange("p a (k two) -> p a k two", two=2)[:, :, :, 0] for x_ in src]
                od = [x_[:, :, 0:2 * w].rearrange("p a (k two) -> p a k two", two=2)[:, :, :, 1] for x_ in src]
                dr_, di_ = dst[0][:, :, 0:w], dst[1][:, :, 0:w]
                t_ = Tpre[:, :, 0:w]
                Pr, Pi = pwb(ct, 8 + m, 0, w), pwb(ct, 8 + m, 1, w)
                VTT(t_, ev[0], Pr, ALU.mult, rw, [b_Tpre])
                VTT(dr_, t_, od[0], ALU.add, rw, [bE])
                VTT(t_, ev[1], Pi, ALU.mult, rw, [b_Tpre])
                VTT(dr_, dr_, t_, ALU.subtract, rw, [bE])
                if m < 3:
                    yield
                VTT(t_, ev[1], Pr, ALU.mult, rw, [b_Tpre])
                VTT(di_, t_, od[1], ALU.add, rw, [bE])
                VTT(t_, ev[0], Pi, ALU.mult, rw, [b_Tpre])
                VTT(di_, di_, t_, ALU.add, rw, [bE])
                yield
            R.op(DVE, lambda: nc.vector.tensor_copy(out=Ss5[:, 4 * ct:4 * ct + 4, 0], in_=E[0][:, :, 0]), reads=[bE], writes=[b_Ss5])
            R.op(DVE, lambda: nc.vector.tensor_copy(out=Ss5[:, 4 * ct:4 * ct + 4, 1], in_=E[1][:, :, 0]), reads=[bE], writes=[b_Ss5])

        def chunk(h, c, wv, b_wv):
            ch = slice(c * 128, (c + 1) * 128)
            T = TTp[c % 2]
            for kt in range(8):
                R.op(PE, lambda kt=kt, ch=ch, T=T, wv=wv: nc.tensor.matmul(T["pv"], lhsT=xnT[:, kt, ch], rhs=wv[:, kt, :], start=(kt == 0), stop=(kt == 7)),
                     reads=[b_xnT, b_wv], writes=[T["b_pv"]])
            vsc = zsc[:, 16 * h + c:16 * h + c + 1]
            R.op(ACT, lambda vsc=vsc, T=T: nc.scalar.activation(out=T["vz_bf"], in_=T["pv"], func=AF.Copy, scale=vsc), reads=[T["b_pv"], b_cst], writes=[T["b_v"]])
            for tl in range(2):
                R.op(PE, lambda tl=tl, ch=ch, T=T: nc.tensor.transpose(out=T["ptk"][:, tl, :], in_=krT[:, tl, ch], identity=ident_b), reads=[b_krT, b_ident], writes=[T["b_ptk"]])
            R.op(ACT, lambda T=T: nc.scalar.copy(out=T["ktok"].rearrange("p (a b) -> p a b", b=128), in_=T["ptk"]), reads=[T["b_ptk"]], writes=[T["b_ktok"]])
            for tl in range(2):
                R.op(PE, lambda tl=tl, T=T, c=c: nc.tensor.matmul(plc[:, tl * 256:(tl + 1) * 256], lhsT=T["ktok"][:, tl * 128:(tl + 1) * 128], rhs=T["vz_bf"],
                                                                 start=(c == 0 and tl == 0), stop=(c == NCH - 1 and tl == 1)),
                     reads=[T["b_ktok"], T["b_v"]], writes=[b_plc])
            if c == NCH - 1:
                R.op(DVE, lambda h=h: nc.vector.scalar_tensor_tensor(out=Sret[:, h, :], in0=Sret[:, h, :], scalar=G128[h] ** 16, in1=plc, op0=ALU.mult, op1=ALU.add),
                     reads=[b_Sret, b_plc], writes=[b_Sret])
                R.op(ACT, lambda h=h: nc.scalar.copy(out=Sbf[:, h, :], in_=Sret[:, h, :]), reads=[b_Sret], writes=[b_Sbf])

        for h in range(4):
            wk, b_wk = load_wblock(w_in0, 3072 + h * 256)
            rotary_k(wk, b_wk)
            wv, b_wv = load_wblock(w_in0, 4096 + h * 256)
            for half in range(2):
                gen = tile_steps(2 * h + half)
                for c in range(8 * half, 8 * half + 8):
                    chunk(h, c, wv, b_wv)
                    for _ in range(2):
                        try:
                            next(gen)
                        except StopIteration:
                            break
                for _ in gen:
                    pass

    def layer0(dst_dram):
        own_x = x_seq[3 * SEG:4 * SEG, :]
        load_gain(norm_even)
        s5_precompute()
        R.op(DVE, lambda: nc.vector.memset(Sret, 0.0), reads=[], writes=[b_Sret])
        R.op(DVE, lambda: nc.vector.memset(Sbf, 0.0), reads=[], writes=[b_Sbf])
        R.op(DVE, lambda: nc.vector.memset(Ss5, 0.0), reads=[], writes=[b_Ss5])
        wo0 = carve_at(OFF_X, [128, 8, D], BF16)
        for seg in range(SEG0, NSEG):
            own = seg == NSEG - 1
            norm_transpose(x_seq, seg * SEG)
            R.barrier()
            if not own:
                prefix_segment(seg)
                R.barrier()
                continue
            s5_segment(own)
            if own:
                R.barrier()
                glu()
                R.barrier()
                b_wo0 = k.buf("wo0a")
                out_proj_half(w_out0, 0, own_x, dst_dram, wo0, b_wo0)
            R.barrier()
            ret_segment(seg, own)
            if own:
                R.barrier()
                b_wo0 = k.buf("wo0b")
                out_proj_half(w_out0, 1024, dst_dram, dst_dram, wo0, b_wo0)
            R.barrier()

    def final(src_dram):
        R.barrier()
        apos[0] = 0
        fg = carve([128, D], F32)
        b_fg = k.buf("fg")
        R.dma(lambda: nc.sync.dma_start(out=fg, in_=final_norm.partition_broadcast(128)), b_fg, writes=[b_fg])
        for c in range(NCH):
            i = c % 2
            key = (id(src_dram), c)
            R.dma(lambda i=i, c=c: nc.sync.dma_start(out=xc[i], in_=src_dram[c * 128:(c + 1) * 128, :]), b_xc[i],
                  reads=[b_x[key]] if key in b_x else [], writes=[b_xc[i]])
            R.op(ACT, lambda i=i: nc.scalar.activation(out=junk, in_=xc[i], func=AF.Square, accum_out=stat[i][:, 0:1]),
                 reads=[b_xc[i]], writes=[b_junk, b_stat[i]])
            R.op(DVE, lambda i=i: nc.vector.tensor_scalar(out=stat[i][:, 1:2], in0=stat[i][:, 0:1], scalar1=1.0 / D, scalar2=EPS,
                                                          op0=ALU.mult, op1=ALU.add), reads=[b_stat[i]], writes=[b_stat[i]])
            R.op(ACT, lambda i=i: nc.scalar.activation(out=stat[i][:, 3:4], in_=stat[i][:, 1:2], func=AF.Sqrt), reads=[b_stat[i]], writes=[b_stat[i]])
            R.op(DVE, lambda i=i: nc.vector.reciprocal(out=stat[i][:, 2:3], in_=stat[i][:, 3:4]), reads=[b_stat[i]], writes=[b_stat[i]])
            R.op(DVE, lambda i=i: nc.vector.scalar_tensor_tensor(out=xc[i], in0=xc[i], scalar=stat[i][:, 2:3], in1=fg, op0=ALU.mult, op1=ALU.mult),
                 reads=[b_xc[i], b_stat[i], b_fg], writes=[b_xc[i]])
            R.dma(lambda i=i, c=c: nc.sync.dma_start(out=out[c * 128:(c + 1) * 128, :], in_=xc[i]), b_xc[i], reads=[b_xc[i]], writes=[])

    SEG0 = 0 if mode != "l0own" else 3
    if mode == "l1":
        layer1(x_seq[3 * SEG:4 * SEG, :], x2)
        final(x2)
    elif mode == "pre":
        load_gain(norm_even)
        s5_precompute()
        for c in range(NCH):
            i = c % 2
            R.dma(lambda i=i, c=c: nc.sync.dma_start(out=xc[i], in_=x_seq[3 * SEG + c * 128:3 * SEG + (c + 1) * 128, :]), b_xc[i], writes=[b_xc[i]])
            R.dma(lambda i=i, c=c: nc.sync.dma_start(out=out[c * 128:(c + 1) * 128, :], in_=xc[i]), b_xc[i], reads=[b_xc[i]], writes=[])
    elif mode in ("l0", "l0own"):
        layer0(x1)
        R.barrier()
        for c in range(NCH):
            i = c % 2
            R.dma(lambda i=i, c=c: nc.sync.dma_start(out=xc[i], in_=x1[c * 128:(c + 1) * 128, :]), b_xc[i], reads=[b_x[(id(x1), c)]], writes=[b_xc[i]])
            R.dma(lambda i=i, c=c: nc.sync.dma_start(out=out[c * 128:(c + 1) * 128, :], in_=xc[i]), b_xc[i], reads=[b_xc[i]], writes=[])
    else:
        layer0(x1)
        layer1(x1, x2)
        final(x2)
    rec.emit()
    return nc


_CACHE = {}


def _consts(q):
    tril = np.triu(np.ones((128, 128), np.float32))
    ident = np.eye(128, dtype=np.float32)
    half = 128
    inv = 10000.0 ** (-np.arange(half, dtype=np.float64) / half)
    pos = (q * SEG - (NSEG - 1) * SEG) + np.arange(NSEG * SEG, dtype=np.float64)
    ang = inv[:, None] * pos[None, :]
    rot = np.stack([np.cos(ang), np.sin(ang)], 0)
    rot = rot.reshape(2, 128, NSEG, SEG).transpose(2, 0, 1, 3).astype(np.float32)
    gam = 1.0 - 2.0 ** (-5.0 - np.arange(4, dtype=np.float64))
    idx = np.arange(128, dtype=np.float64)
    diff = idx[None, :] - idx[:, None]
    dtab = np.where(diff[:, None, :] >= 0, gam[None, :, None] ** np.maximum(diff[:, None, :], 0.0), 0.0) * (256.0 ** -0.5)
    xi = gam[None, :] ** (idx[:, None] + 1.0)
    zs = gam[None, :] ** (127.0 - idx[:, None]) * (256.0 ** -0.5)
    cc = np.arange(16, dtype=np.float64)
    zsc = zs[:, :, None] * (gam[None, :, None] ** (128.0 * (15.0 - cc[None, None, :])))
    return {"c_tril": tril, "c_ident": ident, "c_rot": np.ascontiguousarray(rot), "c_dt": dtab.astype(np.float32),
            "c_xizs": np.concatenate([xi, zs], 1).astype(np.float32), "c_zsc": zsc.reshape(128, 64).astype(np.float32)}


def make_maps(inputs):
    x = np.asarray(inputs["x"], np.float32)
    sq = lambda n: np.ascontiguousarray(np.asarray(inputs[n], np.float32)[0])
    shared = {n: sq(n) for n in ["norm_even", "w_in_even", "s5_lam_re", "s5_lam_im", "s5_log_dt", "s5_b_re", "s5_b_im", "s5_c_re", "s5_c_im",
                                 "s5_d", "s5_w_glu", "s5_b_glu", "ret_gn_gain", "w_out_even", "norm_odd", "w_in_odd", "sgu_norm_gain",
                                 "sgu_w_spatial", "sgu_b_spatial", "w_out_odd"]}
    shared["final_norm"] = np.ascontiguousarray(np.asarray(inputs["final_norm"], np.float32))
    maps = []
    for c in range(8):
        b, q = c // 4, c % 4
        xs = np.zeros((NSEG * SEG, D), np.float32)
        lo = q * SEG - (NSEG - 1) * SEG
        src = x[b, max(lo, 0):(q + 1) * SEG]
        xs[NSEG * SEG - src.shape[0]:] = src
        m = dict(shared)
        m["x_seq"] = xs
        m.update(_consts(q))
        maps.append(m)
    return maps


def kernel(**inputs):
    maps = make_maps(inputs)
    nc = bass.Bass("TRN2", target_bir_lowering=False)
    build(nc, "full")
    res = run_bass_kernel_spmd(nc, maps, core_ids=list(range(8)))
    out = np.stack([np.asarray(r["out"], np.float32) for r in res.results]).reshape(2, 4 * SEG, D)
    return out
```

```python
import math
import os
import numpy as np
SKIP = {k_: True for k_ in os.environ.get('KSKIP', '').split(',') if k_}
import ml_dtypes
import concourse.bass as bass
import concourse.mybir as mybir
from concourse.bass_utils import run_bass_kernel_spmd

F32 = mybir.dt.float32
BF16 = mybir.dt.bfloat16
AF = mybir.ActivationFunctionType
ALU = mybir.AluOpType
AX = mybir.AxisListType

D = 1024
SEG = 2048
NCH = SEG // 128
NSEG = 4
EPS = 1e-6
PE, ACT, DVE, POOL, SP = 0, 1, 2, 3, 4


class Buf:
    __slots__ = ("name", "w", "r", "dsem", "dcnt")

    def __init__(self, name):
        self.name = name
        self.w = None
        self.r = []
        self.dsem = None
        self.dcnt = 0


class Rec:
    def __init__(self, nc):
        self.nc = nc
        self.ops = []
        self.engs = [nc.tensor, nc.scalar, nc.vector, nc.gpsimd, nc.sync]

    def op(self, eng, fn, reads=(), writes=(), dma=False):
        idx = len(self.ops)
        deps = set()
        raw = set()
        for b in reads:
            if b.w is not None:
                deps.add(b.w)
                raw.add(b.w)
        for b in writes:
            if b.w is not None:
                deps.add(b.w)
            for r in b.r:
                deps.add(r)
        self.ops.append(dict(eng=eng, fn=fn, deps=deps, raw=raw, dma=dma, sig=False, dbuf=None))
        for b in reads:
            if not dma:
                b.r = [r for r in b.r if self.ops[r]["dma"] or self.ops[r]["eng"] != eng]
            b.r.append(idx)
        for b in writes:
            b.w = idx
            b.r = []
        return idx

    def barrier(self):
        n = len(self.ops)
        lb = getattr(self, "_lb", 0)
        deps = set()
        last = {}
        for j in range(lb, n):
            o = self.ops[j]
            if o["dma"]:
                deps.add(j)
            else:
                last[o["eng"]] = j
        deps.update(last.values())
        for e in range(5):
            eng = self.engs[e]
            self.ops.append(dict(eng=e, fn=(lambda eng=eng: eng.nop()), deps=set(deps), raw=set(), dma=False, sig=False, dbuf=None))
        self._lb = n

    def dma(self, fn, sbuf_side, reads=(), writes=(), eng=SP):
        idx = self.op(eng, fn, reads, writes, dma=True)
        self.ops[idx]["dbuf"] = sbuf_side
        return idx

    def emit(self):
        nc = self.nc
        ops = self.ops
        for i, o in enumerate(ops):
            for d in o["deps"]:
                p = ops[d]
                if p["dma"] or p["eng"] != o["eng"] or o["dma"] or o["eng"] != PE:
                    p["sig"] = True
        esem = [nc.alloc_semaphore("es%d" % i) for i in range(5)]
        ecnt = [0] * 5
        tok = [None] * len(ops)
        waited = [dict() for _ in range(5)]
        final = {}
        for i, o in enumerate(ops):
            e = o["eng"]
            eng = self.engs[e]
            need = {}
            for d in o["deps"]:
                p = ops[d]
                if (not p["dma"]) and p["eng"] == e and not o["dma"] and e == PE:
                    continue
                if (not p["dma"]) and p["eng"] == e and o["dma"] and e == SP:
                    continue
                s, v = tok[d]
                k = id(s)
                if k not in need or need[k][1] < v:
                    need[k] = (s, v)
            for k, (s, v) in need.items():
                if waited[e].get(k, 0) >= v:
                    continue
                eng.wait_ge(s, v)
                waited[e][k] = v
            ins = o["fn"]()
            if o["dma"]:
                b = o["dbuf"]
                if b.dsem is None or b.dcnt >= 800:
                    b.dcnt = 0
                    self._nsem = getattr(self, "_nsem", 0) + 1
                    b.dsem = nc.alloc_semaphore("d%d_%s" % (self._nsem, b.name))
                b.dcnt += 16
                ins.then_inc(b.dsem, 16)
                tok[i] = (b.dsem, b.dcnt)
                final[id(b.dsem)] = (b.dsem, b.dcnt)
            elif o["sig"]:
                if ecnt[e] >= 3000:
                    self._nsem = getattr(self, "_nsem", 0) + 1
                    esem[e] = nc.alloc_semaphore("es%d_%d" % (e, self._nsem))
                    ecnt[e] = 0
                ecnt[e] += 1
                ins.then_inc(esem[e], 1)
                tok[i] = (esem[e], ecnt[e])
            o["fn"] = None
        for k, (s, v) in final.items():
            if waited[SP].get(k, 0) < v:
                nc.sync.wait_ge(s, v)


class K:
    def __init__(self, nc, rec):
        self.nc = nc
        self.rec = rec
        self.nb = 0

    def sb(self, name, shape, dt):
        t = self.nc.alloc_sbuf_tensor(name, list(shape), dt).ap()
        return t

    def ps(self, name, shape, dt=F32):
        return self.nc.alloc_psum_tensor(name, list(shape), dt).ap()

    def buf(self, name=None):
        self.nb += 1
        return Buf(name or ("b%d" % self.nb))


def bcast_rows(ap_1d_dram, n):
    return ap_1d_dram.partition_broadcast(128)


def build(nc, mode="full"):
    rec = Rec(nc)
    k = K(nc, rec)
    dr = lambda name, shape, dt=F32, kind="ExternalInput": nc.dram_tensor(name, list(shape), dt, kind=kind).ap()
    x_seq = dr("x_seq", [NSEG * SEG, D])
    w_in1 = dr("w_in_odd", [D, 6144])
    w_out1 = dr("w_out_odd", [2048, D])
    norm_odd = dr("norm_odd", [D])
    sgu_gain = dr("sgu_norm_gain", [2048])
    sgu_w = dr("sgu_w_spatial", [4, 128, 128])
    sgu_b = dr("sgu_b_spatial", [4, 128])
    final_norm = dr("final_norm", [D])
    tril = dr("c_tril", [128, 128])
    ident_d = dr("c_ident", [128, 128])
    out = dr("out", [SEG, D], F32, kind="ExternalOutput")
    x1 = dr("x1_scratch", [SEG, D], F32, kind="Internal")
    x2 = dr("x2_scratch", [SEG, D], F32, kind="Internal")

    nct = nc
    R = rec

    ident_f = k.sb("ident_f", [128, 128], F32)
    ident_b = k.sb("ident_b", [128, 128], BF16)
    b_ident = k.buf("ident")
    R.dma(lambda: nc.sync.dma_start(out=ident_f, in_=ident_d), b_ident, writes=[b_ident])
    R.op(DVE, lambda: nc.vector.tensor_copy(out=ident_b, in_=ident_f), reads=[b_ident], writes=[b_ident])

    ARENA = 53248
    arena = k.sb("arena", [128, ARENA], BF16)
    apos = [0]

    def carve(shape, dt):
        n = 1
        for d_ in shape[1:]:
            n *= d_
        nb16 = n * (2 if dt == F32 else 1)
        a = apos[0]
        apos[0] += nb16
        assert apos[0] <= ARENA, apos[0]
        v = arena[:, a:a + nb16]
        if dt == F32:
            v = v.bitcast(F32)
        if len(shape) == 3:
            v = v.rearrange("p (a b) -> p a b", b=shape[2])
        return v

    def carve_at(off, shape, dt):
        n = 1
        for d_ in shape[1:]:
            n *= d_
        nb16 = n * (2 if dt == F32 else 1)
        assert off + nb16 <= ARENA, (off, nb16)
        v = arena[:, off:off + nb16]
        if dt == F32:
            v = v.bitcast(F32)
        if len(shape) == 3:
            v = v.rearrange("p (a b) -> p a b", b=shape[2])
        elif len(shape) == 4:
            v = v.rearrange("p (a b c) -> p a b c", b=shape[2], c=shape[3])
        return v

    xnT = k.sb("xnT", [128, 8, SEG], BF16)
    b_xnT = k.buf("xnT")
    pg = k.ps("pg", [128, 4, 512], F32)
    b_pg = [k.buf("pg%d" % i) for i in range(4)]
    yT = k.sb("yT", [128, 8, SEG], BF16)
    b_yT = k.buf("yT")

    xc = [k.sb("xc%d" % i, [128, D], F32) for i in range(2)]
    b_xc = [k.buf("xc%d" % i) for i in range(2)]
    xb = [k.sb("xb0", [128, D], BF16)] * 2
    b_xb = [k.buf("xb0")] * 2
    t1 = k.sb("t1", [128, 1024], F32)
    b_t1 = k.buf("t1")
    junk = t1[:, :D]
    b_junk = b_t1
    stat = [k.sb("stat%d" % i, [128, 8], F32) for i in range(2)]
    b_stat = [k.buf("stat%d" % i) for i in range(2)]
    pT = [k.ps("pT0", [128, 8, 128], BF16)] * 2
    pT2 = k.ps("pT2", [128, 8, 128], BF16)
    b_pT2 = k.buf("pT2")
    b_pT = [k.buf("pT0")] * 2
    b_ptq_fix = [b_pT[0], b_pT2]

    def norm_transpose(src_dram, row0):
        for c in range(NCH):
            i = c % 2
            rows = src_dram[row0 + c * 128: row0 + (c + 1) * 128, :]
            skey = (id(src_dram), c)
            R.dma(lambda i=i, rows=rows: nc.sync.dma_start(out=xc[i], in_=rows), b_xc[i], reads=[b_x[skey]] if (row0 == 0 and skey in b_x) else [], writes=[b_xc[i]])
            R.op(ACT, lambda i=i: nc.scalar.activation(out=junk, in_=xc[i], func=AF.Square, accum_out=stat[i][:, 0:1]),
                 reads=[b_xc[i]], writes=[b_junk, b_stat[i]])
            R.op(DVE, lambda i=i: nc.vector.tensor_scalar(out=stat[i][:, 1:2], in0=stat[i][:, 0:1], scalar1=1.0 / D, scalar2=EPS,
                                                          op0=ALU.mult, op1=ALU.add), reads=[b_stat[i]], writes=[b_stat[i]])
            R.op(ACT, lambda i=i: nc.scalar.activation(out=stat[i][:, 3:4], in_=stat[i][:, 1:2], func=AF.Sqrt), reads=[b_stat[i]], writes=[b_stat[i]])
            R.op(DVE, lambda i=i: nc.vector.reciprocal(out=stat[i][:, 2:3], in_=stat[i][:, 3:4]), reads=[b_stat[i]], writes=[b_stat[i]])
            R.op(DVE, lambda i=i: nc.vector.scalar_tensor_tensor(out=xb[i], in0=xc[i], scalar=stat[i][:, 2:3], in1=gbc, op0=ALU.mult, op1=ALU.mult),
                 reads=[b_xc[i], b_stat[i], b_gbc], writes=[b_xb[i]])
            for kt in range(8):
                R.op(PE, lambda i=i, kt=kt: nc.tensor.transpose(out=pT[i][:, kt, :], in_=xb[i][:, kt * 128:(kt + 1) * 128], identity=ident_b),
                     reads=[b_xb[i], b_ident], writes=[b_pT[i]])
            R.op(DVE, lambda i=i, c=c: nc.vector.tensor_copy(out=xnT[:, :, c * 128:(c + 1) * 128], in_=pT[i]),
                 reads=[b_pT[i]], writes=[b_xnT])

    wbf = [k.sb("wbf%d" % i, [128, 8, 256], BF16) for i in range(3)]
    b_wbf = [k.buf("wbf%d" % i) for i in range(3)]
    gbc = k.sb("gbc", [128, D], F32)
    b_gbc = k.buf("gbc")
    wctr = [0]

    def load_gain(g_dram):
        R.dma(lambda: nc.sync.dma_start(out=gbc, in_=g_dram.partition_broadcast(128)), b_gbc, writes=[b_gbc])

    def load_wblock(w_dram, c0, ncols=256, gain=True):
        i = wctr[0] % 3
        wctr[0] += 1
        src = w_dram.rearrange("(kt p) n -> p kt n", p=128)[:, :, c0:c0 + ncols]
        R.dma(lambda: nc.gpsimd.dma_start(out=wbf[i][:, :, :ncols], in_=src), b_wbf[i], writes=[b_wbf[i]], eng=POOL)
        return wbf[i], b_wbf[i]

    pacc = [k.ps("pacc%d" % i, [128, 512], F32) for i in range(2)]
    b_pacc = [k.buf("pacc%d" % i) for i in range(2)]
    pctr = [0]

    wide = [True]

    def next_pacc():
        if wide[0]:
            i = pctr[0] % 6
            pctr[0] += 1
            if i < 2:
                return pacc[i], b_pacc[i]
            return pg[:, i - 2, :], b_pg[i - 2]
        i = pctr[0] % 2
        pctr[0] += 1
        return pacc[i], b_pacc[i]

    def layer1(src_dram, dst_dram):
        load_gain(norm_odd)
        norm_transpose(src_dram, 0)
        R.barrier()
        apos[0] = 0
        vgain = carve([128, 2048], F32)
        b_vgain = k.buf("vgain")
        R.dma(lambda: nc.sync.dma_start(out=vgain, in_=sgu_gain.partition_broadcast(128)), b_vgain, writes=[b_vgain])
        bsp = k.sb("bsp", [128, 4, 128], F32)
        b_bsp = k.buf("bsp")
        R.dma(lambda: nc.sync.dma_start(out=bsp, in_=sgu_b.partition_broadcast(128)), b_bsp, writes=[b_bsp])
        wraw = k.sb("wraw", [128, 4, 128], F32)
        b_wraw = k.buf("wraw")
        R.dma(lambda: nc.sync.dma_start(out=wraw, in_=sgu_w.rearrange("g t s -> t g s")), b_wraw, writes=[b_wraw])
        trl = k.sb("trl", [128, 128], F32)
        b_trl = k.buf("trl")
        R.dma(lambda: nc.sync.dma_start(out=trl, in_=tril), b_trl, writes=[b_trl])
        wmT = k.sb("wmT", [128, 4, 128], BF16)
        b_wmT = k.buf("wmT")
        ptw = pacc[0].rearrange("p (g t) -> p g t", t=128)
        b_ptw = b_pacc[0]
        for g in range(4):
            R.op(PE, lambda g=g: nc.tensor.transpose(out=ptw[:, g, :], in_=wraw[:, g, :], identity=ident_f),
                 reads=[b_wraw, b_ident], writes=[b_ptw])
        R.op(DVE, lambda: nc.vector.tensor_tensor(out=wmT, in0=ptw, in1=trl.unsqueeze(1).to_broadcast([128, 4, 128]), op=ALU.mult),
             reads=[b_ptw, b_trl], writes=[b_wmT])

        vn = carve([128, NCH, 2048], BF16)
        b_vn = k.buf("vn")
        vf = carve([128, 2048], F32)
        b_vf = k.buf("vf")
        bst = k.sb("bst", [128, 4, 6], F32)
        mv = k.sb("mv", [128, 4], F32)
        b_bst = k.buf("bst")

        def gelu_from(psrc, b_psrc, dst, b_dst, n):
            R.op(ACT, lambda: nc.scalar.activation(out=dst, in_=psrc, func=AF.Gelu_apprx_tanh), reads=[b_psrc], writes=[b_dst])

        uT1 = carve([128, SEG], BF16)
        b_uT1 = k.buf("uT1")
        zT1 = carve([128, SEG], BF16)
        b_zT1 = k.buf("zT1")
        vf2 = arena[:, apos[0] - 4096:apos[0]].bitcast(F32)
        vfs = [vf, vf2]
        b_vfs = [[b_vf], [b_uT1, b_zT1]]
        for blk in range(8):
            src = w_in1.rearrange("(kt p) n -> p kt n", p=128)[:, :, 2048 + blk * 256:2048 + (blk + 1) * 256]
            R.dma(lambda blk=blk, src=src: nc.gpsimd.dma_start(out=yT[:, :, blk * 256:(blk + 1) * 256], in_=src), b_yT, writes=[b_yT], eng=POOL)
        for c in range(NCH):
            vfc, bvf = vfs[c % 2], b_vfs[c % 2]
            for blk in range(8):
                pa, b_pa = next_pacc()
                for kt in range(8):
                    R.op(PE, lambda pa=pa, kt=kt, c=c, blk=blk: nc.tensor.matmul(pa[:, :256], lhsT=xnT[:, kt, c * 128:(c + 1) * 128], rhs=yT[:, kt, blk * 256:(blk + 1) * 256],
                                                                                 start=(kt == 0), stop=(kt == 7)),
                         reads=[b_xnT, b_yT], writes=[b_pa])
                R.op(ACT, lambda pa=pa, vfc=vfc, blk=blk: nc.scalar.activation(out=vfc[:, blk * 256:(blk + 1) * 256], in_=pa[:, :256], func=AF.Gelu_apprx_tanh), reads=[b_pa], writes=bvf)
            for j in range(4):
                R.op(DVE, lambda j=j, vfc=vfc: nc.vector.bn_stats(out=bst[:, j, :], in_=vfc[:, j * 512:(j + 1) * 512]), reads=bvf, writes=[b_bst])
            R.op(DVE, lambda: nc.vector.bn_aggr(out=mv[:, 0:2], in_=bst), reads=[b_bst], writes=[b_bst])
            R.op(DVE, lambda: nc.vector.tensor_scalar(out=mv[:, 3:4], in0=mv[:, 1:2], scalar1=EPS, scalar2=None, op0=ALU.add),
                 reads=[b_bst], writes=[b_bst])
            R.op(ACT, lambda: nc.scalar.activation(out=mv[:, 3:4], in_=mv[:, 3:4], func=AF.Sqrt), reads=[b_bst], writes=[b_bst])
            R.op(DVE, lambda: nc.vector.reciprocal(out=mv[:, 2:3], in_=mv[:, 3:4]), reads=[b_bst], writes=[b_bst])
            R.op(DVE, lambda vfc=vfc: nc.vector.tensor_scalar(out=vfc, in0=vfc, scalar1=mv[:, 0:1], scalar2=mv[:, 2:3], op0=ALU.subtract, op1=ALU.mult),
                 reads=bvf + [b_bst], writes=bvf)
            R.op(DVE, lambda c=c, vfc=vfc: nc.vector.tensor_tensor(out=vn[:, c, :], in0=vfc, in1=vgain, op=ALU.mult), reads=bvf + [b_vgain], writes=[b_vn])

        psT = pg
        b_psT = k.buf("psT")
        wo = carve([128, 8, D], BF16)
        b_wo = k.buf("wo")
        for half in range(2):
            for jt in range(8):
                j = half * 8 + jt
                g = j // 4
                if jt % 2 == 0:
                    wu, b_wu = load_wblock(w_in1, j * 128)
                    wz, b_wz = load_wblock(w_in1, 4096 + j * 128)
                    off = 0
                else:
                    off = 128
                for nb in range(4):
                    pa, b_pa = next_pacc()
                    for kt in range(8):
                        R.op(PE, lambda pa=pa, wu=wu, kt=kt, nb=nb, off=off: nc.tensor.matmul(pa, lhsT=wu[:, kt, off:off + 128], rhs=xnT[:, kt, nb * 512:(nb + 1) * 512],
                                                                                              start=(kt == 0), stop=(kt == 7)),
                             reads=[b_xnT, b_wu], writes=[b_pa])
                    gelu_from(pa, b_pa, uT1[:, nb * 512:(nb + 1) * 512], b_uT1, 512)
                    pz, b_pz = next_pacc()
                    for kt in range(8):
                        R.op(PE, lambda pz=pz, wz=wz, kt=kt, nb=nb, off=off: nc.tensor.matmul(pz, lhsT=wz[:, kt, off:off + 128], rhs=xnT[:, kt, nb * 512:(nb + 1) * 512],
                                                                                              start=(kt == 0), stop=(kt == 7)),
                             reads=[b_xnT, b_wz], writes=[b_pz])
                    R.op(ACT, lambda pz=pz, nb=nb: nc.scalar.activation(out=zT1[:, nb * 512:(nb + 1) * 512], in_=pz, func=AF.Silu),
                         reads=[b_pz], writes=[b_zT1])
                for c in range(NCH):
                    R.op(PE, lambda c=c, j=j, g=g: nc.tensor.matmul(psT[:, c // 4, (c % 4) * 128:(c % 4 + 1) * 128], lhsT=vn[:, c, j * 128:(j + 1) * 128],
                                                                     rhs=wmT[:, g, :], start=True, stop=True),
                         reads=[b_vn, b_wmT], writes=[b_pg[c // 4]])
                R.op(DVE, lambda: nc.vector.tensor_tensor(out=uT1, in0=uT1, in1=zT1, op=ALU.mult), reads=[b_uT1, b_zT1], writes=[b_uT1])
                R.op(DVE, lambda g=g: nc.vector.tensor_tensor(out=zT1.rearrange("p (c t) -> p c t", t=128), in0=psT.rearrange("p a (b t) -> p (a b) t", t=128),
                                                              in1=bsp[:, g:g + 1, :].to_broadcast([128, NCH, 128]), op=ALU.add),
                     reads=b_pg + [b_bsp], writes=[b_zT1])
                R.op(DVE, lambda jt=jt: nc.vector.tensor_tensor(out=yT[:, jt, :], in0=uT1, in1=zT1, op=ALU.mult), reads=[b_uT1, b_zT1], writes=[b_yT])
            out_proj_half(w_out1, half * 1024, src_dram if half == 0 else dst_dram, dst_dram, wo, b_wo)

    b_x = {}

    def out_proj_half(w_dram, row0, srcd, dstd, wo, b_wo):
        for ct2 in range(2):
            r0 = row0 + ct2 * 512
            R.dma(lambda r0=r0, ct2=ct2: nc.gpsimd.dma_start(out=wo[:, 4 * ct2:4 * ct2 + 4, :], in_=w_dram[r0:r0 + 512, :].rearrange("(c p) n -> p c n", p=128)), b_wo, writes=[b_wo], eng=POOL)
        for c in range(NCH):
            i = c % 2
            skey = (id(srcd), c)
            R.dma(lambda i=i, c=c: nc.sync.dma_start(out=xc[i], in_=srcd[c * 128:(c + 1) * 128, :]), b_xc[i],
                  reads=[b_x[skey]] if skey in b_x else [], writes=[b_xc[i]])
            for nb in range(2):
                pa, b_pa = next_pacc()
                for ct in range(8):
                    R.op(PE, lambda pa=pa, ct=ct, c=c, nb=nb: nc.tensor.matmul(pa, lhsT=yT[:, ct, c * 128:(c + 1) * 128], rhs=wo[:, ct, nb * 512:(nb + 1) * 512],
                                                                               start=(ct == 0), stop=(ct == 7)),
                         reads=[b_yT, b_wo], writes=[b_pa])
                R.op(DVE, lambda pa=pa, i=i, nb=nb: nc.vector.tensor_tensor(out=xc[i][:, nb * 512:(nb + 1) * 512], in0=xc[i][:, nb * 512:(nb + 1) * 512], in1=pa, op=ALU.add),
                     reads=[b_pa, b_xc[i]], writes=[b_xc[i]])
            key = (id(dstd), c)
            if key not in b_x:
                b_x[key] = k.buf("xd")
            R.dma(lambda i=i, c=c: nc.sync.dma_start(out=dstd[c * 128:(c + 1) * 128, :], in_=xc[i]), b_xc[i],
                  reads=[b_xc[i]], writes=[b_x[key]])

    norm_even = dr("norm_even", [D])
    w_in0 = dr("w_in_even", [D, 6144])
    w_out0 = dr("w_out_even", [2048, D])
    lam_re_d = dr("s5_lam_re", [64, 64])
    lam_im_d = dr("s5_lam_im", [64, 64])
    log_dt_d = dr("s5_log_dt", [64])
    b_re_d = dr("s5_b_re", [64, 64, 16])
    b_im_d = dr("s5_b_im", [64, 64, 16])
    c_re_d = dr("s5_c_re", [64, 16, 64])
    c_im_d = dr("s5_c_im", [64, 16, 64])
    s5_d_d = dr("s5_d", [1024])
    w_glu_d = dr("s5_w_glu", [1024, 1024])
    b_glu_d = dr("s5_b_glu", [1024])
    gn_gain_d = dr("ret_gn_gain", [1024])
    rot_d = dr("c_rot", [NSEG, 2, 128, SEG])
    dt_d = dr("c_dt", [128, 4, 128])
    xizs_d = dr("c_xizs", [128, 8])
    zsc_d = dr("c_zsc", [128, 64])
    tabB = dr("tabB", [8, 16, 128, 512], BF16, kind="Internal")
    tabCL = dr("tabCL", [8, 16, 128, 512], BF16, kind="Internal")
    tabK = dr("tabK", [8, 128, 1024], BF16, kind="Internal")
    b_tabK = k.buf("tabK_d")
    b_ptq = [None, None]
    b_tabB = k.buf("tabB_d")
    b_tabC = k.buf("tabC_d")

    OFF_UT, OFF_X, OFF_XB, OFF_E, OFF_PW, OFF_TBC, OFF_SRET, OFF_SBF, OFF_QK = 0, 16384, 24576, 28672, 31744, 34816, 38912, 43008, 45056
    uT = carve_at(OFF_UT, [128, 8, SEG], BF16)
    b_uT = k.buf("uT")
    Xre = carve_at(OFF_X, [128, SEG], F32)
    Xim = carve_at(OFF_X + 4096, [128, SEG], F32)
    b_X = k.buf("X")
    Xbre = carve_at(OFF_XB, [128, SEG], BF16)
    Xbim = carve_at(OFF_XB + 2048, [128, SEG], BF16)
    b_Xb = k.buf("Xb")
    EA = [carve_at(OFF_E + 768 * j, [128, 384], F32) for j in range(2)]
    EB = [carve_at(OFF_E + 768 * (2 + j), [128, 384], F32) for j in range(2)]
    b_E = k.buf("E")
    pw = carve_at(OFF_PW, [128, 32, 16, 3], F32)
    b_pw = k.buf("pw")
    tB = [carve_at(OFF_TBC + 1024 * j, [128, 2, 512], BF16) for j in range(2)]
    b_tB = [k.buf("tB%d" % j) for j in range(2)]
    tC = [carve_at(OFF_TBC + 2048 + 1024 * j, [128, 2, 512], BF16) for j in range(2)]
    b_tC = [k.buf("tC%d" % j) for j in range(2)]
    Sret = carve_at(OFF_SRET, [128, 4, 512], F32)
    b_Sret = k.buf("Sret")
    Sbf = carve_at(OFF_SBF, [128, 4, 512], BF16)
    b_Sbf = k.buf("Sbf")
    qrT = carve_at(OFF_QK, [128, 2, SEG], BF16)
    krT = carve_at(OFF_QK + 4096, [128, 2, SEG], BF16)
    b_qrT = k.buf("qrT")
    b_krT = k.buf("krT")
    Ss5 = k.sb("Ss5", [128, 32, 2], F32)
    b_Ss5 = k.buf("Ss5")
    dcol = k.sb("dcol", [128, 16], F32)
    b_dcol = k.buf("dcol")

    def gelu_sb(src, b_src, dst, b_dst, scr, b_scr):
        R.op(ACT, lambda: nc.scalar.activation(out=scr, in_=src, func=AF.Square), reads=[b_src], writes=[b_scr])
        R.op(DVE, lambda: nc.vector.tensor_scalar(out=scr, in0=scr, scalar1=0.044715 * 1.5957691216, scalar2=1.5957691216,
                                                  op0=ALU.mult, op1=ALU.add), reads=[b_scr], writes=[b_scr])
        R.op(DVE, lambda: nc.vector.tensor_tensor(out=scr, in0=scr, in1=src, op=ALU.mult), reads=[b_scr, b_src], writes=[b_scr])
        R.op(ACT, lambda: nc.scalar.activation(out=scr, in_=scr, func=AF.Sigmoid), reads=[b_scr], writes=[b_scr])
        R.op(DVE, lambda: nc.vector.tensor_tensor(out=dst, in0=scr, in1=src, op=ALU.mult), reads=[b_scr, b_src], writes=[b_dst])

    def s5_precompute():
        R.barrier()
        bp = k.buf("pre")
        pos = [OFF_UT]

        def tmp(shape, dt=F32):
            n = 1
            for d_ in shape[1:]:
                n *= d_
            n16 = n * (2 if dt == F32 else 1)
            v = carve_at(pos[0], shape, dt)
            pos[0] += n16
            assert pos[0] <= OFF_PW, pos[0]
            return v

        def VT(out, a, b, op):
            R.op(DVE, lambda: nc.vector.tensor_tensor(out=out, in0=a, in1=b, op=op), reads=[bp], writes=[bp])

        def VS(out, a, s1, op0, s2=None, op1=None):
            if op1 is None:
                R.op(DVE, lambda: nc.vector.tensor_scalar(out=out, in0=a, scalar1=s1, scalar2=None, op0=op0), reads=[bp], writes=[bp])
            else:
                R.op(DVE, lambda: nc.vector.tensor_scalar(out=out, in0=a, scalar1=s1, scalar2=s2, op0=op0, op1=op1), reads=[bp], writes=[bp])

        def AC(out, a, func):
            R.op(ACT, lambda: nc.scalar.activation(out=out, in_=a, func=func), reads=[bp], writes=[bp])

        def LD(out, src):
            R.dma(lambda: nc.sync.dma_start(out=out, in_=src, allow_slow_non_contiguous=True), bp, writes=[bp])

        S = [128, 32]
        lr, li, ldt, dtv, x1, mag, ang, r, sn, cs, are, aim, den, nre, zre, zim, ta, tb = [tmp(S) for _ in range(18)]
        LD(lr, lam_re_d.rearrange("(i gg) p -> (gg p) i", gg=2))
        LD(li, lam_im_d.rearrange("(i gg) p -> (gg p) i", gg=2))
        for gg in range(2):
            LD(ldt[gg * 64:(gg + 1) * 64, :], log_dt_d.rearrange("(i gg) -> gg i", gg=2)[gg].partition_broadcast(64))
        VS(lr, lr, -1e-4, ALU.min)
        AC(dtv, ldt, AF.Exp)
        VT(x1, lr, dtv, ALU.mult)
        AC(mag, x1, AF.Exp)
        VT(ang, li, dtv, ALU.mult)
        MAGIC = 12582912.0

        def reduce_sin(dst, a_in):
            VS(r, a_in, 1.0 / (2 * math.pi), ALU.mult)
            VS(ta, r, MAGIC, ALU.add)
            VS(ta, ta, -MAGIC, ALU.add)
            VS(tb, ta, -2 * math.pi, ALU.mult)
            VT(r, a_in, tb, ALU.add)
            VS(r, r, 3.14159, ALU.min, -3.14159, ALU.max)
            AC(dst, r, AF.Sin)

        reduce_sin(sn, ang)
        VS(x1, ang, 0.5 * math.pi, ALU.add)
        reduce_sin(cs, x1)
        VT(are, mag, cs, ALU.mult)
        VT(aim, mag, sn, ALU.mult)
        VT(den, lr, lr, ALU.mult)
        VT(ta, li, li, ALU.mult)
        VT(den, den, ta, ALU.add)
        R.op(DVE, lambda: nc.vector.reciprocal(out=den, in_=den), reads=[bp], writes=[bp])
        VS(nre, are, -1.0, ALU.add)
        VT(ta, nre, lr, ALU.mult)
        VT(tb, aim, li, ALU.mult)
        VT(ta, ta, tb, ALU.add)
        VT(zre, ta, den, ALU.mult)
        VT(ta, aim, lr, ALU.mult)
        VT(tb, nre, li, ALU.mult)
        VT(ta, ta, tb, ALU.subtract)
        VT(zim, ta, den, ALU.mult)
        def P(kk, j):
            return pw[:, :, kk, j]

        def setp(kk, re_ap, im_ap):
            R.op(DVE, lambda: nc.vector.tensor_copy(out=P(kk, 0), in_=re_ap), reads=[bp], writes=[bp, b_pw])
            R.op(DVE, lambda: nc.vector.tensor_copy(out=P(kk, 1), in_=im_ap), reads=[bp], writes=[bp, b_pw])
            R.op(DVE, lambda: nc.vector.tensor_scalar(out=P(kk, 2), in0=im_ap, scalar1=-1.0, scalar2=None, op0=ALU.mult), reads=[bp], writes=[bp, b_pw])

        def cmul(ore, oim, a_re, a_im, b_re_, b_im_):
            VT(ta, a_re, b_re_, ALU.mult)
            VT(tb, a_im, b_im_, ALU.mult)
            VT(ore, ta, tb, ALU.subtract)
            VT(ta, a_re, b_im_, ALU.mult)
            VT(tb, a_im, b_re_, ALU.mult)
            VT(oim, ta, tb, ALU.add)

        cr, ci, nr, ni = [tmp(S) for _ in range(4)]
        setp(0, are, aim)
        R.op(DVE, lambda: nc.vector.tensor_copy(out=cr, in_=are), reads=[bp], writes=[bp])
        R.op(DVE, lambda: nc.vector.tensor_copy(out=ci, in_=aim), reads=[bp], writes=[bp])
        for kk in range(1, 8):
            cmul(nr, ni, cr, ci, are, aim)
            R.op(DVE, lambda: nc.vector.tensor_copy(out=cr, in_=nr), reads=[bp], writes=[bp])
            R.op(DVE, lambda: nc.vector.tensor_copy(out=ci, in_=ni), reads=[bp], writes=[bp])
            setp(kk, cr, ci)
        setp(8, cr, ci)
        for m in range(1, 8):
            cmul(nr, ni, cr, ci, cr, ci)
            R.op(DVE, lambda: nc.vector.tensor_copy(out=cr, in_=nr), reads=[bp], writes=[bp])
            R.op(DVE, lambda: nc.vector.tensor_copy(out=ci, in_=ni), reads=[bp], writes=[bp])
            setp(8 + m, cr, ci)
        S3 = [128, 32, 16]
        bre, bim, Bre, Bim, t3a, t3b = [tmp(S3) for _ in range(6)]
        LD(bre, b_re_d.rearrange("(i gg) p h -> (gg p) i h", gg=2))
        LD(bim, b_im_d.rearrange("(i gg) p h -> (gg p) i h", gg=2))
        zre3 = zre.unsqueeze(2).to_broadcast(S3)
        zim3 = zim.unsqueeze(2).to_broadcast(S3)
        VT(t3a, bre, zre3, ALU.mult)
        VT(t3b, bim, zim3, ALU.mult)
        VT(Bre, t3a, t3b, ALU.subtract)
        VT(t3a, bim, zre3, ALU.mult)
        VT(t3b, bre, zim3, ALU.mult)
        VT(Bim, t3a, t3b, ALU.add)
        Blre, Blim = tmp(S3), tmp(S3)
        Wb = [tmp([128, 32, 128], BF16) for _ in range(2)]
        b_wp = [k.buf("wpad%d" % j) for j in range(2)]
        stages = [tmp([128, 4, 128], BF16) for _ in range(2)]
        b_stage = [k.buf("stg%d" % j) for j in range(2)]
        Cpad0 = tmp([128, 8, 2, 512], BF16)
        b_c0 = k.buf("cpad0")
        cin = [tmp([128, 128]) for _ in range(2)]
        b_cin = [k.buf("cin%d" % j) for j in range(2)]
        b_trs = k.buf("trs")
        Kst = [tmp([128, 128], BF16) for _ in range(2)]
        ptmp = tmp([128, 16])
        b_kst = [k.buf("kst%d" % j) for j in range(2)]
        for j in range(2):
            R.op(DVE, lambda j=j: nc.vector.memset(Wb[j], 0.0), reads=[], writes=[b_wp[j]])
        R.op(DVE, lambda: nc.vector.memset(Cpad0, 0.0), reads=[], writes=[b_c0])
        TrsAll = [tmp([128, 8, 128], BF16) for _ in range(2)]
        for t in range(8):
            for ri, cd in enumerate((c_re_d, c_im_d)):
                src = cd.rearrange("(t gl) h p -> t (gl h) p", gl=8)[t]
                R.dma(lambda ri=ri, src=src: nc.sync.dma_start(out=cin[ri][:, 0:64], in_=src), b_cin[ri], writes=[b_cin[ri]])
                R.dma(lambda ri=ri, src=src: nc.sync.dma_start(out=cin[ri][:, 64:128], in_=src), b_cin[ri], writes=[b_cin[ri]])
                ptc = pacc[ri][:, 0:128]
                R.op(PE, lambda ri=ri, ptc=ptc: nc.tensor.transpose(out=ptc, in_=cin[ri], identity=ident_f), reads=[b_cin[ri], b_ident], writes=[b_pacc[ri]])
                R.op(ACT, lambda ri=ri, ptc=ptc, t=t: nc.scalar.copy(out=TrsAll[ri][:, t, :], in_=ptc), reads=[b_pacc[ri]], writes=[b_trs])
        for kk in range(4):
            for gg in range(2):
                rs = slice(gg * 64, (gg + 1) * 64)
                c0 = 32 * kk + 16 * gg
                R.op(DVE, lambda kk=kk, rs=rs, c0=c0: nc.vector.tensor_copy(out=Cpad0[rs, :, 0, kk * 128 + c0:kk * 128 + c0 + 16], in_=TrsAll[0][rs, :, c0:c0 + 16]), reads=[b_trs], writes=[b_c0])
                R.op(DVE, lambda kk=kk, rs=rs, c0=c0: nc.vector.tensor_scalar(out=Cpad0[rs, :, 1, kk * 128 + c0:kk * 128 + c0 + 16], in0=TrsAll[1][rs, :, c0:c0 + 16], scalar1=-1.0, scalar2=None, op0=ALU.mult),
                     reads=[b_trs], writes=[b_c0])
        nst = [0]
        for lag in range(8):
            if lag == 0:
                srcs = (Bre, Bim)
            else:
                ar3 = pw[:, :, lag - 1, 0].unsqueeze(2).to_broadcast(S3)
                ai3 = pw[:, :, lag - 1, 1].unsqueeze(2).to_broadcast(S3)
                VT(t3a, Bre, ar3, ALU.mult)
                VT(t3b, Bim, ai3, ALU.mult)
                VT(Blre, t3a, t3b, ALU.subtract)
                VT(t3a, Bim, ar3, ALU.mult)
                VT(t3b, Bre, ai3, ALU.mult)
                VT(Blim, t3a, t3b, ALU.add)
                srcs = (Blre, Blim)
            for ri, Bt in enumerate(srcs):
                for kk in range(4):
                    R.op(DVE, lambda Bt=Bt, kk=kk, ri=ri: nc.vector.tensor_copy(out=Wb[ri][0:64, kk::4, 32 * kk:32 * kk + 16], in_=Bt[0:64, kk::4, :]), reads=[bp, b_wp[ri]], writes=[b_wp[ri]])
                    R.op(DVE, lambda Bt=Bt, kk=kk, ri=ri: nc.vector.tensor_copy(out=Wb[ri][64:128, kk::4, 32 * kk + 16:32 * kk + 32], in_=Bt[64:128, kk::4, :]), reads=[bp, b_wp[ri]], writes=[b_wp[ri]])
                for t in range(8):
                    j = nst[0] % 2
                    nst[0] += 1
                    ptr = (pT[0] if j == 0 else pT2)[:, 0:4, :]
                    for kk in range(4):
                        R.op(PE, lambda kk=kk, t=t, ptr=ptr, ri=ri: nc.tensor.transpose(out=ptr[:, kk, :], in_=Wb[ri][:, 4 * t + kk, :], identity=ident_b), reads=[b_wp[ri], b_ident], writes=[b_ptq_fix[j]])
                    R.op(ACT, lambda j=j, ptr=ptr: nc.scalar.copy(out=stages[j], in_=ptr), reads=[b_ptq_fix[j]], writes=[b_stage[j]])
                    R.dma(lambda t=t, ri=ri, lag=lag, j=j: nc.sync.dma_start(out=tabB[t, 2 * lag + ri], in_=stages[j].rearrange("p a b -> p (a b)")), b_stage[j], reads=[b_stage[j]], writes=[b_tabB])
            for t in range(8 if not SKIP.get('K') else 0):
                j = t % 2
                pk = pacc[j][:, 0:128]
                for kk in range(4):
                    for ri in range(2):
                        R.op(PE, lambda pk=pk, kk=kk, ri=ri, t=t: nc.tensor.matmul(pk, lhsT=Wb[ri][:, 4 * t + kk, :], rhs=Cpad0[:, t, ri, kk * 128:(kk + 1) * 128],
                                                                                   start=(kk == 0 and ri == 0), stop=(kk == 3 and ri == 1)), reads=[b_wp[ri], b_c0], writes=[b_pacc[j]])
                R.op(ACT, lambda j=j, pk=pk: nc.scalar.copy(out=Kst[j], in_=pk), reads=[b_pacc[j]], writes=[b_kst[j]])
                R.dma(lambda t=t, lag=lag, j=j: nc.sync.dma_start(out=tabK[t, :, lag * 128:(lag + 1) * 128], in_=Kst[j]), b_kst[j], reads=[b_kst[j]], writes=[b_tabK])
        CpadAll = [Wb[ri].rearrange("p a b -> p (a b)").rearrange("p (t n) -> p t n", n=512) for ri in range(2)]
        tc1, tc2 = tmp([128, 8, 16]), tmp([128, 8, 16])
        for ri in range(2):
            R.op(DVE, lambda ri=ri: nc.vector.memset(Wb[ri], 0.0), reads=[b_wp[ri]], writes=[b_wp[ri]])
        for s_ in range(8):
            for kk in range(4):
                for gg in range(2):
                    rs = slice(gg * 64, (gg + 1) * 64)
                    c0 = 32 * kk + 16 * gg
                    S8 = [64, 8, 16]
                    Ar = pw[rs, kk::4, s_, 0].unsqueeze(2).to_broadcast(S8)
                    Ai = pw[rs, kk::4, s_, 1].unsqueeze(2).to_broadcast(S8)
                    Tr_, Ti_ = TrsAll[0][rs, :, c0:c0 + 16], TrsAll[1][rs, :, c0:c0 + 16]
                    o_re = CpadAll[0][rs, :, kk * 128 + c0:kk * 128 + c0 + 16]
                    o_im = CpadAll[1][rs, :, kk * 128 + c0:kk * 128 + c0 + 16]
                    a_, b_ = tc1[rs], tc2[rs]
                    rd = [b_trs, b_pw, bp]
                    R.op(DVE, lambda a_=a_, Tr_=Tr_, Ar=Ar: nc.vector.tensor_tensor(out=a_, in0=Tr_, in1=Ar, op=ALU.mult), reads=rd, writes=[bp])
                    R.op(DVE, lambda b_=b_, Ti_=Ti_, Ai=Ai: nc.vector.tensor_tensor(out=b_, in0=Ti_, in1=Ai, op=ALU.mult), reads=rd, writes=[bp])
                    R.op(DVE, lambda o_re=o_re, a_=a_, b_=b_: nc.vector.tensor_tensor(out=o_re, in0=a_, in1=b_, op=ALU.subtract), reads=[bp, b_wp[0]], writes=[b_wp[0]])
                    R.op(DVE, lambda a_=a_, Tr_=Tr_, Ai=Ai: nc.vector.tensor_tensor(out=a_, in0=Tr_, in1=Ai, op=ALU.mult), reads=rd, writes=[bp])
                    R.op(DVE, lambda b_=b_, Ti_=Ti_, Ar=Ar: nc.vector.tensor_tensor(out=b_, in0=Ti_, in1=Ar, op=ALU.mult), reads=rd, writes=[bp])
                    R.op(DVE, lambda o_im=o_im, a_=a_, b_=b_: nc.vector.tensor_tensor(out=o_im, in0=a_, in1=b_, op=ALU.add), reads=[bp, b_wp[1]], writes=[b_wp[1]])
            for ri in range(2):
                R.dma(lambda s_=s_, ri=ri: nc.sync.dma_start(out=tabCL[:, 2 * s_ + ri].rearrange("t p n -> p t n"), in_=CpadAll[ri]), b_wp[ri], reads=[b_wp[ri]], writes=[b_tabC])
        R.dma(lambda: nc.sync.dma_start(out=dcol[:, 0:8], in_=s5_d_d.rearrange("(t p) -> p t", p=128), allow_slow_non_contiguous=True), b_dcol, writes=[b_dcol])
        R.dma(lambda: nc.sync.dma_start(out=dcol[:, 8:16], in_=b_glu_d.rearrange("(t p) -> p t", p=128), allow_slow_non_contiguous=True), b_dcol, writes=[b_dcol])
        R.barrier()

    def stt(eng_id, out, in0, scalar, in1, reads, writes):
        e = nc.vector if eng_id == DVE else nc.gpsimd
        R.op(eng_id, lambda: e.scalar_tensor_tensor(out=out, in0=in0, scalar=scalar, in1=in1, op0=ALU.mult, op1=ALU.add), reads=reads, writes=writes)

    tBL = [yT.rearrange("p a b -> p (a b)")[:, j * 8192:(j + 1) * 8192].rearrange("p (l n) -> p l n", n=512) for j in range(2)]
    b_tBL = [k.buf("tBL%d" % j) for j in range(2)]
    tCL = carve_at(OFF_X, [128, 16, 512], BF16)
    b_tCL = k.buf("tCL")
    tK = [carve_at(OFF_TBC + 1024 * j, [128, 8, 128], BF16) for j in range(2)]
    b_tK = [k.buf("tK%d" % j) for j in range(2)]
    carry_b = [carve_at(OFF_XB + 6144, [128, 4, 256], BF16), carve_at(OFF_TBC + 2048, [128, 4, 256], BF16)]
    b_carry = k.buf("carry")
    Epre = [[carve_at(base + 2048 * j, [128, 4, 256], F32) for j in range(4)] for base in (OFF_X, OFF_QK)]
    b_Epre = [k.buf("Epre%d" % j) for j in range(2)]
    Tpre = carve_at(OFF_XB, [128, 4, 128], F32)
    b_Tpre = k.buf("Tpre")
    HA = [carve_at(OFF_QK + 3072 * j, [128, 4, 384], F32) for j in range(2)]
    HB = [carve_at(OFF_XB + 3072 * j, [128, 4, 384], F32) for j in range(2)]
    Ths = carve_at(OFF_QK + 6144, [128, 4, 256], F32)
    b_H = k.buf("H")
    pgv = pg.rearrange("p a b -> p (a b)").rearrange("p (s j) -> p s j", j=256)

    def VTT(out, a, b, op, reads, writes):
        R.op(DVE, lambda: nc.vector.tensor_tensor(out=out, in0=a, in1=b, op=op), reads=reads, writes=writes)

    def pwb(ct, idx, comp, w):
        return pw[:, 4 * ct:4 * ct + 4, idx, comp].unsqueeze(2).to_broadcast([128, 4, w])

    def inject(ct, e_re, e_im, tmp4, reads, writes):
        sre, sim = Ss5[:, 4 * ct:4 * ct + 4, 0], Ss5[:, 4 * ct:4 * ct + 4, 1]
        p8r, p8i = pw[:, 4 * ct:4 * ct + 4, 7, 0], pw[:, 4 * ct:4 * ct + 4, 7, 1]
        rd = reads + [b_pw, b_Ss5]
        VTT(tmp4, sre, p8r, ALU.mult, rd, writes)
        VTT(e_re, e_re, tmp4, ALU.add, rd, writes)
        VTT(tmp4, sim, p8i, ALU.mult, rd, writes)
        VTT(e_re, e_re, tmp4, ALU.subtract, rd, writes)
        VTT(tmp4, sim, p8r, ALU.mult, rd, writes)
        VTT(e_im, e_im, tmp4, ALU.add, rd, writes)
        VTT(tmp4, sre, p8i, ALU.mult, rd, writes)
        VTT(e_im, e_im, tmp4, ALU.add, rd, writes)

    uS3 = t1.bitcast(BF16).rearrange("p (s j) -> p s j", j=256)

    def deinterleave(ct):
        uv_ = uT[:, ct, :].rearrange("p (j s) -> p s j", s=8)
        R.op(POOL, lambda: nc.gpsimd.tensor_copy(out=uS3, in_=uv_), reads=[b_uT], writes=[b_t1])

    def e_matmuls(ct, sl, dst_re, dst_im, bdst, col0):
        uv = uS3
        for kk in range(4):
            for ri, dst in enumerate((dst_re, dst_im)):
                pe_ = pacc[ri][:, 0:256]
                for s_ in range(8):
                    lag = 7 - s_
                    R.op(PE, lambda pe_=pe_, lag=lag, ri=ri, kk=kk, s_=s_: nc.tensor.matmul(pe_, lhsT=tBL[sl][:, 2 * lag + ri, kk * 128:(kk + 1) * 128], rhs=uv[:, s_, :],
                                                                                            start=(s_ == 0), stop=(s_ == 7)), reads=[b_tBL[sl], b_t1], writes=[b_pacc[ri]])
                R.op(ACT, lambda pe_=pe_, dst=dst, kk=kk: nc.scalar.copy(out=dst[:, kk, col0:col0 + 256], in_=pe_), reads=[b_pacc[ri]], writes=[bdst])

    def s5_prefix_tile(ct):
        sl = ct % 2
        R.dma(lambda: nc.sync.dma_start(out=tBL[sl], in_=tabB[ct].rearrange("r p n -> p r n")), b_tBL[sl], reads=[b_tabB], writes=[b_tBL[sl]])
        E = Epre[sl]
        bE = b_Epre[sl]
        deinterleave(ct)
        e_matmuls(ct, sl, E[0], E[1], bE, 0)
        inject(ct, E[0][:, :, 0], E[1][:, :, 0], Tpre[:, :, 0], [bE, b_Tpre], [bE, b_Tpre])
        rw = [bE, b_pw, b_Tpre]
        for m in range(8):
            w = 128 >> m
            src = (E[0], E[1]) if m % 2 == 0 else (E[2], E[3])
            dst = (E[2], E[3]) if m % 2 == 0 else (E[0], E[1])
            ev = [x_[:, :, 0:2 * w].rearrange("p a (k two) -> p a k two", two=2)[:, :, :, 0] for x_ in src]
            od = [x_[:, :, 0:2 * w].rearrange("p a (k two) -> p a k two", two=2)[:, :, :, 1] for x_ in src]
            dr_, di_ = dst[0][:, :, 0:w], dst[1][:, :, 0:w]
            t_ = Tpre[:, :, 0:w]
            Pr, Pi = pwb(ct, 8 + m, 0, w), pwb(ct, 8 + m, 1, w)
            VTT(t_, ev[0], Pr, ALU.mult, rw, [b_Tpre])
            VTT(dr_, t_, od[0], ALU.add, rw, [bE])
            VTT(t_, ev[1], Pi, ALU.mult, rw, [b_Tpre])
            VTT(dr_, dr_, t_, ALU.subtract, rw, [bE])
            VTT(t_, ev[1], Pr, ALU.mult, rw, [b_Tpre])
            VTT(di_, t_, od[1], ALU.add, rw, [bE])
            VTT(t_, ev[0], Pi, ALU.mult, rw, [b_Tpre])
            VTT(di_, di_, t_, ALU.add, rw, [bE])
        R.op(DVE, lambda: nc.vector.tensor_copy(out=Ss5[:, 4 * ct:4 * ct + 4, 0], in_=E[0][:, :, 0]), reads=[bE], writes=[b_Ss5])
        R.op(DVE, lambda: nc.vector.tensor_copy(out=Ss5[:, 4 * ct:4 * ct + 4, 1], in_=E[1][:, :, 0]), reads=[bE], writes=[b_Ss5])

    def s5_own_tile(ct):
        sl = ct % 2
        R.dma(lambda: nc.sync.dma_start(out=tBL[sl], in_=tabB[ct].rearrange("r p n -> p r n")), b_tBL[sl], reads=[b_tabB], writes=[b_tBL[sl]])
        R.dma(lambda: nc.sync.dma_start(out=tCL, in_=tabCL[ct].rearrange("r p n -> p r n")), b_tCL, reads=[b_tabC], writes=[b_tCL])
        R.dma(lambda: nc.sync.dma_start(out=tK[sl], in_=tabK[ct].rearrange("p (l n) -> p l n", n=128)), b_tK[sl], reads=[b_tabK], writes=[b_tK[sl]])
        uv = uT[:, ct, :].rearrange("p (j s) -> p s j", s=8)
        deinterleave(ct)
        for s_ in range(8):
            for lag in range(s_ + 1):
                R.op(PE, lambda s_=s_, lag=lag: nc.tensor.matmul(pgv[:, s_, :], lhsT=tK[sl][:, lag, :], rhs=uS3[:, s_ - lag, :], start=(lag == 0 and s_ % 2 == 0), stop=False),
                     reads=[b_tK[sl], b_t1], writes=[b_pg[s_ // 2]])
        e_matmuls(ct, sl, HA[0], HA[1], b_H, 128)
        inject(ct, HA[0][:, :, 128], HA[1][:, :, 128], Ths[:, :, 0], [b_H], [b_H])
        rw = [b_H, b_pw]

        def cmac(dre, dim_, sre_, sim_, m, w):
            Pr, Pi = pwb(ct, 8 + m, 0, w), pwb(ct, 8 + m, 1, w)
            t_ = Ths[:, :, 0:w]
            VTT(t_, sre_, Pr, ALU.mult, rw, [b_H])
            VTT(dre, dre, t_, ALU.add, rw, [b_H])
            VTT(t_, sim_, Pi, ALU.mult, rw, [b_H])
            VTT(dre, dre, t_, ALU.subtract, rw, [b_H])
            VTT(t_, sim_, Pr, ALU.mult, rw, [b_H])
            VTT(dim_, dim_, t_, ALU.add, rw, [b_H])
            VTT(t_, sre_, Pi, ALU.mult, rw, [b_H])
            VTT(dim_, dim_, t_, ALU.add, rw, [b_H])

        def sview(buf_, first, cnt, step):
            return buf_[:, :, 128 + first:128 + first + (cnt - 1) * step + 1:step]

        for m in range(8):
            d_ = 1 << m
            n_ = 256 // (2 * d_)
            cmac(sview(HA[0], 2 * d_ - 1, n_, 2 * d_), sview(HA[1], 2 * d_ - 1, n_, 2 * d_), sview(HA[0], d_ - 1, n_, 2 * d_), sview(HA[1], d_ - 1, n_, 2 * d_), m, n_)
        for m in range(6, -1, -1):
            d_ = 1 << m
            n_ = 256 // (2 * d_) - 1
            cmac(sview(HA[0], 3 * d_ - 1, n_, 2 * d_), sview(HA[1], 3 * d_ - 1, n_, 2 * d_), sview(HA[0], 2 * d_ - 1, n_, 2 * d_), sview(HA[1], 2 * d_ - 1, n_, 2 * d_), m, n_)
        for ri in range(2):
            R.op(DVE, lambda ri=ri: nc.vector.tensor_copy(out=HA[ri][:, :, 127], in_=Ss5[:, 4 * ct:4 * ct + 4, ri]), reads=[b_Ss5, b_H], writes=[b_H])
        for ri in range(2):
            R.op(DVE, lambda ri=ri: nc.vector.tensor_copy(out=Ss5[:, 4 * ct:4 * ct + 4, ri], in_=HA[ri][:, :, 383]), reads=[b_H], writes=[b_Ss5])
        R.op(ACT, lambda: nc.scalar.copy(out=carry_b[0], in_=HA[0][:, :, 127:383]), reads=[b_H], writes=[b_carry])
        R.op(ACT, lambda: nc.scalar.activation(out=carry_b[1], in_=HA[1][:, :, 127:383], func=AF.Copy, scale=-1.0), reads=[b_H], writes=[b_carry])
        for ri in range(2):
            R.op(DVE, lambda ri=ri: nc.vector.memset(HA[ri][:, :, 127:128], 0.0), reads=[b_carry], writes=[b_H])
        for kk in range(4):
            for s_ in range(8):
                for ri in range(2):
                    R.op(PE, lambda kk=kk, s_=s_, ri=ri: nc.tensor.matmul(pgv[:, s_, :], lhsT=tCL[:, 2 * s_ + ri, kk * 128:(kk + 1) * 128], rhs=carry_b[ri][:, kk, :],
                                                                         start=False, stop=(kk == 3 and ri == 1 and s_ % 2 == 1)), reads=[b_tCL, b_carry], writes=[b_pg[s_ // 2]])
        for nb in range(4):
            ya, sc = t1[:, 0:512], t1[:, 512:1024]
            ya3 = ya.rearrange("p (s j) -> p s j", j=256)
            uvs = uv[:, 2 * nb:2 * nb + 2, :]
            R.op(DVE, lambda nb=nb, ya3=ya3, uvs=uvs: nc.vector.scalar_tensor_tensor(out=ya3, in0=uvs, scalar=dcol[:, ct:ct + 1], in1=pg[:, nb, :].rearrange("p (s j) -> p s j", j=256),
                                                                                    op0=ALU.mult, op1=ALU.add), reads=[b_uT, b_dcol, b_pg[nb]], writes=[b_t1])
            R.op(ACT, lambda uvs=uvs, ya3=ya3: nc.scalar.activation(out=uvs, in_=ya3, func=AF.Gelu_apprx_tanh), reads=[b_t1], writes=[b_uT])

    def s5_segment(own):
        for ct2 in range(4):
            wb, b_wb = load_wblock(w_in0, ct2 * 256)
            for hf in range(2):
                ct = 2 * ct2 + hf
                for nb in range(4):
                    pa, b_pa = next_pacc()
                    for kt in range(8):
                        R.op(PE, lambda pa=pa, wb=wb, kt=kt, nb=nb, hf=hf: nc.tensor.matmul(pa, lhsT=wb[:, kt, hf * 128:(hf + 1) * 128], rhs=xnT[:, kt, nb * 512:(nb + 1) * 512],
                                                                                            start=(kt == 0), stop=(kt == 7)), reads=[b_xnT, b_wb], writes=[b_pa])
                    R.op(ACT, lambda pa=pa, ct=ct, nb=nb: nc.scalar.copy(out=uT[:, ct, nb * 512:(nb + 1) * 512], in_=pa), reads=[b_pa], writes=[b_uT])
        if own:
            for buf_ in HA + HB:
                R.op(DVE, lambda buf_=buf_: nc.vector.memset(buf_[:, :, 0:128], 0.0), reads=[], writes=[b_H])
        for ct in range(8):
            if own:
                s5_own_tile(ct)
            else:
                s5_prefix_tile(ct)

    def glu():
        gsb = carve_at(OFF_X, [128, 512], F32)
        zsb = carve_at(OFF_X + 1024, [128, 512], F32)
        b_g = k.buf("gsb")
        for jt2 in range(4):
            wg, b_wg = load_wblock(w_glu_d, jt2 * 256, gain=False)
            wa, b_wa = load_wblock(w_in0, 1024 + jt2 * 256)
            for hf in range(2):
                jt = 2 * jt2 + hf
                for nb in range(4):
                    pa, b_pa = next_pacc()
                    for kt in range(8):
                        R.op(PE, lambda pa=pa, wg=wg, kt=kt, nb=nb, hf=hf: nc.tensor.matmul(pa, lhsT=wg[:, kt, hf * 128:(hf + 1) * 128], rhs=uT[:, kt, nb * 512:(nb + 1) * 512],
                                                                                            start=(kt == 0), stop=(kt == 7)), reads=[b_uT, b_wg], writes=[b_pa])
                    R.op(ACT, lambda pa=pa, jt=jt: nc.scalar.activation(out=gsb, in_=pa, func=AF.Sigmoid, bias=dcol[:, 8 + jt:9 + jt]), reads=[b_pa, b_dcol], writes=[b_g])
                    pz, b_pz = next_pacc()
                    for kt in range(8):
                        R.op(PE, lambda pz=pz, wa=wa, kt=kt, nb=nb, hf=hf: nc.tensor.matmul(pz, lhsT=wa[:, kt, hf * 128:(hf + 1) * 128], rhs=xnT[:, kt, nb * 512:(nb + 1) * 512],
                                                                                            start=(kt == 0), stop=(kt == 7)), reads=[b_xnT, b_wa], writes=[b_pz])
                    R.op(ACT, lambda pz=pz: nc.scalar.activation(out=zsb, in_=pz, func=AF.Silu), reads=[b_pz], writes=[b_g])
                    R.op(DVE, lambda: nc.vector.tensor_tensor(out=gsb, in0=gsb, in1=zsb, op=ALU.mult), reads=[b_g], writes=[b_g])
                    R.op(DVE, lambda jt=jt, nb=nb: nc.vector.tensor_tensor(out=yT[:, jt, nb * 512:(nb + 1) * 512], in0=uT[:, jt, nb * 512:(nb + 1) * 512], in1=gsb, op=ALU.mult),
                         reads=[b_g, b_uT], writes=[b_yT])

    G128 = [(1.0 - 2.0 ** (-5 - h)) ** 128 for h in range(4)]

    def ret_segment(seg, own):
        cosT = carve_at(OFF_X, [128, SEG], F32)
        sinT = carve_at(OFF_X + 4096, [128, SEG], F32)
        b_rot = k.buf("rot")
        R.dma(lambda: nc.sync.dma_start(out=cosT, in_=rot_d[seg, 0]), b_rot, writes=[b_rot])
        R.dma(lambda: nc.sync.dma_start(out=sinT, in_=rot_d[seg, 1]), b_rot, writes=[b_rot])
        DTt = carve_at(OFF_XB, [128, 4, 128], F32)
        xz = carve_at(OFF_XB + 1024, [128, 8], F32)
        gng = carve_at(OFF_XB + 1040, [128, 1024], F32)
        b_cst = k.buf("rcst")
        R.dma(lambda: nc.sync.dma_start(out=DTt, in_=dt_d), b_cst, writes=[b_cst])
        R.dma(lambda: nc.sync.dma_start(out=xz, in_=xizs_d), b_cst, writes=[b_cst])
        zsc = carve_at(OFF_XB + 1040 + 2048, [128, 64], F32)
        R.dma(lambda: nc.sync.dma_start(out=zsc, in_=zsc_d), b_cst, writes=[b_cst])
        R.dma(lambda: nc.sync.dma_start(out=gng, in_=gn_gain_d.partition_broadcast(128)), b_cst, writes=[b_cst])
        eo = [OFF_UT]

        def et(shape, dt):
            n = 1
            for d_ in shape[1:]:
                n *= d_
            n16 = n * (2 if dt == F32 else 1)
            v = carve_at(eo[0], shape, dt)
            eo[0] += n16
            assert eo[0] <= OFF_UT + 16384
            return v

        TT = []
        for pp in range(2):
            d_ = dict(PT=et([128, 128], BF16), o_sb=et([128, 256], F32), in_sb=et([128, 256], F32), ktok=et([128, 256], BF16), v_bf=et([128, 256], BF16),
                      vz_bf=et([128, 256], BF16), yb=et([128, 256], BF16), sbz=et([128, 256], F32), bstt=et([128, 6], F32), mvv=et([128, 4], F32))
            for nm in ("b_PT", "b_o", "b_in", "b_ktok", "b_v", "b_yb", "b_sbz", "b_bs"):
                d_[nm] = k.buf()
            d_["b_psc"], d_["b_pin"], d_["b_pcr"] = b_pg[0], b_pg[1], b_pg[2]
            tb_, btb_ = (pT[0], b_pT[0]) if pp == 0 else (pT2, b_pT2)
            d_["b_ptk"], d_["b_pty"] = btb_, btb_
            d_["ptk"] = tb_[:, 0:2, :]
            d_["pty"] = tb_[:, 2:4, :]
            d_["psc"] = pg[:, 0, 0:128]
            d_["pin"] = pg[:, 1, 0:256]
            d_["pcr"] = pg[:, 2, 0:256]
            if own:
                d_["b_pv"], d_["b_pz"] = b_pacc[0], b_pacc[1]
                d_["pv"] = pacc[0][:, 0:256]
                d_["pz"] = pacc[1][:, 0:256]
                d_["plc"], d_["b_plc"] = pg[:, 3, :], b_pg[3]
            else:
                d_["b_pv"], d_["b_pz"] = b_pacc[pp], None
                d_["pv"] = pacc[pp][:, 0:256]
                d_["pz"] = None
                d_["plc"], d_["b_plc"] = pg[:, 2 + pp, :], b_pg[2 + pp]
            TT.append(d_)
        hb = [[b_pacc[0]], [b_pacc[1]]]

        rbanks = [((pacc[0], b_pacc[0]), (pacc[1], b_pacc[1])), ((pg[:, 0, :], b_pg[0]), (pg[:, 1, :], b_pg[1])), ((pg[:, 2, :], b_pg[2]), (pg[:, 3, :], b_pg[3]))]
        rctr = [0]

        def rotary_proj(w, b_w, dstT, b_dst):
            for nb in range(4):
                blk = slice(nb * 512, (nb + 1) * 512)
                bk = rbanks[rctr[0] % 3]
                rctr[0] += 1
                for hf in range(2):
                    ps_, bps_ = bk[hf]
                    for kt in range(8):
                        R.op(PE, lambda hf=hf, kt=kt, blk=blk, ps_=ps_: nc.tensor.matmul(ps_, lhsT=w[:, kt, hf * 128:(hf + 1) * 128], rhs=xnT[:, kt, blk], start=(kt == 0), stop=(kt == 7)),
                             reads=[b_xnT, b_w], writes=[bps_])
                (p0, bp0), (p1, bp1) = bk
                ta, tb = t1[:, 0:512], t1[:, 512:1024]
                R.op(DVE, lambda blk=blk, p0=p0: nc.vector.tensor_tensor(out=ta, in0=p0, in1=cosT[:, blk], op=ALU.mult), reads=[bp0, b_rot], writes=[b_t1])
                R.op(DVE, lambda blk=blk, p1=p1: nc.vector.tensor_tensor(out=tb, in0=p1, in1=sinT[:, blk], op=ALU.mult), reads=[bp1, b_rot], writes=[b_t1])
                R.op(DVE, lambda blk=blk: nc.vector.tensor_tensor(out=dstT[:, 0, blk], in0=ta, in1=tb, op=ALU.subtract), reads=[b_t1], writes=[b_dst])
                R.op(DVE, lambda blk=blk, p0=p0: nc.vector.tensor_tensor(out=ta, in0=p0, in1=sinT[:, blk], op=ALU.mult), reads=[bp0, b_rot, b_dst], writes=[b_t1])
                R.op(DVE, lambda blk=blk, p1=p1: nc.vector.tensor_tensor(out=tb, in0=p1, in1=cosT[:, blk], op=ALU.mult), reads=[bp1, b_rot], writes=[b_t1])
                R.op(DVE, lambda blk=blk: nc.vector.tensor_tensor(out=dstT[:, 1, blk], in0=ta, in1=tb, op=ALU.add), reads=[b_t1], writes=[b_dst])

        for h in range(4):
            wk, b_wk = load_wblock(w_in0, 3072 + h * 256)
            rotary_proj(wk, b_wk, krT, b_krT)
            if own:
                wq, b_wq = load_wblock(w_in0, 2048 + h * 256)
                rotary_proj(wq, b_wq, qrT, b_qrT)
            wv, b_wv = load_wblock(w_in0, 4096 + h * 256)
            if own:
                wz, b_wz = load_wblock(w_in0, 5120 + h * 256)
            for c in range(NCH):
                ch = slice(c * 128, (c + 1) * 128)
                T = TT[c % 2]
                for kt in range(8):
                    R.op(PE, lambda kt=kt, ch=ch, T=T, wv=wv: nc.tensor.matmul(T["pv"], lhsT=xnT[:, kt, ch], rhs=wv[:, kt, :], start=(kt == 0), stop=(kt == 7)),
                         reads=[b_xnT, b_wv], writes=[T["b_pv"]])
                R.op(ACT, lambda T=T: nc.scalar.copy(out=T["v_bf"], in_=T["pv"]), reads=[T["b_pv"]], writes=[T["b_v"]])
                vsc = xz[:, 4 + h:5 + h] if own else zsc[:, 16 * h + c:16 * h + c + 1]
                R.op(ACT, lambda vsc=vsc, T=T: nc.scalar.activation(out=T["vz_bf"], in_=T["pv"], func=AF.Copy, scale=vsc), reads=[T["b_pv"], b_cst], writes=[T["b_v"]])
                for tl in range(2):
                    R.op(PE, lambda tl=tl, ch=ch, T=T: nc.tensor.transpose(out=T["ptk"][:, tl, :], in_=krT[:, tl, ch], identity=ident_b), reads=[b_krT, b_ident], writes=[T["b_ptk"]])
                R.op(DVE, lambda T=T: nc.vector.tensor_copy(out=T["ktok"].rearrange("p (a b) -> p a b", b=128), in_=T["ptk"]), reads=[T["b_ptk"]], writes=[T["b_ktok"]])
                if own:
                    for tl in range(2):
                        R.op(PE, lambda tl=tl, ch=ch, T=T: nc.tensor.matmul(T["psc"], lhsT=krT[:, tl, ch], rhs=qrT[:, tl, ch], start=(tl == 0), stop=(tl == 1)),
                             reads=[b_krT, b_qrT], writes=[T["b_psc"]])
                    R.op(DVE, lambda h=h, T=T: nc.vector.tensor_tensor(out=T["PT"], in0=T["psc"], in1=DTt[:, h, :], op=ALU.mult), reads=[T["b_psc"], b_cst], writes=[T["b_PT"]])
                    R.op(PE, lambda T=T: nc.tensor.matmul(T["pin"], lhsT=T["PT"], rhs=T["v_bf"], start=True, stop=True), reads=[T["b_PT"], T["b_v"]], writes=[T["b_pin"]])
                    for tl in range(2):
                        R.op(PE, lambda tl=tl, ch=ch, h=h, T=T: nc.tensor.matmul(T["pcr"], lhsT=qrT[:, tl, ch], rhs=Sbf[:, h, tl * 256:(tl + 1) * 256], start=(tl == 0), stop=(tl == 1)),
                             reads=[b_qrT, b_Sbf], writes=[T["b_pcr"]])
                    R.op(ACT, lambda T=T: nc.scalar.copy(out=T["in_sb"], in_=T["pin"]), reads=[T["b_pin"]], writes=[T["b_in"]])
                    R.op(DVE, lambda h=h, T=T: nc.vector.scalar_tensor_tensor(out=T["o_sb"], in0=T["pcr"], scalar=xz[:, h:h + 1], in1=T["in_sb"], op0=ALU.mult, op1=ALU.add),
                         reads=[T["b_pcr"], b_cst, T["b_in"]], writes=[T["b_o"]])
                    R.op(DVE, lambda T=T: nc.vector.bn_stats(out=T["bstt"], in_=T["o_sb"]), reads=[T["b_o"]], writes=[T["b_bs"]])
                    R.op(DVE, lambda T=T: nc.vector.bn_aggr(out=T["mvv"][:, 0:2], in_=T["bstt"]), reads=[T["b_bs"]], writes=[T["b_bs"]])
                    R.op(DVE, lambda T=T: nc.vector.tensor_scalar(out=T["mvv"][:, 3:4], in0=T["mvv"][:, 1:2], scalar1=EPS, scalar2=None, op0=ALU.add), reads=[T["b_bs"]], writes=[T["b_bs"]])
                    R.op(ACT, lambda T=T: nc.scalar.activation(out=T["mvv"][:, 3:4], in_=T["mvv"][:, 3:4], func=AF.Sqrt), reads=[T["b_bs"]], writes=[T["b_bs"]])
                    R.op(DVE, lambda T=T: nc.vector.reciprocal(out=T["mvv"][:, 2:3], in_=T["mvv"][:, 3:4]), reads=[T["b_bs"]], writes=[T["b_bs"]])
                    R.op(DVE, lambda T=T: nc.vector.tensor_scalar(out=T["o_sb"], in0=T["o_sb"], scalar1=T["mvv"][:, 0:1], scalar2=T["mvv"][:, 2:3], op0=ALU.subtract, op1=ALU.mult),
                         reads=[T["b_o"], T["b_bs"]], writes=[T["b_o"]])
                    for kt in range(8):
                        R.op(PE, lambda kt=kt, ch=ch, T=T, wz=wz: nc.tensor.matmul(T["pz"], lhsT=xnT[:, kt, ch], rhs=wz[:, kt, :], start=(kt == 0), stop=(kt == 7)),
                             reads=[b_xnT, b_wz], writes=[T["b_pz"]])
                    R.op(ACT, lambda T=T: nc.scalar.activation(out=T["sbz"], in_=T["pz"], func=AF.Silu), reads=[T["b_pz"]], writes=[T["b_sbz"]])
                    R.op(DVE, lambda h=h, T=T: nc.vector.tensor_tensor(out=T["o_sb"], in0=T["o_sb"], in1=gng[:, h * 256:(h + 1) * 256], op=ALU.mult), reads=[T["b_o"], b_cst], writes=[T["b_o"]])
                    R.op(DVE, lambda T=T: nc.vector.tensor_tensor(out=T["yb"], in0=T["o_sb"], in1=T["sbz"], op=ALU.mult), reads=[T["b_o"], T["b_sbz"]], writes=[T["b_yb"]])
                    for tl in range(2):
                        R.op(PE, lambda tl=tl, T=T: nc.tensor.transpose(out=T["pty"][:, tl, :], in_=T["yb"][:, tl * 128:(tl + 1) * 128], identity=ident_b), reads=[T["b_yb"], b_ident], writes=[T["b_pty"]])
                    R.op(DVE, lambda h=h, ch=ch, T=T: nc.vector.tensor_copy(out=yT[:, 2 * h:2 * h + 2, ch], in_=T["pty"]), reads=[T["b_pty"]], writes=[b_yT])
                if own:
                    plc, b_plc = T["plc"], T["b_plc"]
                    for tl in range(2):
                        R.op(PE, lambda tl=tl, T=T, plc=plc: nc.tensor.matmul(plc[:, tl * 256:(tl + 1) * 256], lhsT=T["ktok"][:, tl * 128:(tl + 1) * 128], rhs=T["vz_bf"], start=(tl == 0), stop=(tl == 1)),
                             reads=[T["b_ktok"], T["b_v"]], writes=[b_plc])
                    R.op(DVE, lambda h=h, plc=plc: nc.vector.scalar_tensor_tensor(out=Sret[:, h, :], in0=Sret[:, h, :], scalar=G128[h], in1=plc, op0=ALU.mult, op1=ALU.add),
                         reads=[b_Sret, b_plc], writes=[b_Sret])
                    R.op(ACT, lambda h=h: nc.scalar.copy(out=Sbf[:, h, :], in_=Sret[:, h, :]), reads=[b_Sret], writes=[b_Sbf])
                else:
                    plc, b_plc = pg[:, 3, :], b_pg[3]
                    for tl in range(2):
                        R.op(PE, lambda tl=tl, T=T, plc=plc, c=c: nc.tensor.matmul(plc[:, tl * 256:(tl + 1) * 256], lhsT=T["ktok"][:, tl * 128:(tl + 1) * 128], rhs=T["vz_bf"],
                                                                                  start=(c == 0 and tl == 0), stop=(c == NCH - 1 and tl == 1)),
                             reads=[T["b_ktok"], T["b_v"]], writes=[b_plc])
                    if c == NCH - 1:
                        R.op(DVE, lambda h=h, plc=plc: nc.vector.scalar_tensor_tensor(out=Sret[:, h, :], in0=Sret[:, h, :], scalar=G128[h] ** 16, in1=plc, op0=ALU.mult, op1=ALU.add),
                             reads=[b_Sret, b_plc], writes=[b_Sret])
                        R.op(ACT, lambda h=h: nc.scalar.copy(out=Sbf[:, h, :], in_=Sret[:, h, :]), reads=[b_Sret], writes=[b_Sbf])

    def prefix_segment(seg):
        for ct2 in range(4):
            wb, b_wb = load_wblock(w_in0, ct2 * 256)
            for hf in range(2):
                ct = 2 * ct2 + hf
                for nb in range(4):
                    pa, b_pa = next_pacc()
                    for kt in range(8):
                        R.op(PE, lambda pa=pa, wb=wb, kt=kt, nb=nb, hf=hf: nc.tensor.matmul(pa, lhsT=wb[:, kt, hf * 128:(hf + 1) * 128], rhs=xnT[:, kt, nb * 512:(nb + 1) * 512],
                                                                                            start=(kt == 0), stop=(kt == 7)), reads=[b_xnT, b_wb], writes=[b_pa])
                    R.op(ACT, lambda pa=pa, ct=ct, nb=nb: nc.scalar.copy(out=uT[:, ct, nb * 512:(nb + 1) * 512], in_=pa), reads=[b_pa], writes=[b_uT])
        yTf = yT.rearrange("p a b -> p (a b)")
        tbl = yTf[:, 0:8192].rearrange("p (l n) -> p l n", n=512)
        b_tbl = k.buf("ptbl")
        cosT = yTf[:, 8192:12288].bitcast(F32)
        sinT = yTf[:, 12288:16384].bitcast(F32)
        b_rot = k.buf("prot")
        R.dma(lambda: nc.sync.dma_start(out=cosT, in_=rot_d[seg, 0]), b_rot, writes=[b_rot])
        R.dma(lambda: nc.sync.dma_start(out=sinT, in_=rot_d[seg, 1]), b_rot, writes=[b_rot])
        zsc = carve_at(OFF_XB + 1040 + 2048, [128, 64], F32)
        b_cst = k.buf("pcst")
        R.dma(lambda: nc.sync.dma_start(out=zsc, in_=zsc_d), b_cst, writes=[b_cst])
        E = [carve_at(OFF_X + 2048 * j, [128, 4, 256], F32) for j in range(4)]
        bE = k.buf("pE")
        TTp = []
        for pp in range(2):
            base = OFF_E + pp * 768
            d_ = dict(ktok=carve_at(base, [128, 256], BF16), v_bf=carve_at(base + 256, [128, 256], BF16), vz_bf=carve_at(base + 512, [128, 256], BF16),
                      b_ktok=k.buf(), b_v=k.buf())
            d_["pv"], d_["b_pv"] = pacc[pp][:, 0:256], b_pacc[pp]
            tb_, btb_ = (pT[0], b_pT[0]) if pp == 0 else (pT2, b_pT2)
            d_["ptk"], d_["b_ptk"] = tb_[:, 0:2, :], btb_
            TTp.append(d_)
        plc, b_plc = pg[:, 3, :], b_pg[3]
        pE = [pg[:, 0, 0:256], pg[:, 1, 0:256]]
        bpE = [b_pg[0], b_pg[1]]
        rb = [((pacc[0], b_pacc[0]), (pacc[1], b_pacc[1])), ((pg[:, 2, :], b_pg[2]), (pg[:, 3, :], b_pg[3]))]
        rc = [0]

        def rotary_k(w, b_w):
            for nb in range(4):
                blk = slice(nb * 512, (nb + 1) * 512)
                bk = rb[rc[0] % 2]
                rc[0] += 1
                for hf in range(2):
                    ps_, bps_ = bk[hf]
                    for kt in range(8):
                        R.op(PE, lambda hf=hf, kt=kt, blk=blk, ps_=ps_: nc.tensor.matmul(ps_, lhsT=w[:, kt, hf * 128:(hf + 1) * 128], rhs=xnT[:, kt, blk], start=(kt == 0), stop=(kt == 7)),
                             reads=[b_xnT, b_w], writes=[bps_])
                (p0, bp0), (p1, bp1) = bk
                ta, tb = Tpre.rearrange("p a b -> p (a b)"), carve_at(OFF_XB + 3216, [128, 512], F32) if False else t1[:, 512:1024]
                ta = t1[:, 0:512]
                R.op(DVE, lambda blk=blk, p0=p0: nc.vector.tensor_tensor(out=ta, in0=p0, in1=cosT[:, blk], op=ALU.mult), reads=[bp0, b_rot], writes=[b_t1])
                R.op(DVE, lambda blk=blk, p1=p1: nc.vector.tensor_tensor(out=tb, in0=p1, in1=sinT[:, blk], op=ALU.mult), reads=[bp1, b_rot], writes=[b_t1])
                R.op(DVE, lambda blk=blk: nc.vector.tensor_tensor(out=krT[:, 0, blk], in0=ta, in1=tb, op=ALU.subtract), reads=[b_t1], writes=[b_krT])
                R.op(DVE, lambda blk=blk, p0=p0: nc.vector.tensor_tensor(out=ta, in0=p0, in1=sinT[:, blk], op=ALU.mult), reads=[bp0, b_rot, b_krT], writes=[b_t1])
                R.op(DVE, lambda blk=blk, p1=p1: nc.vector.tensor_tensor(out=tb, in0=p1, in1=cosT[:, blk], op=ALU.mult), reads=[bp1, b_rot], writes=[b_t1])
                R.op(DVE, lambda blk=blk: nc.vector.tensor_tensor(out=krT[:, 1, blk], in0=ta, in1=tb, op=ALU.add), reads=[b_t1], writes=[b_krT])

        def tile_steps(ct):
            R.dma(lambda: nc.sync.dma_start(out=tbl, in_=tabB[ct].rearrange("r p n -> p r n")), b_tbl, reads=[b_tabB], writes=[b_tbl])
            deinterleave(ct)
            for kk in range(4):
                for ri in range(2):
                    for s_ in range(8):
                        lag = 7 - s_
                        R.op(PE, lambda lag=lag, ri=ri, kk=kk, s_=s_: nc.tensor.matmul(pE[ri], lhsT=tbl[:, 2 * lag + ri, kk * 128:(kk + 1) * 128], rhs=uS3[:, s_, :],
                                                                                       start=(s_ == 0), stop=(s_ == 7)), reads=[b_tbl, b_t1], writes=[bpE[ri]])
                    R.op(ACT, lambda ri=ri, kk=kk: nc.scalar.copy(out=E[ri][:, kk, :], in_=pE[ri]), reads=[bpE[ri]], writes=[bE])
                if kk % 2 == 1:
                    yield
            inject(ct, E[0][:, :, 0], E[1][:, :, 0], Tpre[:, :, 0], [bE, b_Tpre], [bE, b_Tpre])
            yield
            rw = [bE, b_pw, b_Tpre]
            for m in range(8):
                w = 128 >> m
                src = (E[0], E[1]) if m % 2 == 0 else (E[2], E[3])
                dst = (E[2], E[3]) if m % 2 == 0 else (E[0], E[1])
                ev = [x_[:, :, 0:2 * w].rearrange("p a (k two) -> p a k two", two=2)[:, :, :, 0] for x_ in src]
                od = [x_[:, :, 0:2 * w].rearrange("p a (k two) -> p a k two", two=2)[:, :, :, 1] for x_ in src]
                dr_, di_ = dst[0][:, :, 0:w], dst[1][:, :, 0:w]
                t_ = Tpre[:, :, 0:w]
                Pr, Pi = pwb(ct, 8 + m, 0, w), pwb(ct, 8 + m, 1, w)
                VTT(t_, ev[0], Pr, ALU.mult, rw, [b_Tpre])
                VTT(dr_, t_, od[0], ALU.add, rw, [bE])
                VTT(t_, ev[1], Pi, ALU.mult, rw, [b_Tpre])
                VTT(dr_, dr_, t_, ALU.subtract, rw, [bE])
                if m < 3:
                    yield
                VTT(t_, ev[1], Pr, ALU.mult, rw, [b_Tpre])
                VTT(di_, t_, od[1], ALU.add, rw, [bE])
                VTT(t_, ev[0], Pi, ALU.mult, rw, [b_Tpre])
                VTT(di_, di_, t_, ALU.add, rw, [bE])
                yield
            R.op(DVE, lambda: nc.vector.tensor_copy(out=Ss5[:, 4 * ct:4 * ct + 4, 0], in_=E[0][:, :, 0]), reads=[bE], writes=[b_Ss5])
            R.op(DVE, lambda: nc.vector.tensor_copy(out=Ss5[:, 4 * ct:4 * ct + 4, 1], in_=E[1][:, :, 0]), reads=[bE], writes=[b_Ss5])

        def chunk(h, c, wv, b_wv):
            ch = slice(c * 128, (c + 1) * 128)
            T = TTp[c % 2]
            for kt in range(8):
                R.op(PE, lambda kt=kt, ch=ch, T=T, wv=wv: nc.tensor.matmul(T["pv"], lhsT=xnT[:, kt, ch], rhs=wv[:, kt, :], start=(kt == 0), stop=(kt == 7)),
                     reads=[b_xnT, b_wv], writes=[T["b_pv"]])
            vsc = zsc[:, 16 * h + c:16 * h + c + 1]
            R.op(ACT, lambda vsc=vsc, T=T: nc.scalar.activation(out=T["vz_bf"], in_=T["pv"], func=AF.Copy, scale=vsc), reads=[T["b_pv"], b_cst], writes=[T["b_v"]])
            for tl in range(2):
                R.op(PE, lambda tl=tl, ch=ch, T=T: nc.tensor.transpose(out=T["ptk"][:, tl, :], in_=krT[:, tl, ch], identity=ident_b), reads=[b_krT, b_ident], writes=[T["b_ptk"]])
            R.op(ACT, lambda T=T: nc.scalar.copy(out=T["ktok"].rearrange("p (a b) -> p a b", b=128), in_=T["ptk"]), reads=[T["b_ptk"]], writes=[T["b_ktok"]])
            for tl in range(2):
                R.op(PE, lambda tl=tl, T=T, c=c: nc.tensor.matmul(plc[:, tl * 256:(tl + 1) * 256], lhsT=T["ktok"][:, tl * 128:(tl + 1) * 128], rhs=T["vz_bf"],
                                                                 start=(c == 0 and tl == 0), stop=(c == NCH - 1 and tl == 1)),
                     reads=[T["b_ktok"], T["b_v"]], writes=[b_plc])
            if c == NCH - 1:
                R.op(DVE, lambda h=h: nc.vector.scalar_tensor_tensor(out=Sret[:, h, :], in0=Sret[:, h, :], scalar=G128[h] ** 16, in1=plc, op0=ALU.mult, op1=ALU.add),
                     reads=[b_Sret, b_plc], writes=[b_Sret])
                R.op(ACT, lambda h=h: nc.scalar.copy(out=Sbf[:, h, :], in_=Sret[:, h, :]), reads=[b_Sret], writes=[b_Sbf])

        for h in range(4):
            wk, b_wk = load_wblock(w_in0, 3072 + h * 256)
            rotary_k(wk, b_wk)
            wv, b_wv = load_wblock(w_in0, 4096 + h * 256)
            for half in range(2):
                gen = tile_steps(2 * h + half)
                for c in range(8 * half, 8 * half + 8):
                    chunk(h, c, wv, b_wv)
                    for _ in range(2):
                        try:
                            next(gen)
                        except StopIteration:
                            break
                for _ in gen:
                    pass

    def layer0(dst_dram):
        own_x = x_seq[3 * SEG:4 * SEG, :]
        load_gain(norm_even)
        s5_precompute()
        R.op(DVE, lambda: nc.vector.memset(Sret, 0.0), reads=[], writes=[b_Sret])
        R.op(DVE, lambda: nc.vector.memset(Sbf, 0.0), reads=[], writes=[b_Sbf])
        R.op(DVE, lambda: nc.vector.memset(Ss5, 0.0), reads=[], writes=[b_Ss5])
        wo0 = carve_at(OFF_X, [128, 8, D], BF16)
        for seg in range(SEG0, NSEG):
            own = seg == NSEG - 1
            norm_transpose(x_seq, seg * SEG)
            R.barrier()
            if not own:
                prefix_segment(seg)
                R.barrier()
                continue
            s5_segment(own)
            if own:
                R.barrier()
                glu()
                R.barrier()
                b_wo0 = k.buf("wo0a")
                out_proj_half(w_out0, 0, own_x, dst_dram, wo0, b_wo0)
            R.barrier()
            ret_segment(seg, own)
            if own:
                R.barrier()
                b_wo0 = k.buf("wo0b")
                out_proj_half(w_out0, 1024, dst_dram, dst_dram, wo0, b_wo0)
            R.barrier()

    def final(src_dram):
        R.barrier()
        apos[0] = 0
        fg = carve([128, D], F32)
        b_fg = k.buf("fg")
        R.dma(lambda: nc.sync.dma_start(out=fg, in_=final_norm.partition_broadcast(128)), b_fg, writes=[b_fg])
        for c in range(NCH):
            i = c % 2
            key = (id(src_dram), c)
            R.dma(lambda i=i, c=c: nc.sync.dma_start(out=xc[i], in_=src_dram[c * 128:(c + 1) * 128, :]), b_xc[i],
                  reads=[b_x[key]] if key in b_x else [], writes=[b_xc[i]])
            R.op(ACT, lambda i=i: nc.scalar.activation(out=junk, in_=xc[i], func=AF.Square, accum_out=stat[i][:, 0:1]),
                 reads=[b_xc[i]], writes=[b_junk, b_stat[i]])
            R.op(DVE, lambda i=i: nc.vector.tensor_scalar(out=stat[i][:, 1:2], in0=stat[i][:, 0:1], scalar1=1.0 / D, scalar2=EPS,
                                                          op0=ALU.mult, op1=ALU.add), reads=[b_stat[i]], writes=[b_stat[i]])
            R.op(ACT, lambda i=i: nc.scalar.activation(out=stat[i][:, 3:4], in_=stat[i][:, 1:2], func=AF.Sqrt), reads=[b_stat[i]], writes=[b_stat[i]])
            R.op(DVE, lambda i=i: nc.vector.reciprocal(out=stat[i][:, 2:3], in_=stat[i][:, 3:4]), reads=[b_stat[i]], writes=[b_stat[i]])
            R.op(DVE, lambda i=i: nc.vector.scalar_tensor_tensor(out=xc[i], in0=xc[i], scalar=stat[i][:, 2:3], in1=fg, op0=ALU.mult, op1=ALU.mult),
                 reads=[b_xc[i], b_stat[i], b_fg], writes=[b_xc[i]])
            R.dma(lambda i=i, c=c: nc.sync.dma_start(out=out[c * 128:(c + 1) * 128, :], in_=xc[i]), b_xc[i], reads=[b_xc[i]], writes=[])

    SEG0 = 0 if mode != "l0own" else 3
    if mode == "l1":
        layer1(x_seq[3 * SEG:4 * SEG, :], x2)
        final(x2)
    elif mode == "pre":
        load_gain(norm_even)
        s5_precompute()
        for c in range(NCH):
            i = c % 2
            R.dma(lambda i=i, c=c: nc.sync.dma_start(out=xc[i], in_=x_seq[3 * SEG + c * 128:3 * SEG + (c + 1) * 128, :]), b_xc[i], writes=[b_xc[i]])
            R.dma(lambda i=i, c=c: nc.sync.dma_start(out=out[c * 128:(c + 1) * 128, :], in_=xc[i]), b_xc[i], reads=[b_xc[i]], writes=[])
    elif mode in ("l0", "l0own"):
        layer0(x1)
        R.barrier()
        for c in range(NCH):
            i = c % 2
            R.dma(lambda i=i, c=c: nc.sync.dma_start(out=xc[i], in_=x1[c * 128:(c + 1) * 128, :]), b_xc[i], reads=[b_x[(id(x1), c)]], writes=[b_xc[i]])
            R.dma(lambda i=i, c=c: nc.sync.dma_start(out=out[c * 128:(c + 1) * 128, :], in_=xc[i]), b_xc[i], reads=[b_xc[i]], writes=[])
    else:
        layer0(x1)
        layer1(x1, x2)
        final(x2)
    rec.emit()
    return nc


_CACHE = {}


def _consts(q):
    tril = np.triu(np.ones((128, 128), np.float32))
    ident = np.eye(128, dtype=np.float32)
    half = 128
    inv = 10000.0 ** (-np.arange(half, dtype=np.float64) / half)
    pos = (q * SEG - (NSEG - 1) * SEG) + np.arange(NSEG * SEG, dtype=np.float64)
    ang = inv[:, None] * pos[None, :]
    rot = np.stack([np.cos(ang), np.sin(ang)], 0)
    rot = rot.reshape(2, 128, NSEG, SEG).transpose(2, 0, 1, 3).astype(np.float32)
    gam = 1.0 - 2.0 ** (-5.0 - np.arange(4, dtype=np.float64))
    idx = np.arange(128, dtype=np.float64)
    diff = idx[None, :] - idx[:, None]
    dtab = np.where(diff[:, None, :] >= 0, gam[None, :, None] ** np.maximum(diff[:, None, :], 0.0), 0.0) * (256.0 ** -0.5)
    xi = gam[None, :] ** (idx[:, None] + 1.0)
    zs = gam[None, :] ** (127.0 - idx[:, None]) * (256.0 ** -0.5)
    cc = np.arange(16, dtype=np.float64)
    zsc = zs[:, :, None] * (gam[None, :, None] ** (128.0 * (15.0 - cc[None, None, :])))
    return {"c_tril": tril, "c_ident": ident, "c_rot": np.ascontiguousarray(rot), "c_dt": dtab.astype(np.float32),
            "c_xizs": np.concatenate([xi, zs], 1).astype(np.float32), "c_zsc": zsc.reshape(128, 64).astype(np.float32)}


def make_maps(inputs):
    x = np.asarray(inputs["x"], np.float32)
    sq = lambda n: np.ascontiguousarray(np.asarray(inputs[n], np.float32)[0])
    shared = {n: sq(n) for n in ["norm_even", "w_in_even", "s5_lam_re", "s5_lam_im", "s5_log_dt", "s5_b_re", "s5_b_im", "s5_c_re", "s5_c_im",
                                 "s5_d", "s5_w_glu", "s5_b_glu", "ret_gn_gain", "w_out_even", "norm_odd", "w_in_odd", "sgu_norm_gain",
                                 "sgu_w_spatial", "sgu_b_spatial", "w_out_odd"]}
    shared["final_norm"] = np.ascontiguousarray(np.asarray(inputs["final_norm"], np.float32))
    maps = []
    for c in range(8):
        b, q = c // 4, c % 4
        xs = np.zeros((NSEG * SEG, D), np.float32)
        lo = q * SEG - (NSEG - 1) * SEG
        src = x[b, max(lo, 0):(q + 1) * SEG]
        xs[NSEG * SEG - src.shape[0]:] = src
        m = dict(shared)
        m["x_seq"] = xs
        m.update(_consts(q))
        maps.append(m)
    return maps


def kernel(**inputs):
    maps = make_maps(inputs)
    nc = bass.Bass("TRN2", target_bir_lowering=False)
    build(nc, "full")
    res = run_bass_kernel_spmd(nc, maps, core_ids=list(range(8)))
    out = np.stack([np.asarray(r["out"], np.float32) for r in res.results]).reshape(2, 4 * SEG, D)
    return out
```

```python
import math
import os
import numpy as np
SKIP = {k_: True for k_ in os.environ.get('KSKIP', '').split(',') if k_}
import ml_dtypes
import concourse.bass as bass
import concourse.mybir as mybir
from concourse.bass_utils import run_bass_kernel_spmd

F32 = mybir.dt.float32
BF16 = mybir.dt.bfloat16
AF = mybir.ActivationFunctionType
ALU = mybir.AluOpType
AX = mybir.AxisListType

D = 1024
SEG = 2048
NCH = SEG // 128
NSEG = 4
EPS = 1e-6
PE, ACT, DVE, POOL, SP = 0, 1, 2, 3, 4


class Buf:
    __slots__ = ("name", "w", "r", "dsem", "dcnt")

    def __init__(self, name):
        self.name = name
        self.w = None
        self.r = []
        self.dsem = None
        self.dcnt = 0


class Rec:
    def __init__(self, nc):
        self.nc = nc
        self.ops = []
        self.engs = [nc.tensor, nc.scalar, nc.vector, nc.gpsimd, nc.sync]

    def op(self, eng, fn, reads=(), writes=(), dma=False):
        idx = len(self.ops)
        deps = set()
        raw = set()
        for b in reads:
            if b.w is not None:
                deps.add(b.w)
                raw.add(b.w)
        for b in writes:
            if b.w is not None:
                deps.add(b.w)
            for r in b.r:
                deps.add(r)
        self.ops.append(dict(eng=eng, fn=fn, deps=deps, raw=raw, dma=dma, sig=False, dbuf=None))
        for b in reads:
            if not dma:
                b.r = [r for r in b.r if self.ops[r]["dma"] or self.ops[r]["eng"] != eng]
            b.r.append(idx)
        for b in writes:
            b.w = idx
            b.r = []
        return idx

    def barrier(self):
        n = len(self.ops)
        lb = getattr(self, "_lb", 0)
        deps = set()
        last = {}
        for j in range(lb, n):
            o = self.ops[j]
            if o["dma"]:
                deps.add(j)
            else:
                last[o["eng"]] = j
        deps.update(last.values())
        for e in range(5):
            eng = self.engs[e]
            self.ops.append(dict(eng=e, fn=(lambda eng=eng: eng.nop()), deps=set(deps), raw=set(), dma=False, sig=False, dbuf=None))
        self._lb = n

    def dma(self, fn, sbuf_side, reads=(), writes=(), eng=SP):
        idx = self.op(eng, fn, reads, writes, dma=True)
        self.ops[idx]["dbuf"] = sbuf_side
        return idx

    def emit(self):
        nc = self.nc
        ops = self.ops
        for i, o in enumerate(ops):
            for d in o["deps"]:
                p = ops[d]
                if p["dma"] or p["eng"] != o["eng"] or o["dma"] or o["eng"] != PE:
                    p["sig"] = True
        esem = [nc.alloc_semaphore("es%d" % i) for i in range(5)]
        ecnt = [0] * 5
        tok = [None] * len(ops)
        waited = [dict() for _ in range(5)]
        final = {}
        for i, o in enumerate(ops):
            e = o["eng"]
            eng = self.engs[e]
            need = {}
            for d in o["deps"]:
                p = ops[d]
                if (not p["dma"]) and p["eng"] == e and not o["dma"] and e == PE:
                    continue
                if (not p["dma"]) and p["eng"] == e and o["dma"] and e == SP:
                    continue
                s, v = tok[d]
                k = id(s)
                if k not in need or need[k][1] < v:
                    need[k] = (s, v)
            for k, (s, v) in need.items():
                if waited[e].get(k, 0) >= v:
                    continue
                eng.wait_ge(s, v)
                waited[e][k] = v
            ins = o["fn"]()
            if o["dma"]:
                b = o["dbuf"]
                if b.dsem is None or b.dcnt >= 800:
                    b.dcnt = 0
                    self._nsem = getattr(self, "_nsem", 0) + 1
                    b.dsem = nc.alloc_semaphore("d%d_%s" % (self._nsem, b.name))
                b.dcnt += 16
                ins.then_inc(b.dsem, 16)
                tok[i] = (b.dsem, b.dcnt)
                final[id(b.dsem)] = (b.dsem, b.dcnt)
            elif o["sig"]:
                if ecnt[e] >= 3000:
                    self._nsem = getattr(self, "_nsem", 0) + 1
                    esem[e] = nc.alloc_semaphore("es%d_%d" % (e, self._nsem))
                    ecnt[e] = 0
                ecnt[e] += 1
                ins.then_inc(esem[e], 1)
                tok[i] = (esem[e], ecnt[e])
            o["fn"] = None
        for k, (s, v) in final.items():
            if waited[SP].get(k, 0) < v:
                nc.sync.wait_ge(s, v)


class K:
    def __init__(self, nc, rec):
        self.nc = nc
        self.rec = rec
        self.nb = 0

    def sb(self, name, shape, dt):
        t = self.nc.alloc_sbuf_tensor(name, list(shape), dt).ap()
        return t

    def ps(self, name, shape, dt=F32):
        return self.nc.alloc_psum_tensor(name, list(shape), dt).ap()

    def buf(self, name=None):
        self.nb += 1
        return Buf(name or ("b%d" % self.nb))


def bcast_rows(ap_1d_dram, n):
    return ap_1d_dram.partition_broadcast(128)


def build(nc, mode="full"):
    rec = Rec(nc)
    k = K(nc, rec)
    dr = lambda name, shape, dt=F32, kind="ExternalInput": nc.dram_tensor(name, list(shape), dt, kind=kind).ap()
    x_seq = dr("x_seq", [NSEG * SEG, D])
    w_in1 = dr("w_in_odd", [D, 6144])
    w_out1 = dr("w_out_odd", [2048, D])
    norm_odd = dr("norm_odd", [D])
    sgu_gain = dr("sgu_norm_gain", [2048])
    sgu_w = dr("sgu_w_spatial", [4, 128, 128])
    sgu_b = dr("sgu_b_spatial", [4, 128])
    final_norm = dr("final_norm", [D])
    tril = dr("c_tril", [128, 128])
    ident_d = dr("c_ident", [128, 128])
    out = dr("out", [SEG, D], F32, kind="ExternalOutput")
    x1 = dr("x1_scratch", [SEG, D], F32, kind="Internal")
    x2 = dr("x2_scratch", [SEG, D], F32, kind="Internal")

    nct = nc
    R = rec

    ident_f = k.sb("ident_f", [128, 128], F32)
    ident_b = k.sb("ident_b", [128, 128], BF16)
    b_ident = k.buf("ident")
    R.dma(lambda: nc.sync.dma_start(out=ident_f, in_=ident_d), b_ident, writes=[b_ident])
    R.op(DVE, lambda: nc.vector.tensor_copy(out=ident_b, in_=ident_f), reads=[b_ident], writes=[b_ident])

    ARENA = 53248
    arena = k.sb("arena", [128, ARENA], BF16)
    apos = [0]

    def carve(shape, dt):
        n = 1
        for d_ in shape[1:]:
            n *= d_
        nb16 = n * (2 if dt == F32 else 1)
        a = apos[0]
        apos[0] += nb16
        assert apos[0] <= ARENA, apos[0]
        v = arena[:, a:a + nb16]
        if dt == F32:
            v = v.bitcast(F32)
        if len(shape) == 3:
            v = v.rearrange("p (a b) -> p a b", b=shape[2])
        return v

    def carve_at(off, shape, dt):
        n = 1
        for d_ in shape[1:]:
            n *= d_
        nb16 = n * (2 if dt == F32 else 1)
        assert off + nb16 <= ARENA, (off, nb16)
        v = arena[:, off:off + nb16]
        if dt == F32:
            v = v.bitcast(F32)
        if len(shape) == 3:
            v = v.rearrange("p (a b) -> p a b", b=shape[2])
        elif len(shape) == 4:
            v = v.rearrange("p (a b c) -> p a b c", b=shape[2], c=shape[3])
        return v

    xnT = k.sb("xnT", [128, 8, SEG], BF16)
    b_xnT = k.buf("xnT")
    pg = k.ps("pg", [128, 4, 512], F32)
    b_pg = [k.buf("pg%d" % i) for i in range(4)]
    yT = k.sb("yT", [128, 8, SEG], BF16)
    b_yT = k.buf("yT")

    xc = [k.sb("xc%d" % i, [128, D], F32) for i in range(2)]
    b_xc = [k.buf("xc%d" % i) for i in range(2)]
    xb = [k.sb("xb0", [128, D], BF16)] * 2
    b_xb = [k.buf("xb0")] * 2
    t1 = k.sb("t1", [128, 1024], F32)
    b_t1 = k.buf("t1")
    junk = t1[:, :D]
    b_junk = b_t1
    stat = [k.sb("stat%d" % i, [128, 8], F32) for i in range(2)]
    b_stat = [k.buf("stat%d" % i) for i in range(2)]
    pT = [k.ps("pT0", [128, 8, 128], BF16)] * 2
    pT2 = k.ps("pT2", [128, 8, 128], BF16)
    b_pT2 = k.buf("pT2")
    b_pT = [k.buf("pT0")] * 2
    b_ptq_fix = [b_pT[0], b_pT2]

    def norm_transpose(src_dram, row0):
        for c in range(NCH):
            i = c % 2
            rows = src_dram[row0 + c * 128: row0 + (c + 1) * 128, :]
            skey = (id(src_dram), c)
            R.dma(lambda i=i, rows=rows: nc.sync.dma_start(out=xc[i], in_=rows), b_xc[i], reads=[b_x[skey]] if (row0 == 0 and skey in b_x) else [], writes=[b_xc[i]])
            R.op(ACT, lambda i=i: nc.scalar.activation(out=junk, in_=xc[i], func=AF.Square, accum_out=stat[i][:, 0:1]),
                 reads=[b_xc[i]], writes=[b_junk, b_stat[i]])
            R.op(DVE, lambda i=i: nc.vector.tensor_scalar(out=stat[i][:, 1:2], in0=stat[i][:, 0:1], scalar1=1.0 / D, scalar2=EPS,
                                                          op0=ALU.mult, op1=ALU.add), reads=[b_stat[i]], writes=[b_stat[i]])
            R.op(ACT, lambda i=i: nc.scalar.activation(out=stat[i][:, 3:4], in_=stat[i][:, 1:2], func=AF.Sqrt), reads=[b_stat[i]], writes=[b_stat[i]])
            R.op(DVE, lambda i=i: nc.vector.reciprocal(out=stat[i][:, 2:3], in_=stat[i][:, 3:4]), reads=[b_stat[i]], writes=[b_stat[i]])
            R.op(DVE, lambda i=i: nc.vector.scalar_tensor_tensor(out=xb[i], in0=xc[i], scalar=stat[i][:, 2:3], in1=gbc, op0=ALU.mult, op1=ALU.mult),
                 reads=[b_xc[i], b_stat[i], b_gbc], writes=[b_xb[i]])
            for kt in range(8):
                R.op(PE, lambda i=i, kt=kt: nc.tensor.transpose(out=pT[i][:, kt, :], in_=xb[i][:, kt * 128:(kt + 1) * 128], identity=ident_b),
                     reads=[b_xb[i], b_ident], writes=[b_pT[i]])
            R.op(DVE, lambda i=i, c=c: nc.vector.tensor_copy(out=xnT[:, :, c * 128:(c + 1) * 128], in_=pT[i]),
                 reads=[b_pT[i]], writes=[b_xnT])

    wbf = [k.sb("wbf%d" % i, [128, 8, 256], BF16) for i in range(3)]
    b_wbf = [k.buf("wbf%d" % i) for i in range(3)]
    gbc = k.sb("gbc", [128, D], F32)
    b_gbc = k.buf("gbc")
    wctr = [0]

    def load_gain(g_dram):
        R.dma(lambda: nc.sync.dma_start(out=gbc, in_=g_dram.partition_broadcast(128)), b_gbc, writes=[b_gbc])

    def load_wblock(w_dram, c0, ncols=256, gain=True):
        i = wctr[0] % 3
        wctr[0] += 1
        src = w_dram.rearrange("(kt p) n -> p kt n", p=128)[:, :, c0:c0 + ncols]
        R.dma(lambda: nc.gpsimd.dma_start(out=wbf[i][:, :, :ncols], in_=src), b_wbf[i], writes=[b_wbf[i]], eng=POOL)
        return wbf[i], b_wbf[i]

    pacc = [k.ps("pacc%d" % i, [128, 512], F32) for i in range(2)]
    b_pacc = [k.buf("pacc%d" % i) for i in range(2)]
    pctr = [0]

    wide = [True]

    def next_pacc():
        if wide[0]:
            i = pctr[0] % 6
            pctr[0] += 1
            if i < 2:
                return pacc[i], b_pacc[i]
            return pg[:, i - 2, :], b_pg[i - 2]
        i = pctr[0] % 2
        pctr[0] += 1
        return pacc[i], b_pacc[i]

    def layer1(src_dram, dst_dram):
        load_gain(norm_odd)
        norm_transpose(src_dram, 0)
        R.barrier()
        apos[0] = 0
        vgain = carve([128, 2048], F32)
        b_vgain = k.buf("vgain")
        R.dma(lambda: nc.sync.dma_start(out=vgain, in_=sgu_gain.partition_broadcast(128)), b_vgain, writes=[b_vgain])
        bsp = k.sb("bsp", [128, 4, 128], F32)
        b_bsp = k.buf("bsp")
        R.dma(lambda: nc.sync.dma_start(out=bsp, in_=sgu_b.partition_broadcast(128)), b_bsp, writes=[b_bsp])
        wraw = k.sb("wraw", [128, 4, 128], F32)
        b_wraw = k.buf("wraw")
        R.dma(lambda: nc.sync.dma_start(out=wraw, in_=sgu_w.rearrange("g t s -> t g s")), b_wraw, writes=[b_wraw])
        trl = k.sb("trl", [128, 128], F32)
        b_trl = k.buf("trl")
        R.dma(lambda: nc.sync.dma_start(out=trl, in_=tril), b_trl, writes=[b_trl])
        wmT = k.sb("wmT", [128, 4, 128], BF16)
        b_wmT = k.buf("wmT")
        ptw = pacc[0].rearrange("p (g t) -> p g t", t=128)
        b_ptw = b_pacc[0]
        for g in range(4):
            R.op(PE, lambda g=g: nc.tensor.transpose(out=ptw[:, g, :], in_=wraw[:, g, :], identity=ident_f),
                 reads=[b_wraw, b_ident], writes=[b_ptw])
        R.op(DVE, lambda: nc.vector.tensor_tensor(out=wmT, in0=ptw, in1=trl.unsqueeze(1).to_broadcast([128, 4, 128]), op=ALU.mult),
             reads=[b_ptw, b_trl], writes=[b_wmT])

        vn = carve([128, NCH, 2048], BF16)
        b_vn = k.buf("vn")
        vf = carve([128, 2048], F32)
        b_vf = k.buf("vf")
        bst = k.sb("bst", [128, 4, 6], F32)
        mv = k.sb("mv", [128, 4], F32)
        b_bst = k.buf("bst")

        def gelu_from(psrc, b_psrc, dst, b_dst, n):
            R.op(ACT, lambda: nc.scalar.activation(out=dst, in_=psrc, func=AF.Gelu_apprx_tanh), reads=[b_psrc], writes=[b_dst])

        uT1 = carve([128, SEG], BF16)
        b_uT1 = k.buf("uT1")
        zT1 = carve([128, SEG], BF16)
        b_zT1 = k.buf("zT1")
        vf2 = arena[:, apos[0] - 4096:apos[0]].bitcast(F32)
        vfs = [vf, vf2]
        b_vfs = [[b_vf], [b_uT1, b_zT1]]
        for blk in range(8):
            src = w_in1.rearrange("(kt p) n -> p kt n", p=128)[:, :, 2048 + blk * 256:2048 + (blk + 1) * 256]
            R.dma(lambda blk=blk, src=src: nc.gpsimd.dma_start(out=yT[:, :, blk * 256:(blk + 1) * 256], in_=src), b_yT, writes=[b_yT], eng=POOL)
        for c in range(NCH):
            vfc, bvf = vfs[c % 2], b_vfs[c % 2]
            for blk in range(8):
                pa, b_pa = next_pacc()
                for kt in range(8):
                    R.op(PE, lambda pa=pa, kt=kt, c=c, blk=blk: nc.tensor.matmul(pa[:, :256], lhsT=xnT[:, kt, c * 128:(c + 1) * 128], rhs=yT[:, kt, blk * 256:(blk + 1) * 256],
                                                                                 start=(kt == 0), stop=(kt == 7)),
                         reads=[b_xnT, b_yT], writes=[b_pa])
                R.op(ACT, lambda pa=pa, vfc=vfc, blk=blk: nc.scalar.activation(out=vfc[:, blk * 256:(blk + 1) * 256], in_=pa[:, :256], func=AF.Gelu_apprx_tanh), reads=[b_pa], writes=bvf)
            for j in range(4):
                R.op(DVE, lambda j=j, vfc=vfc: nc.vector.bn_stats(out=bst[:, j, :], in_=vfc[:, j * 512:(j + 1) * 512]), reads=bvf, writes=[b_bst])
            R.op(DVE, lambda: nc.vector.bn_aggr(out=mv[:, 0:2], in_=bst), reads=[b_bst], writes=[b_bst])
            R.op(DVE, lambda: nc.vector.tensor_scalar(out=mv[:, 3:4], in0=mv[:, 1:2], scalar1=EPS, scalar2=None, op0=ALU.add),
                 reads=[b_bst], writes=[b_bst])
            R.op(ACT, lambda: nc.scalar.activation(out=mv[:, 3:4], in_=mv[:, 3:4], func=AF.Sqrt), reads=[b_bst], writes=[b_bst])
            R.op(DVE, lambda: nc.vector.reciprocal(out=mv[:, 2:3], in_=mv[:, 3:4]), reads=[b_bst], writes=[b_bst])
            R.op(DVE, lambda vfc=vfc: nc.vector.tensor_scalar(out=vfc, in0=vfc, scalar1=mv[:, 0:1], scalar2=mv[:, 2:3], op0=ALU.subtract, op1=ALU.mult),
                 reads=bvf + [b_bst], writes=bvf)
            R.op(DVE, lambda c=c, vfc=vfc: nc.vector.tensor_tensor(out=vn[:, c, :], in0=vfc, in1=vgain, op=ALU.mult), reads=bvf + [b_vgain], writes=[b_vn])

        psT = pg
        b_psT = k.buf("psT")
        wo = carve([128, 8, D], BF16)
        b_wo = k.buf("wo")
        for half in range(2):
            for jt in range(8):
                j = half * 8 + jt
                g = j // 4
                if jt % 2 == 0:
                    wu, b_wu = load_wblock(w_in1, j * 128)
                    wz, b_wz = load_wblock(w_in1, 4096 + j * 128)
                    off = 0
                else:
                    off = 128
                for nb in range(4):
                    pa, b_pa = next_pacc()
                    for kt in range(8):
                        R.op(PE, lambda pa=pa, wu=wu, kt=kt, nb=nb, off=off: nc.tensor.matmul(pa, lhsT=wu[:, kt, off:off + 128], rhs=xnT[:, kt, nb * 512:(nb + 1) * 512],
                                                                                              start=(kt == 0), stop=(kt == 7)),
                             reads=[b_xnT, b_wu], writes=[b_pa])
                    gelu_from(pa, b_pa, uT1[:, nb * 512:(nb + 1) * 512], b_uT1, 512)
                    pz, b_pz = next_pacc()
                    for kt in range(8):
                        R.op(PE, lambda pz=pz, wz=wz, kt=kt, nb=nb, off=off: nc.tensor.matmul(pz, lhsT=wz[:, kt, off:off + 128], rhs=xnT[:, kt, nb * 512:(nb + 1) * 512],
                                                                                              start=(kt == 0), stop=(kt == 7)),
                             reads=[b_xnT, b_wz], writes=[b_pz])
                    R.op(ACT, lambda pz=pz, nb=nb: nc.scalar.activation(out=zT1[:, nb * 512:(nb + 1) * 512], in_=pz, func=AF.Silu),
                         reads=[b_pz], writes=[b_zT1])
                for c in range(NCH):
                    R.op(PE, lambda c=c, j=j, g=g: nc.tensor.matmul(psT[:, c // 4, (c % 4) * 128:(c % 4 + 1) * 128], lhsT=vn[:, c, j * 128:(j + 1) * 128],
                                                                     rhs=wmT[:, g, :], start=True, stop=True),
                         reads=[b_vn, b_wmT], writes=[b_pg[c // 4]])
                R.op(DVE, lambda: nc.vector.tensor_tensor(out=uT1, in0=uT1, in1=zT1, op=ALU.mult), reads=[b_uT1, b_zT1], writes=[b_uT1])
                R.op(DVE, lambda g=g: nc.vector.tensor_tensor(out=zT1.rearrange("p (c t) -> p c t", t=128), in0=psT.rearrange("p a (b t) -> p (a b) t", t=128),
                                                              in1=bsp[:, g:g + 1, :].to_broadcast([128, NCH, 128]), op=ALU.add),
                     reads=b_pg + [b_bsp], writes=[b_zT1])
                R.op(DVE, lambda jt=jt: nc.vector.tensor_tensor(out=yT[:, jt, :], in0=uT1, in1=zT1, op=ALU.mult), reads=[b_uT1, b_zT1], writes=[b_yT])
            out_proj_half(w_out1, half * 1024, src_dram if half == 0 else dst_dram, dst_dram, wo, b_wo)

    b_x = {}

    def out_proj_half(w_dram, row0, srcd, dstd, wo, b_wo):
        for ct2 in range(2):
            r0 = row0 + ct2 * 512
            R.dma(lambda r0=r0, ct2=ct2: nc.gpsimd.dma_start(out=wo[:, 4 * ct2:4 * ct2 + 4, :], in_=w_dram[r0:r0 + 512, :].rearrange("(c p) n -> p c n", p=128)), b_wo, writes=[b_wo], eng=POOL)
        for c in range(NCH):
            i = c % 2
            skey = (id(srcd), c)
            R.dma(lambda i=i, c=c: nc.sync.dma_start(out=xc[i], in_=srcd[c * 128:(c + 1) * 128, :]), b_xc[i],
                  reads=[b_x[skey]] if skey in b_x else [], writes=[b_xc[i]])
            for nb in range(2):
                pa, b_pa = next_pacc()
                for ct in range(8):
                    R.op(PE, lambda pa=pa, ct=ct, c=c, nb=nb: nc.tensor.matmul(pa, lhsT=yT[:, ct, c * 128:(c + 1) * 128], rhs=wo[:, ct, nb * 512:(nb + 1) * 512],
                                                                               start=(ct == 0), stop=(ct == 7)),
                         reads=[b_yT, b_wo], writes=[b_pa])
                R.op(DVE, lambda pa=pa, i=i, nb=nb: nc.vector.tensor_tensor(out=xc[i][:, nb * 512:(nb + 1) * 512], in0=xc[i][:, nb * 512:(nb + 1) * 512], in1=pa, op=ALU.add),
                     reads=[b_pa, b_xc[i]], writes=[b_xc[i]])
            key = (id(dstd), c)
            if key not in b_x:
                b_x[key] = k.buf("xd")
            R.dma(lambda i=i, c=c: nc.sync.dma_start(out=dstd[c * 128:(c + 1) * 128, :], in_=xc[i]), b_xc[i],
                  reads=[b_xc[i]], writes=[b_x[key]])

    norm_even = dr("norm_even", [D])
    w_in0 = dr("w_in_even", [D, 6144])
    w_out0 = dr("w_out_even", [2048, D])
    lam_re_d = dr("s5_lam_re", [64, 64])
    lam_im_d = dr("s5_lam_im", [64, 64])
    log_dt_d = dr("s5_log_dt", [64])
    b_re_d = dr("s5_b_re", [64, 64, 16])
    b_im_d = dr("s5_b_im", [64, 64, 16])
    c_re_d = dr("s5_c_re", [64, 16, 64])
    c_im_d = dr("s5_c_im", [64, 16, 64])
    s5_d_d = dr("s5_d", [1024])
    w_glu_d = dr("s5_w_glu", [1024, 1024])
    b_glu_d = dr("s5_b_glu", [1024])
    gn_gain_d = dr("ret_gn_gain", [1024])
    rot_d = dr("c_rot", [NSEG, 2, 128, SEG])
    dt_d = dr("c_dt", [128, 4, 128])
    xizs_d = dr("c_xizs", [128, 8])
    zsc_d = dr("c_zsc", [128, 64])
    tabB = dr("tabB", [8, 16, 128, 512], BF16, kind="Internal")
    tabCL = dr("tabCL", [8, 16, 128, 512], BF16, kind="Internal")
    tabK = dr("tabK", [8, 128, 1024], BF16, kind="Internal")
    b_tabK = k.buf("tabK_d")
    b_ptq = [None, None]
    b_tabB = k.buf("tabB_d")
    b_tabC = k.buf("tabC_d")

    OFF_UT, OFF_X, OFF_XB, OFF_E, OFF_PW, OFF_TBC, OFF_SRET, OFF_SBF, OFF_QK = 0, 16384, 24576, 28672, 31744, 34816, 38912, 43008, 45056
    uT = carve_at(OFF_UT, [128, 8, SEG], BF16)
    b_uT = k.buf("uT")
    Xre = carve_at(OFF_X, [128, SEG], F32)
    Xim = carve_at(OFF_X + 4096, [128, SEG], F32)
    b_X = k.buf("X")
    Xbre = carve_at(OFF_XB, [128, SEG], BF16)
    Xbim = carve_at(OFF_XB + 2048, [128, SEG], BF16)
    b_Xb = k.buf("Xb")
    EA = [carve_at(OFF_E + 768 * j, [128, 384], F32) for j in range(2)]
    EB = [carve_at(OFF_E + 768 * (2 + j), [128, 384], F32) for j in range(2)]
    b_E = k.buf("E")
    pw = carve_at(OFF_PW, [128, 32, 16, 3], F32)
    b_pw = k.buf("pw")
    tB = [carve_at(OFF_TBC + 1024 * j, [128, 2, 512], BF16) for j in range(2)]
    b_tB = [k.buf("tB%d" % j) for j in range(2)]
    tC = [carve_at(OFF_TBC + 2048 + 1024 * j, [128, 2, 512], BF16) for j in range(2)]
    b_tC = [k.buf("tC%d" % j) for j in range(2)]
    Sret = carve_at(OFF_SRET, [128, 4, 512], F32)
    b_Sret = k.buf("Sret")
    Sbf = carve_at(OFF_SBF, [128, 4, 512], BF16)
    b_Sbf = k.buf("Sbf")
    qrT = carve_at(OFF_QK, [128, 2, SEG], BF16)
    krT = carve_at(OFF_QK + 4096, [128, 2, SEG], BF16)
    b_qrT = k.buf("qrT")
    b_krT = k.buf("krT")
    Ss5 = k.sb("Ss5", [128, 32, 2], F32)
    b_Ss5 = k.buf("Ss5")
    dcol = k.sb("dcol", [128, 16], F32)
    b_dcol = k.buf("dcol")

    def gelu_sb(src, b_src, dst, b_dst, scr, b_scr):
        R.op(ACT, lambda: nc.scalar.activation(out=scr, in_=src, func=AF.Square), reads=[b_src], writes=[b_scr])
        R.op(DVE, lambda: nc.vector.tensor_scalar(out=scr, in0=scr, scalar1=0.044715 * 1.5957691216, scalar2=1.5957691216,
                                                  op0=ALU.mult, op1=ALU.add), reads=[b_scr], writes=[b_scr])
        R.op(DVE, lambda: nc.vector.tensor_tensor(out=scr, in0=scr, in1=src, op=ALU.mult), reads=[b_scr, b_src], writes=[b_scr])
        R.op(ACT, lambda: nc.scalar.activation(out=scr, in_=scr, func=AF.Sigmoid), reads=[b_scr], writes=[b_scr])
        R.op(DVE, lambda: nc.vector.tensor_tensor(out=dst, in0=scr, in1=src, op=ALU.mult), reads=[b_scr, b_src], writes=[b_dst])

    def s5_precompute():
        R.barrier()
        bp = k.buf("pre")
        pos = [OFF_UT]

        def tmp(shape, dt=F32):
            n = 1
            for d_ in shape[1:]:
                n *= d_
            n16 = n * (2 if dt == F32 else 1)
            v = carve_at(pos[0], shape, dt)
            pos[0] += n16
            assert pos[0] <= OFF_PW, pos[0]
            return v

        def VT(out, a, b, op):
            R.op(DVE, lambda: nc.vector.tensor_tensor(out=out, in0=a, in1=b, op=op), reads=[bp], writes=[bp])

        def VS(out, a, s1, op0, s2=None, op1=None):
            if op1 is None:
                R.op(DVE, lambda: nc.vector.tensor_scalar(out=out, in0=a, scalar1=s1, scalar2=None, op0=op0), reads=[bp], writes=[bp])
            else:
                R.op(DVE, lambda: nc.vector.tensor_scalar(out=out, in0=a, scalar1=s1, scalar2=s2, op0=op0, op1=op1), reads=[bp], writes=[bp])

        def AC(out, a, func):
            R.op(ACT, lambda: nc.scalar.activation(out=out, in_=a, func=func), reads=[bp], writes=[bp])

        def LD(out, src):
            R.dma(lambda: nc.sync.dma_start(out=out, in_=src, allow_slow_non_contiguous=True), bp, writes=[bp])

        S = [128, 32]
        lr, li, ldt, dtv, x1, mag, ang, r, sn, cs, are, aim, den, nre, zre, zim, ta, tb = [tmp(S) for _ in range(18)]
        LD(lr, lam_re_d.rearrange("(i gg) p -> (gg p) i", gg=2))
        LD(li, lam_im_d.rearrange("(i gg) p -> (gg p) i", gg=2))
        for gg in range(2):
            LD(ldt[gg * 64:(gg + 1) * 64, :], log_dt_d.rearrange("(i gg) -> gg i", gg=2)[gg].partition_broadcast(64))
        VS(lr, lr, -1e-4, ALU.min)
        AC(dtv, ldt, AF.Exp)
        VT(x1, lr, dtv, ALU.mult)
        AC(mag, x1, AF.Exp)
        VT(ang, li, dtv, ALU.mult)
        MAGIC = 12582912.0

        def reduce_sin(dst, a_in):
            VS(r, a_in, 1.0 / (2 * math.pi), ALU.mult)
            VS(ta, r, MAGIC, ALU.add)
            VS(ta, ta, -MAGIC, ALU.add)
            VS(tb, ta, -2 * math.pi, ALU.mult)
            VT(r, a_in, tb, ALU.add)
            VS(r, r, 3.14159, ALU.min, -3.14159, ALU.max)
            AC(dst, r, AF.Sin)

        reduce_sin(sn, ang)
        VS(x1, ang, 0.5 * math.pi, ALU.add)
        reduce_sin(cs, x1)
        VT(are, mag, cs, ALU.mult)
        VT(aim, mag, sn, ALU.mult)
        VT(den, lr, lr, ALU.mult)
        VT(ta, li, li, ALU.mult)
        VT(den, den, ta, ALU.add)
        R.op(DVE, lambda: nc.vector.reciprocal(out=den, in_=den), reads=[bp], writes=[bp])
        VS(nre, are, -1.0, ALU.add)
        VT(ta, nre, lr, ALU.mult)
        VT(tb, aim, li, ALU.mult)
        VT(ta, ta, tb, ALU.add)
        VT(zre, ta, den, ALU.mult)
        VT(ta, aim, lr, ALU.mult)
        VT(tb, nre, li, ALU.mult)
        VT(ta, ta, tb, ALU.subtract)
        VT(zim, ta, den, ALU.mult)
        def P(kk, j):
            return pw[:, :, kk, j]

        def setp(kk, re_ap, im_ap):
            R.op(DVE, lambda: nc.vector.tensor_copy(out=P(kk, 0), in_=re_ap), reads=[bp], writes=[bp, b_pw])
            R.op(DVE, lambda: nc.vector.tensor_copy(out=P(kk, 1), in_=im_ap), reads=[bp], writes=[bp, b_pw])
            R.op(DVE, lambda: nc.vector.tensor_scalar(out=P(kk, 2), in0=im_ap, scalar1=-1.0, scalar2=None, op0=ALU.mult), reads=[bp], writes=[bp, b_pw])

        def cmul(ore, oim, a_re, a_im, b_re_, b_im_):
            VT(ta, a_re, b_re_, ALU.mult)
            VT(tb, a_im, b_im_, ALU.mult)
            VT(ore, ta, tb, ALU.subtract)
            VT(ta, a_re, b_im_, ALU.mult)
            VT(tb, a_im, b_re_, ALU.mult)
            VT(oim, ta, tb, ALU.add)

        cr, ci, nr, ni = [tmp(S) for _ in range(4)]
        setp(0, are, aim)
        R.op(DVE, lambda: nc.vector.tensor_copy(out=cr, in_=are), reads=[bp], writes=[bp])
        R.op(DVE, lambda: nc.vector.tensor_copy(out=ci, in_=aim), reads=[bp], writes=[bp])
        for kk in range(1, 8):
            cmul(nr, ni, cr, ci, are, aim)
            R.op(DVE, lambda: nc.vector.tensor_copy(out=cr, in_=nr), reads=[bp], writes=[bp])
            R.op(DVE, lambda: nc.vector.tensor_copy(out=ci, in_=ni), reads=[bp], writes=[bp])
            setp(kk, cr, ci)
        setp(8, cr, ci)
        for m in range(1, 8):
            cmul(nr, ni, cr, ci, cr, ci)
            R.op(DVE, lambda: nc.vector.tensor_copy(out=cr, in_=nr), reads=[bp], writes=[bp])
            R.op(DVE, lambda: nc.vector.tensor_copy(out=ci, in_=ni), reads=[bp], writes=[bp])
            setp(8 + m, cr, ci)
        S3 = [128, 32, 16]
        bre, bim, Bre, Bim, t3a, t3b = [tmp(S3) for _ in range(6)]
        LD(bre, b_re_d.rearrange("(i gg) p h -> (gg p) i h", gg=2))
        LD(bim, b_im_d.rearrange("(i gg) p h -> (gg p) i h", gg=2))
        zre3 = zre.unsqueeze(2).to_broadcast(S3)
        zim3 = zim.unsqueeze(2).to_broadcast(S3)
        VT(t3a, bre, zre3, ALU.mult)
        VT(t3b, bim, zim3, ALU.mult)
        VT(Bre, t3a, t3b, ALU.subtract)
        VT(t3a, bim, zre3, ALU.mult)
        VT(t3b, bre, zim3, ALU.mult)
        VT(Bim, t3a, t3b, ALU.add)
        Blre, Blim = tmp(S3), tmp(S3)
        Wb = [tmp([128, 32, 128], BF16) for _ in range(2)]
        b_wp = [k.buf("wpad%d" % j) for j in range(2)]
        stages = [tmp([128, 8, 128], BF16) for _ in range(2)]
        b_stage = [k.buf("stg%d" % j) for j in range(2)]
        Cpad0 = tmp([128, 8, 2, 512], BF16)
        b_c0 = k.buf("cpad0")
        cin = [tmp([128, 128]) for _ in range(2)]
        b_cin = [k.buf("cin%d" % j) for j in range(2)]
        b_trs = k.buf("trs")
        Kst = [tmp([128, 128], BF16) for _ in range(2)]
        ptmp = tmp([128, 16])
        b_kst = [k.buf("kst%d" % j) for j in range(2)]
        for j in range(2):
            R.op(DVE, lambda j=j: nc.vector.memset(Wb[j], 0.0), reads=[], writes=[b_wp[j]])
        R.op(DVE, lambda: nc.vector.memset(Cpad0, 0.0), reads=[], writes=[b_c0])
        TrsAll = [tmp([128, 8, 128], BF16) for _ in range(2)]
        for t in range(8):
            for ri, cd in enumerate((c_re_d, c_im_d)):
                src = cd.rearrange("(t gl) h p -> t (gl h) p", gl=8)[t]
                R.dma(lambda ri=ri, src=src: nc.sync.dma_start(out=cin[ri][:, 0:64], in_=src), b_cin[ri], writes=[b_cin[ri]])
                R.dma(lambda ri=ri, src=src: nc.sync.dma_start(out=cin[ri][:, 64:128], in_=src), b_cin[ri], writes=[b_cin[ri]])
                ptc = pacc[ri][:, 0:128]
                R.op(PE, lambda ri=ri, ptc=ptc: nc.tensor.transpose(out=ptc, in_=cin[ri], identity=ident_f), reads=[b_cin[ri], b_ident], writes=[b_pacc[ri]])
                R.op(ACT, lambda ri=ri, ptc=ptc, t=t: nc.scalar.copy(out=TrsAll[ri][:, t, :], in_=ptc), reads=[b_pacc[ri]], writes=[b_trs])
        for kk in range(4):
            for gg in range(2):
                rs = slice(gg * 64, (gg + 1) * 64)
                c0 = 32 * kk + 16 * gg
                R.op(DVE, lambda kk=kk, rs=rs, c0=c0: nc.vector.tensor_copy(out=Cpad0[rs, :, 0, kk * 128 + c0:kk * 128 + c0 + 16], in_=TrsAll[0][rs, :, c0:c0 + 16]), reads=[b_trs], writes=[b_c0])
                R.op(DVE, lambda kk=kk, rs=rs, c0=c0: nc.vector.tensor_scalar(out=Cpad0[rs, :, 1, kk * 128 + c0:kk * 128 + c0 + 16], in0=TrsAll[1][rs, :, c0:c0 + 16], scalar1=-1.0, scalar2=None, op0=ALU.mult),
                     reads=[b_trs], writes=[b_c0])
        nst = [0]
        for lag in range(8):
            if lag == 0:
                srcs = (Bre, Bim)
            else:
                ar3 = pw[:, :, lag - 1, 0].unsqueeze(2).to_broadcast(S3)
                ai3 = pw[:, :, lag - 1, 1].unsqueeze(2).to_broadcast(S3)
                VT(t3a, Bre, ar3, ALU.mult)
                VT(t3b, Bim, ai3, ALU.mult)
                VT(Blre, t3a, t3b, ALU.subtract)
                VT(t3a, Bim, ar3, ALU.mult)
                VT(t3b, Bre, ai3, ALU.mult)
                VT(Blim, t3a, t3b, ALU.add)
                srcs = (Blre, Blim)
            for ri, Bt in enumerate(srcs):
                for kk in range(4):
                    R.op(DVE, lambda Bt=Bt, kk=kk, ri=ri: nc.vector.tensor_copy(out=Wb[ri][0:64, kk::4, 32 * kk:32 * kk + 16], in_=Bt[0:64, kk::4, :]), reads=[bp, b_wp[ri]], writes=[b_wp[ri]])
                    R.op(DVE, lambda Bt=Bt, kk=kk, ri=ri: nc.vector.tensor_copy(out=Wb[ri][64:128, kk::4, 32 * kk + 16:32 * kk + 32], in_=Bt[64:128, kk::4, :]), reads=[bp, b_wp[ri]], writes=[b_wp[ri]])
                for t2 in range(4):
                    j = nst[0] % 2
                    nst[0] += 1
                    ptr = (pT[0] if j == 0 else pT2)
                    for kk in range(8):
                        R.op(PE, lambda kk=kk, t2=t2, ptr=ptr, ri=ri: nc.tensor.transpose(out=ptr[:, kk, :], in_=Wb[ri][:, 8 * t2 + kk, :], identity=ident_b), reads=[b_wp[ri], b_ident], writes=[b_ptq_fix[j]])
                    R.op(ACT, lambda j=j, ptr=ptr: nc.scalar.copy(out=stages[j], in_=ptr), reads=[b_ptq_fix[j]], writes=[b_stage[j]])
                    R.dma(lambda t2=t2, ri=ri, lag=lag, j=j: nc.sync.dma_start(out=tabB[2 * t2:2 * t2 + 2, 2 * lag + ri].rearrange("t p n -> p t n"),
                                                                              in_=stages[j].rearrange("p (t a) b -> p t (a b)", t=2)), b_stage[j], reads=[b_stage[j]], writes=[b_tabB])
            for t in range(8 if not SKIP.get('K') else 0):
                j = t % 2
                pk = pacc[j][:, 0:128]
                for kk in range(4):
                    for ri in range(2):
                        R.op(PE, lambda pk=pk, kk=kk, ri=ri, t=t: nc.tensor.matmul(pk, lhsT=Wb[ri][:, 4 * t + kk, :], rhs=Cpad0[:, t, ri, kk * 128:(kk + 1) * 128],
                                                                                   start=(kk == 0 and ri == 0), stop=(kk == 3 and ri == 1)), reads=[b_wp[ri], b_c0], writes=[b_pacc[j]])
                R.op(ACT, lambda j=j, pk=pk: nc.scalar.copy(out=Kst[j], in_=pk), reads=[b_pacc[j]], writes=[b_kst[j]])
                R.dma(lambda t=t, lag=lag, j=j: nc.sync.dma_start(out=tabK[t, :, lag * 128:(lag + 1) * 128], in_=Kst[j]), b_kst[j], reads=[b_kst[j]], writes=[b_tabK])
        CpadAll = [Wb[ri].rearrange("p a b -> p (a b)").rearrange("p (t n) -> p t n", n=512) for ri in range(2)]
        tc1, tc2 = tmp([128, 8, 16]), tmp([128, 8, 16])
        for ri in range(2):
            R.op(DVE, lambda ri=ri: nc.vector.memset(Wb[ri], 0.0), reads=[b_wp[ri]], writes=[b_wp[ri]])
        for s_ in range(8):
            for kk in range(4):
                for gg in range(2):
                    rs = slice(gg * 64, (gg + 1) * 64)
                    c0 = 32 * kk + 16 * gg
                    S8 = [64, 8, 16]
                    Ar = pw[rs, kk::4, s_, 0].unsqueeze(2).to_broadcast(S8)
                    Ai = pw[rs, kk::4, s_, 1].unsqueeze(2).to_broadcast(S8)
                    Tr_, Ti_ = TrsAll[0][rs, :, c0:c0 + 16], TrsAll[1][rs, :, c0:c0 + 16]
                    o_re = CpadAll[0][rs, :, kk * 128 + c0:kk * 128 + c0 + 16]
                    o_im = CpadAll[1][rs, :, kk * 128 + c0:kk * 128 + c0 + 16]
                    a_, b_ = tc1[rs], tc2[rs]
                    rd = [b_trs, b_pw, bp]
                    R.op(DVE, lambda a_=a_, Tr_=Tr_, Ar=Ar: nc.vector.tensor_tensor(out=a_, in0=Tr_, in1=Ar, op=ALU.mult), reads=rd, writes=[bp])
                    R.op(DVE, lambda b_=b_, Ti_=Ti_, Ai=Ai: nc.vector.tensor_tensor(out=b_, in0=Ti_, in1=Ai, op=ALU.mult), reads=rd, writes=[bp])
                    R.op(DVE, lambda o_re=o_re, a_=a_, b_=b_: nc.vector.tensor_tensor(out=o_re, in0=a_, in1=b_, op=ALU.subtract), reads=[bp, b_wp[0]], writes=[b_wp[0]])
                    R.op(DVE, lambda a_=a_, Tr_=Tr_, Ai=Ai: nc.vector.tensor_tensor(out=a_, in0=Tr_, in1=Ai, op=ALU.mult), reads=rd, writes=[bp])
                    R.op(DVE, lambda b_=b_, Ti_=Ti_, Ar=Ar: nc.vector.tensor_tensor(out=b_, in0=Ti_, in1=Ar, op=ALU.mult), reads=rd, writes=[bp])
                    R.op(DVE, lambda o_im=o_im, a_=a_, b_=b_: nc.vector.tensor_tensor(out=o_im, in0=a_, in1=b_, op=ALU.add), reads=[bp, b_wp[1]], writes=[b_wp[1]])
            for ri in range(2):
                R.dma(lambda s_=s_, ri=ri: nc.sync.dma_start(out=tabCL[:, 2 * s_ + ri].rearrange("t p n -> p t n"), in_=CpadAll[ri]), b_wp[ri], reads=[b_wp[ri]], writes=[b_tabC])
        R.dma(lambda: nc.sync.dma_start(out=dcol[:, 0:8], in_=s5_d_d.rearrange("(t p) -> p t", p=128), allow_slow_non_contiguous=True), b_dcol, writes=[b_dcol])
        R.dma(lambda: nc.sync.dma_start(out=dcol[:, 8:16], in_=b_glu_d.rearrange("(t p) -> p t", p=128), allow_slow_non_contiguous=True), b_dcol, writes=[b_dcol])
        R.barrier()

    def stt(eng_id, out, in0, scalar, in1, reads, writes):
        e = nc.vector if eng_id == DVE else nc.gpsimd
        R.op(eng_id, lambda: e.scalar_tensor_tensor(out=out, in0=in0, scalar=scalar, in1=in1, op0=ALU.mult, op1=ALU.add), reads=reads, writes=writes)

    tBL = [yT.rearrange("p a b -> p (a b)")[:, j * 8192:(j + 1) * 8192].rearrange("p (l n) -> p l n", n=512) for j in range(2)]
    b_tBL = [k.buf("tBL%d" % j) for j in range(2)]
    tCL = carve_at(OFF_X, [128, 16, 512], BF16)
    b_tCL = k.buf("tCL")
    tK = [carve_at(OFF_TBC + 1024 * j, [128, 8, 128], BF16) for j in range(2)]
    b_tK = [k.buf("tK%d" % j) for j in range(2)]
    carry_b = [carve_at(OFF_XB + 6144, [128, 4, 256], BF16), carve_at(OFF_TBC + 2048, [128, 4, 256], BF16)]
    b_carry = k.buf("carry")
    Epre = [[carve_at(base + 2048 * j, [128, 4, 256], F32) for j in range(4)] for base in (OFF_X, OFF_QK)]
    b_Epre = [k.buf("Epre%d" % j) for j in range(2)]
    Tpre = carve_at(OFF_XB, [128, 4, 128], F32)
    b_Tpre = k.buf("Tpre")
    HA = [carve_at(OFF_QK + 3072 * j, [128, 4, 384], F32) for j in range(2)]
    HB = [carve_at(OFF_XB + 3072 * j, [128, 4, 384], F32) for j in range(2)]
    Ths = carve_at(OFF_QK + 6144, [128, 4, 256], F32)
    b_H = k.buf("H")
    pgv = pg.rearrange("p a b -> p (a b)").rearrange("p (s j) -> p s j", j=256)

    def VTT(out, a, b, op, reads, writes):
        R.op(DVE, lambda: nc.vector.tensor_tensor(out=out, in0=a, in1=b, op=op), reads=reads, writes=writes)

    def pwb(ct, idx, comp, w):
        return pw[:, 4 * ct:4 * ct + 4, idx, comp].unsqueeze(2).to_broadcast([128, 4, w])

    def inject(ct, e_re, e_im, tmp4, reads, writes):
        sre, sim = Ss5[:, 4 * ct:4 * ct + 4, 0], Ss5[:, 4 * ct:4 * ct + 4, 1]
        p8r, p8i = pw[:, 4 * ct:4 * ct + 4, 7, 0], pw[:, 4 * ct:4 * ct + 4, 7, 1]
        rd = reads + [b_pw, b_Ss5]
        VTT(tmp4, sre, p8r, ALU.mult, rd, writes)
        VTT(e_re, e_re, tmp4, ALU.add, rd, writes)
        VTT(tmp4, sim, p8i, ALU.mult, rd, writes)
        VTT(e_re, e_re, tmp4, ALU.subtract, rd, writes)
        VTT(tmp4, sim, p8r, ALU.mult, rd, writes)
        VTT(e_im, e_im, tmp4, ALU.add, rd, writes)
        VTT(tmp4, sre, p8i, ALU.mult, rd, writes)
        VTT(e_im, e_im, tmp4, ALU.add, rd, writes)

    uS3 = t1.bitcast(BF16).rearrange("p (s j) -> p s j", j=256)

    def deinterleave(ct):
        uv_ = uT[:, ct, :].rearrange("p (j s) -> p s j", s=8)
        R.op(POOL, lambda: nc.gpsimd.tensor_copy(out=uS3, in_=uv_), reads=[b_uT], writes=[b_t1])

    def e_matmuls(ct, sl, dst_re, dst_im, bdst, col0):
        uv = uS3
        for kk in range(4):
            for ri, dst in enumerate((dst_re, dst_im)):
                pe_ = pacc[ri][:, 0:256]
                for s_ in range(8):
                    lag = 7 - s_
                    R.op(PE, lambda pe_=pe_, lag=lag, ri=ri, kk=kk, s_=s_: nc.tensor.matmul(pe_, lhsT=tBL[sl][:, 2 * lag + ri, kk * 128:(kk + 1) * 128], rhs=uv[:, s_, :],
                                                                                            start=(s_ == 0), stop=(s_ == 7)), reads=[b_tBL[sl], b_t1], writes=[b_pacc[ri]])
                R.op(ACT, lambda pe_=pe_, dst=dst, kk=kk: nc.scalar.copy(out=dst[:, kk, col0:col0 + 256], in_=pe_), reads=[b_pacc[ri]], writes=[bdst])

    def s5_prefix_tile(ct):
        sl = ct % 2
        R.dma(lambda: nc.sync.dma_start(out=tBL[sl], in_=tabB[ct].rearrange("r p n -> p r n")), b_tBL[sl], reads=[b_tabB], writes=[b_tBL[sl]])
        E = Epre[sl]
        bE = b_Epre[sl]
        deinterleave(ct)
        e_matmuls(ct, sl, E[0], E[1], bE, 0)
        inject(ct, E[0][:, :, 0], E[1][:, :, 0], Tpre[:, :, 0], [bE, b_Tpre], [bE, b_Tpre])
        rw = [bE, b_pw, b_Tpre]
        for m in range(8):
            w = 128 >> m
            src = (E[0], E[1]) if m % 2 == 0 else (E[2], E[3])
            dst = (E[2], E[3]) if m % 2 == 0 else (E[0], E[1])
            ev = [x_[:, :, 0:2 * w].rearrange("p a (k two) -> p a k two", two=2)[:, :, :, 0] for x_ in src]
            od = [x_[:, :, 0:2 * w].rearrange("p a (k two) -> p a k two", two=2)[:, :, :, 1] for x_ in src]
            dr_, di_ = dst[0][:, :, 0:w], dst[1][:, :, 0:w]
            t_ = Tpre[:, :, 0:w]
            Pr, Pi = pwb(ct, 8 + m, 0, w), pwb(ct, 8 + m, 1, w)
            VTT(t_, ev[0], Pr, ALU.mult, rw, [b_Tpre])
            VTT(dr_, t_, od[0], ALU.add, rw, [bE])
            VTT(t_, ev[1], Pi, ALU.mult, rw, [b_Tpre])
            VTT(dr_, dr_, t_, ALU.subtract, rw, [bE])
            VTT(t_, ev[1], Pr, ALU.mult, rw, [b_Tpre])
            VTT(di_, t_, od[1], ALU.add, rw, [bE])
            VTT(t_, ev[0], Pi, ALU.mult, rw, [b_Tpre])
            VTT(di_, di_, t_, ALU.add, rw, [bE])
        R.op(DVE, lambda: nc.vector.tensor_copy(out=Ss5[:, 4 * ct:4 * ct + 4, 0], in_=E[0][:, :, 0]), reads=[bE], writes=[b_Ss5])
        R.op(DVE, lambda: nc.vector.tensor_copy(out=Ss5[:, 4 * ct:4 * ct + 4, 1], in_=E[1][:, :, 0]), reads=[bE], writes=[b_Ss5])

    def s5_own_tile(ct):
        sl = ct % 2
        R.dma(lambda: nc.sync.dma_start(out=tBL[sl], in_=tabB[ct].rearrange("r p n -> p r n")), b_tBL[sl], reads=[b_tabB], writes=[b_tBL[sl]])
        R.dma(lambda: nc.sync.dma_start(out=tCL, in_=tabCL[ct].rearrange("r p n -> p r n")), b_tCL, reads=[b_tabC], writes=[b_tCL])
        R.dma(lambda: nc.sync.dma_start(out=tK[sl], in_=tabK[ct].rearrange("p (l n) -> p l n", n=128)), b_tK[sl], reads=[b_tabK], writes=[b_tK[sl]])
        uv = uT[:, ct, :].rearrange("p (j s) -> p s j", s=8)
        deinterleave(ct)
        for s_ in range(8):
            for lag in range(s_ + 1):
                R.op(PE, lambda s_=s_, lag=lag: nc.tensor.matmul(pgv[:, s_, :], lhsT=tK[sl][:, lag, :], rhs=uS3[:, s_ - lag, :], start=(lag == 0 and s_ % 2 == 0), stop=False),
                     reads=[b_tK[sl], b_t1], writes=[b_pg[s_ // 2]])
        e_matmuls(ct, sl, HA[0], HA[1], b_H, 128)
        inject(ct, HA[0][:, :, 128], HA[1][:, :, 128], Ths[:, :, 0], [b_H], [b_H])
        rw = [b_H, b_pw]

        def cmac(dre, dim_, sre_, sim_, m, w):
            Pr, Pi = pwb(ct, 8 + m, 0, w), pwb(ct, 8 + m, 1, w)
            t_ = Ths[:, :, 0:w]
            VTT(t_, sre_, Pr, ALU.mult, rw, [b_H])
            VTT(dre, dre, t_, ALU.add, rw, [b_H])
            VTT(t_, sim_, Pi, ALU.mult, rw, [b_H])
            VTT(dre, dre, t_, ALU.subtract, rw, [b_H])
            VTT(t_, sim_, Pr, ALU.mult, rw, [b_H])
            VTT(dim_, dim_, t_, ALU.add, rw, [b_H])
            VTT(t_, sre_, Pi, ALU.mult, rw, [b_H])
            VTT(dim_, dim_, t_, ALU.add, rw, [b_H])

        def sview(buf_, first, cnt, step):
            return buf_[:, :, 128 + first:128 + first + (cnt - 1) * step + 1:step]

        for m in range(8):
            d_ = 1 << m
            n_ = 256 // (2 * d_)
            cmac(sview(HA[0], 2 * d_ - 1, n_, 2 * d_), sview(HA[1], 2 * d_ - 1, n_, 2 * d_), sview(HA[0], d_ - 1, n_, 2 * d_), sview(HA[1], d_ - 1, n_, 2 * d_), m, n_)
        for m in range(6, -1, -1):
            d_ = 1 << m
            n_ = 256 // (2 * d_) - 1
            cmac(sview(HA[0], 3 * d_ - 1, n_, 2 * d_), sview(HA[1], 3 * d_ - 1, n_, 2 * d_), sview(HA[0], 2 * d_ - 1, n_, 2 * d_), sview(HA[1], 2 * d_ - 1, n_, 2 * d_), m, n_)
        for ri in range(2):
            R.op(DVE, lambda ri=ri: nc.vector.tensor_copy(out=HA[ri][:, :, 127], in_=Ss5[:, 4 * ct:4 * ct + 4, ri]), reads=[b_Ss5, b_H], writes=[b_H])
        for ri in range(2):
            R.op(DVE, lambda ri=ri: nc.vector.tensor_copy(out=Ss5[:, 4 * ct:4 * ct + 4, ri], in_=HA[ri][:, :, 383]), reads=[b_H], writes=[b_Ss5])
        R.op(ACT, lambda: nc.scalar.copy(out=carry_b[0], in_=HA[0][:, :, 127:383]), reads=[b_H], writes=[b_carry])
        R.op(ACT, lambda: nc.scalar.activation(out=carry_b[1], in_=HA[1][:, :, 127:383], func=AF.Copy, scale=-1.0), reads=[b_H], writes=[b_carry])
        for ri in range(2):
            R.op(DVE, lambda ri=ri: nc.vector.memset(HA[ri][:, :, 127:128], 0.0), reads=[b_carry], writes=[b_H])
        for kk in range(4):
            for s_ in range(8):
                for ri in range(2):
                    R.op(PE, lambda kk=kk, s_=s_, ri=ri: nc.tensor.matmul(pgv[:, s_, :], lhsT=tCL[:, 2 * s_ + ri, kk * 128:(kk + 1) * 128], rhs=carry_b[ri][:, kk, :],
                                                                         start=False, stop=(kk == 3 and ri == 1 and s_ % 2 == 1)), reads=[b_tCL, b_carry], writes=[b_pg[s_ // 2]])
        for nb in range(4):
            ya, sc = t1[:, 0:512], t1[:, 512:1024]
            ya3 = ya.rearrange("p (s j) -> p s j", j=256)
            uvs = uv[:, 2 * nb:2 * nb + 2, :]
            R.op(DVE, lambda nb=nb, ya3=ya3, uvs=uvs: nc.vector.scalar_tensor_tensor(out=ya3, in0=uvs, scalar=dcol[:, ct:ct + 1], in1=pg[:, nb, :].rearrange("p (s j) -> p s j", j=256),
                                                                                    op0=ALU.mult, op1=ALU.add), reads=[b_uT, b_dcol, b_pg[nb]], writes=[b_t1])
            R.op(ACT, lambda uvs=uvs, ya3=ya3: nc.scalar.activation(out=uvs, in_=ya3, func=AF.Gelu_apprx_tanh), reads=[b_t1], writes=[b_uT])

    def s5_segment(own):
        for ct2 in range(4):
            wb, b_wb = load_wblock(w_in0, ct2 * 256)
            for hf in range(2):
                ct = 2 * ct2 + hf
                for nb in range(4):
                    pa, b_pa = next_pacc()
                    for kt in range(8):
                        R.op(PE, lambda pa=pa, wb=wb, kt=kt, nb=nb, hf=hf: nc.tensor.matmul(pa, lhsT=wb[:, kt, hf * 128:(hf + 1) * 128], rhs=xnT[:, kt, nb * 512:(nb + 1) * 512],
                                                                                            start=(kt == 0), stop=(kt == 7)), reads=[b_xnT, b_wb], writes=[b_pa])
                    R.op(ACT, lambda pa=pa, ct=ct, nb=nb: nc.scalar.copy(out=uT[:, ct, nb * 512:(nb + 1) * 512], in_=pa), reads=[b_pa], writes=[b_uT])
        if own:
            for buf_ in HA + HB:
                R.op(DVE, lambda buf_=buf_: nc.vector.memset(buf_[:, :, 0:128], 0.0), reads=[], writes=[b_H])
        for ct in range(8):
            if own:
                s5_own_tile(ct)
            else:
                s5_prefix_tile(ct)

    def glu():
        gsb = carve_at(OFF_X, [128, 512], F32)
        zsb = carve_at(OFF_X + 1024, [128, 512], F32)
        b_g = k.buf("gsb")
        for jt2 in range(4):
            wg, b_wg = load_wblock(w_glu_d, jt2 * 256, gain=False)
            wa, b_wa = load_wblock(w_in0, 1024 + jt2 * 256)
            for hf in range(2):
                jt = 2 * jt2 + hf
                for nb in range(4):
                    pa, b_pa = next_pacc()
                    for kt in range(8):
                        R.op(PE, lambda pa=pa, wg=wg, kt=kt, nb=nb, hf=hf: nc.tensor.matmul(pa, lhsT=wg[:, kt, hf * 128:(hf + 1) * 128], rhs=uT[:, kt, nb * 512:(nb + 1) * 512],
                                                                                            start=(kt == 0), stop=(kt == 7)), reads=[b_uT, b_wg], writes=[b_pa])
                    R.op(ACT, lambda pa=pa, jt=jt: nc.scalar.activation(out=gsb, in_=pa, func=AF.Sigmoid, bias=dcol[:, 8 + jt:9 + jt]), reads=[b_pa, b_dcol], writes=[b_g])
                    pz, b_pz = next_pacc()
                    for kt in range(8):
                        R.op(PE, lambda pz=pz, wa=wa, kt=kt, nb=nb, hf=hf: nc.tensor.matmul(pz, lhsT=wa[:, kt, hf * 128:(hf + 1) * 128], rhs=xnT[:, kt, nb * 512:(nb + 1) * 512],
                                                                                            start=(kt == 0), stop=(kt == 7)), reads=[b_xnT, b_wa], writes=[b_pz])
                    R.op(ACT, lambda pz=pz: nc.scalar.activation(out=zsb, in_=pz, func=AF.Silu), reads=[b_pz], writes=[b_g])
                    R.op(DVE, lambda: nc.vector.tensor_tensor(out=gsb, in0=gsb, in1=zsb, op=ALU.mult), reads=[b_g], writes=[b_g])
                    R.op(DVE, lambda jt=jt, nb=nb: nc.vector.tensor_tensor(out=yT[:, jt, nb * 512:(nb + 1) * 512], in0=uT[:, jt, nb * 512:(nb + 1) * 512], in1=gsb, op=ALU.mult),
                         reads=[b_g, b_uT], writes=[b_yT])

    G128 = [(1.0 - 2.0 ** (-5 - h)) ** 128 for h in range(4)]

    def ret_segment(seg, own):
        cosT = carve_at(OFF_X, [128, SEG], F32)
        sinT = carve_at(OFF_X + 4096, [128, SEG], F32)
        b_rot = k.buf("rot")
        R.dma(lambda: nc.sync.dma_start(out=cosT, in_=rot_d[seg, 0]), b_rot, writes=[b_rot])
        R.dma(lambda: nc.sync.dma_start(out=sinT, in_=rot_d[seg, 1]), b_rot, writes=[b_rot])
        DTt = carve_at(OFF_XB, [128, 4, 128], F32)
        xz = carve_at(OFF_XB + 1024, [128, 8], F32)
        gng = carve_at(OFF_XB + 1040, [128, 1024], F32)
        b_cst = k.buf("rcst")
        R.dma(lambda: nc.sync.dma_start(out=DTt, in_=dt_d), b_cst, writes=[b_cst])
        R.dma(lambda: nc.sync.dma_start(out=xz, in_=xizs_d), b_cst, writes=[b_cst])
        zsc = carve_at(OFF_XB + 1040 + 2048, [128, 64], F32)
        R.dma(lambda: nc.sync.dma_start(out=zsc, in_=zsc_d), b_cst, writes=[b_cst])
        R.dma(lambda: nc.sync.dma_start(out=gng, in_=gn_gain_d.partition_broadcast(128)), b_cst, writes=[b_cst])
        eo = [OFF_UT]

        def et(shape, dt):
            n = 1
            for d_ in shape[1:]:
                n *= d_
            n16 = n * (2 if dt == F32 else 1)
            v = carve_at(eo[0], shape, dt)
            eo[0] += n16
            assert eo[0] <= OFF_UT + 16384
            return v

        TT = []
        for pp in range(2):
            d_ = dict(PT=et([128, 128], BF16), o_sb=et([128, 256], F32), in_sb=et([128, 256], F32), ktok=et([128, 256], BF16), v_bf=et([128, 256], BF16),
                      vz_bf=et([128, 256], BF16), yb=et([128, 256], BF16), sbz=et([128, 256], F32), bstt=et([128, 6], F32), mvv=et([128, 4], F32))
            for nm in ("b_PT", "b_o", "b_in", "b_ktok", "b_v", "b_yb", "b_sbz", "b_bs"):
                d_[nm] = k.buf()
            d_["b_psc"], d_["b_pin"], d_["b_pcr"] = b_pg[0], b_pg[1], b_pg[2]
            tb_, btb_ = (pT[0], b_pT[0]) if pp == 0 else (pT2, b_pT2)
            d_["b_ptk"], d_["b_pty"] = btb_, btb_
            d_["ptk"] = tb_[:, 0:2, :]
            d_["pty"] = tb_[:, 2:4, :]
            d_["psc"] = pg[:, 0, 0:128]
            d_["pin"] = pg[:, 1, 0:256]
            d_["pcr"] = pg[:, 2, 0:256]
            if own:
                d_["b_pv"], d_["b_pz"] = b_pacc[0], b_pacc[1]
                d_["pv"] = pacc[0][:, 0:256]
                d_["pz"] = pacc[1][:, 0:256]
                d_["plc"], d_["b_plc"] = pg[:, 3, :], b_pg[3]
            else:
                d_["b_pv"], d_["b_pz"] = b_pacc[pp], None
                d_["pv"] = pacc[pp][:, 0:256]
                d_["pz"] = None
                d_["plc"], d_["b_plc"] = pg[:, 2 + pp, :], b_pg[2 + pp]
            TT.append(d_)
        hb = [[b_pacc[0]], [b_pacc[1]]]

        rbanks = [((pacc[0], b_pacc[0]), (pacc[1], b_pacc[1])), ((pg[:, 0, :], b_pg[0]), (pg[:, 1, :], b_pg[1])), ((pg[:, 2, :], b_pg[2]), (pg[:, 3, :], b_pg[3]))]
        rctr = [0]

        def rotary_proj(w, b_w, dstT, b_dst):
            for nb in range(4):
                blk = slice(nb * 512, (nb + 1) * 512)
                bk = rbanks[rctr[0] % 3]
                rctr[0] += 1
                for hf in range(2):
                    ps_, bps_ = bk[hf]
                    for kt in range(8):
                        R.op(PE, lambda hf=hf, kt=kt, blk=blk, ps_=ps_: nc.tensor.matmul(ps_, lhsT=w[:, kt, hf * 128:(hf + 1) * 128], rhs=xnT[:, kt, blk], start=(kt == 0), stop=(kt == 7)),
                             reads=[b_xnT, b_w], writes=[bps_])
                (p0, bp0), (p1, bp1) = bk
                ta, tb = t1[:, 0:512], t1[:, 512:1024]
                R.op(DVE, lambda blk=blk, p0=p0: nc.vector.tensor_tensor(out=ta, in0=p0, in1=cosT[:, blk], op=ALU.mult), reads=[bp0, b_rot], writes=[b_t1])
                R.op(DVE, lambda blk=blk, p1=p1: nc.vector.tensor_tensor(out=tb, in0=p1, in1=sinT[:, blk], op=ALU.mult), reads=[bp1, b_rot], writes=[b_t1])
                R.op(DVE, lambda blk=blk: nc.vector.tensor_tensor(out=dstT[:, 0, blk], in0=ta, in1=tb, op=ALU.subtract), reads=[b_t1], writes=[b_dst])
                R.op(DVE, lambda blk=blk, p0=p0: nc.vector.tensor_tensor(out=ta, in0=p0, in1=sinT[:, blk], op=ALU.mult), reads=[bp0, b_rot, b_dst], writes=[b_t1])
                R.op(DVE, lambda blk=blk, p1=p1: nc.vector.tensor_tensor(out=tb, in0=p1, in1=cosT[:, blk], op=ALU.mult), reads=[bp1, b_rot], writes=[b_t1])
                R.op(DVE, lambda blk=blk: nc.vector.tensor_tensor(out=dstT[:, 1, blk], in0=ta, in1=tb, op=ALU.add), reads=[b_t1], writes=[b_dst])

        for h in range(4):
            wk, b_wk = load_wblock(w_in0, 3072 + h * 256)
            rotary_proj(wk, b_wk, krT, b_krT)
            if own:
                wq, b_wq = load_wblock(w_in0, 2048 + h * 256)
                rotary_proj(wq, b_wq, qrT, b_qrT)
            wv, b_wv = load_wblock(w_in0, 4096 + h * 256)
            if own:
                wz, b_wz = load_wblock(w_in0, 5120 + h * 256)
            for c in range(NCH):
                ch = slice(c * 128, (c + 1) * 128)
                T = TT[c % 2]
                for kt in range(8):
                    R.op(PE, lambda kt=kt, ch=ch, T=T, wv=wv: nc.tensor.matmul(T["pv"], lhsT=xnT[:, kt, ch], rhs=wv[:, kt, :], start=(kt == 0), stop=(kt == 7)),
                         reads=[b_xnT, b_wv], writes=[T["b_pv"]])
                R.op(ACT, lambda T=T: nc.scalar.copy(out=T["v_bf"], in_=T["pv"]), reads=[T["b_pv"]], writes=[T["b_v"]])
                vsc = xz[:, 4 + h:5 + h] if own else zsc[:, 16 * h + c:16 * h + c + 1]
                R.op(ACT, lambda vsc=vsc, T=T: nc.scalar.activation(out=T["vz_bf"], in_=T["pv"], func=AF.Copy, scale=vsc), reads=[T["b_pv"], b_cst], writes=[T["b_v"]])
                for tl in range(2):
                    R.op(PE, lambda tl=tl, ch=ch, T=T: nc.tensor.transpose(out=T["ptk"][:, tl, :], in_=krT[:, tl, ch], identity=ident_b), reads=[b_krT, b_ident], writes=[T["b_ptk"]])
                R.op(DVE, lambda T=T: nc.vector.tensor_copy(out=T["ktok"].rearrange("p (a b) -> p a b", b=128), in_=T["ptk"]), reads=[T["b_ptk"]], writes=[T["b_ktok"]])
                if own:
                    for tl in range(2):
                        R.op(PE, lambda tl=tl, ch=ch, T=T: nc.tensor.matmul(T["psc"], lhsT=krT[:, tl, ch], rhs=qrT[:, tl, ch], start=(tl == 0), stop=(tl == 1)),
                             reads=[b_krT, b_qrT], writes=[T["b_psc"]])
                    R.op(DVE, lambda h=h, T=T: nc.vector.tensor_tensor(out=T["PT"], in0=T["psc"], in1=DTt[:, h, :], op=ALU.mult), reads=[T["b_psc"], b_cst], writes=[T["b_PT"]])
                    R.op(PE, lambda T=T: nc.tensor.matmul(T["pin"], lhsT=T["PT"], rhs=T["v_bf"], start=True, stop=True), reads=[T["b_PT"], T["b_v"]], writes=[T["b_pin"]])
                    for tl in range(2):
                        R.op(PE, lambda tl=tl, ch=ch, h=h, T=T: nc.tensor.matmul(T["pcr"], lhsT=qrT[:, tl, ch], rhs=Sbf[:, h, tl * 256:(tl + 1) * 256], start=(tl == 0), stop=(tl == 1)),
                             reads=[b_qrT, b_Sbf], writes=[T["b_pcr"]])
                    R.op(ACT, lambda T=T: nc.scalar.copy(out=T["in_sb"], in_=T["pin"]), reads=[T["b_pin"]], writes=[T["b_in"]])
                    R.op(DVE, lambda h=h, T=T: nc.vector.scalar_tensor_tensor(out=T["o_sb"], in0=T["pcr"], scalar=xz[:, h:h + 1], in1=T["in_sb"], op0=ALU.mult, op1=ALU.add),
                         reads=[T["b_pcr"], b_cst, T["b_in"]], writes=[T["b_o"]])
                    R.op(DVE, lambda T=T: nc.vector.bn_stats(out=T["bstt"], in_=T["o_sb"]), reads=[T["b_o"]], writes=[T["b_bs"]])
                    R.op(DVE, lambda T=T: nc.vector.bn_aggr(out=T["mvv"][:, 0:2], in_=T["bstt"]), reads=[T["b_bs"]], writes=[T["b_bs"]])
                    R.op(DVE, lambda T=T: nc.vector.tensor_scalar(out=T["mvv"][:, 3:4], in0=T["mvv"][:, 1:2], scalar1=EPS, scalar2=None, op0=ALU.add), reads=[T["b_bs"]], writes=[T["b_bs"]])
                    R.op(ACT, lambda T=T: nc.scalar.activation(out=T["mvv"][:, 3:4], in_=T["mvv"][:, 3:4], func=AF.Sqrt), reads=[T["b_bs"]], writes=[T["b_bs"]])
                    R.op(DVE, lambda T=T: nc.vector.reciprocal(out=T["mvv"][:, 2:3], in_=T["mvv"][:, 3:4]), reads=[T["b_bs"]], writes=[T["b_bs"]])
                    R.op(DVE, lambda T=T: nc.vector.tensor_scalar(out=T["o_sb"], in0=T["o_sb"], scalar1=T["mvv"][:, 0:1], scalar2=T["mvv"][:, 2:3], op0=ALU.subtract, op1=ALU.mult),
                         reads=[T["b_o"], T["b_bs"]], writes=[T["b_o"]])
                    for kt in range(8):
                        R.op(PE, lambda kt=kt, ch=ch, T=T, wz=wz: nc.tensor.matmul(T["pz"], lhsT=xnT[:, kt, ch], rhs=wz[:, kt, :], start=(kt == 0), stop=(kt == 7)),
                             reads=[b_xnT, b_wz], writes=[T["b_pz"]])
                    R.op(ACT, lambda T=T: nc.scalar.activation(out=T["sbz"], in_=T["pz"], func=AF.Silu), reads=[T["b_pz"]], writes=[T["b_sbz"]])
                    R.op(DVE, lambda h=h, T=T: nc.vector.tensor_tensor(out=T["o_sb"], in0=T["o_sb"], in1=gng[:, h * 256:(h + 1) * 256], op=ALU.mult), reads=[T["b_o"], b_cst], writes=[T["b_o"]])
                    R.op(DVE, lambda T=T: nc.vector.tensor_tensor(out=T["yb"], in0=T["o_sb"], in1=T["sbz"], op=ALU.mult), reads=[T["b_o"], T["b_sbz"]], writes=[T["b_yb"]])
                    for tl in range(2):
                        R.op(PE, lambda tl=tl, T=T: nc.tensor.transpose(out=T["pty"][:, tl, :], in_=T["yb"][:, tl * 128:(tl + 1) * 128], identity=ident_b), reads=[T["b_yb"], b_ident], writes=[T["b_pty"]])
                    R.op(DVE, lambda h=h, ch=ch, T=T: nc.vector.tensor_copy(out=yT[:, 2 * h:2 * h + 2, ch], in_=T["pty"]), reads=[T["b_pty"]], writes=[b_yT])
                if own:
                    plc, b_plc = T["plc"], T["b_plc"]
                    for tl in range(2):
                        R.op(PE, lambda tl=tl, T=T, plc=plc: nc.tensor.matmul(plc[:, tl * 256:(tl + 1) * 256], lhsT=T["ktok"][:, tl * 128:(tl + 1) * 128], rhs=T["vz_bf"], start=(tl == 0), stop=(tl == 1)),
                             reads=[T["b_ktok"], T["b_v"]], writes=[b_plc])
                    R.op(DVE, lambda h=h, plc=plc: nc.vector.scalar_tensor_tensor(out=Sret[:, h, :], in0=Sret[:, h, :], scalar=G128[h], in1=plc, op0=ALU.mult, op1=ALU.add),
                         reads=[b_Sret, b_plc], writes=[b_Sret])
                    R.op(ACT, lambda h=h: nc.scalar.copy(out=Sbf[:, h, :], in_=Sret[:, h, :]), reads=[b_Sret], writes=[b_Sbf])
                else:
                    plc, b_plc = pg[:, 3, :], b_pg[3]
                    for tl in range(2):
                        R.op(PE, lambda tl=tl, T=T, plc=plc, c=c: nc.tensor.matmul(plc[:, tl * 256:(tl + 1) * 256], lhsT=T["ktok"][:, tl * 128:(tl + 1) * 128], rhs=T["vz_bf"],
                                                                                  start=(c == 0 and tl == 0), stop=(c == NCH - 1 and tl == 1)),
                             reads=[T["b_ktok"], T["b_v"]], writes=[b_plc])
                    if c == NCH - 1:
                        R.op(DVE, lambda h=h, plc=plc: nc.vector.scalar_tensor_tensor(out=Sret[:, h, :], in0=Sret[:, h, :], scalar=G128[h] ** 16, in1=plc, op0=ALU.mult, op1=ALU.add),
                             reads=[b_Sret, b_plc], writes=[b_Sret])
                        R.op(ACT, lambda h=h: nc.scalar.copy(out=Sbf[:, h, :], in_=Sret[:, h, :]), reads=[b_Sret], writes=[b_Sbf])

    def prefix_segment(seg):
        for ct2 in range(4):
            wb, b_wb = load_wblock(w_in0, ct2 * 256)
            for hf in range(2):
                ct = 2 * ct2 + hf
                for nb in range(4):
                    pa, b_pa = next_pacc()
                    for kt in range(8):
                        R.op(PE, lambda pa=pa, wb=wb, kt=kt, nb=nb, hf=hf: nc.tensor.matmul(pa, lhsT=wb[:, kt, hf * 128:(hf + 1) * 128], rhs=xnT[:, kt, nb * 512:(nb + 1) * 512],
                                                                                            start=(kt == 0), stop=(kt == 7)), reads=[b_xnT, b_wb], writes=[b_pa])
                    R.op(ACT, lambda pa=pa, ct=ct, nb=nb: nc.scalar.copy(out=uT[:, ct, nb * 512:(nb + 1) * 512], in_=pa), reads=[b_pa], writes=[b_uT])
        yTf = yT.rearrange("p a b -> p (a b)")
        tbl = yTf[:, 0:8192].rearrange("p (l n) -> p l n", n=512)
        b_tbl = k.buf("ptbl")
        cosT = yTf[:, 8192:12288].bitcast(F32)
        sinT = yTf[:, 12288:16384].bitcast(F32)
        b_rot = k.buf("prot")
        R.dma(lambda: nc.sync.dma_start(out=cosT, in_=rot_d[seg, 0]), b_rot, writes=[b_rot])
        R.dma(lambda: nc.sync.dma_start(out=sinT, in_=rot_d[seg, 1]), b_rot, writes=[b_rot])
        zsc = carve_at(OFF_XB + 1040 + 2048, [128, 64], F32)
        b_cst = k.buf("pcst")
        R.dma(lambda: nc.sync.dma_start(out=zsc, in_=zsc_d), b_cst, writes=[b_cst])
        E = [carve_at(OFF_X + 2048 * j, [128, 4, 256], F32) for j in range(4)]
        bE = k.buf("pE")
        TTp = []
        for pp in range(2):
            base = OFF_E + pp * 768
            d_ = dict(ktok=carve_at(base, [128, 256], BF16), v_bf=carve_at(base + 256, [128, 256], BF16), vz_bf=carve_at(base + 512, [128, 256], BF16),
                      b_ktok=k.buf(), b_v=k.buf())
            d_["pv"], d_["b_pv"] = pacc[pp][:, 0:256], b_pacc[pp]
            tb_, btb_ = (pT[0], b_pT[0]) if pp == 0 else (pT2, b_pT2)
            d_["ptk"], d_["b_ptk"] = tb_[:, 0:2, :], btb_
            TTp.append(d_)
        plc, b_plc = pg[:, 3, :], b_pg[3]
        pE = [pg[:, 0, 0:256], pg[:, 1, 0:256]]
        bpE = [b_pg[0], b_pg[1]]
        rb = [((pacc[0], b_pacc[0]), (pacc[1], b_pacc[1])), ((pg[:, 2, :], b_pg[2]), (pg[:, 3, :], b_pg[3]))]
        rc = [0]

        def rotary_k(w, b_w):
            for nb in range(4):
                blk = slice(nb * 512, (nb + 1) * 512)
                bk = rb[rc[0] % 2]
                rc[0] += 1
                for hf in range(2):
                    ps_, bps_ = bk[hf]
                    for kt in range(8):
                        R.op(PE, lambda hf=hf, kt=kt, blk=blk, ps_=ps_: nc.tensor.matmul(ps_, lhsT=w[:, kt, hf * 128:(hf + 1) * 128], rhs=xnT[:, kt, blk], start=(kt == 0), stop=(kt == 7)),
                             reads=[b_xnT, b_w], writes=[bps_])
                (p0, bp0), (p1, bp1) = bk
                ta, tb = Tpre.rearrange("p a b -> p (a b)"), carve_at(OFF_XB + 3216, [128, 512], F32) if False else t1[:, 512:1024]
                ta = t1[:, 0:512]
                R.op(DVE, lambda blk=blk, p0=p0: nc.vector.tensor_tensor(out=ta, in0=p0, in1=cosT[:, blk], op=ALU.mult), reads=[bp0, b_rot], writes=[b_t1])
                R.op(DVE, lambda blk=blk, p1=p1: nc.vector.tensor_tensor(out=tb, in0=p1, in1=sinT[:, blk], op=ALU.mult), reads=[bp1, b_rot], writes=[b_t1])
                R.op(DVE, lambda blk=blk: nc.vector.tensor_tensor(out=krT[:, 0, blk], in0=ta, in1=tb, op=ALU.subtract), reads=[b_t1], writes=[b_krT])
                R.op(DVE, lambda blk=blk, p0=p0: nc.vector.tensor_tensor(out=ta, in0=p0, in1=sinT[:, blk], op=ALU.mult), reads=[bp0, b_rot, b_krT], writes=[b_t1])
                R.op(DVE, lambda blk=blk, p1=p1: nc.vector.tensor_tensor(out=tb, in0=p1, in1=cosT[:, blk], op=ALU.mult), reads=[bp1, b_rot], writes=[b_t1])
                R.op(DVE, lambda blk=blk: nc.vector.tensor_tensor(out=krT[:, 1, blk], in0=ta, in1=tb, op=ALU.add), reads=[b_t1], writes=[b_krT])

        def tile_steps(ct):
            R.dma(lambda: nc.sync.dma_start(out=tbl, in_=tabB[ct].rearrange("r p n -> p r n")), b_tbl, reads=[b_tabB], writes=[b_tbl])
            deinterleave(ct)
            for kk in range(4):
                for ri in range(2):
                    for s_ in range(8):
                        lag = 7 - s_
                        R.op(PE, lambda lag=lag, ri=ri, kk=kk, s_=s_: nc.tensor.matmul(pE[ri], lhsT=tbl[:, 2 * lag + ri, kk * 128:(kk + 1) * 128], rhs=uS3[:, s_, :],
                                                                                       start=(s_ == 0), stop=(s_ == 7)), reads=[b_tbl, b_t1], writes=[bpE[ri]])
                    R.op(ACT, lambda ri=ri, kk=kk: nc.scalar.copy(out=E[ri][:, kk, :], in_=pE[ri]), reads=[bpE[ri]], writes=[bE])
                if kk % 2 == 1:
                    yield
            inject(ct, E[0][:, :, 0], E[1][:, :, 0], Tpre[:, :, 0], [bE, b_Tpre], [bE, b_Tpre])
            yield
            rw = [bE, b_pw, b_Tpre]
            for m in range(8):
                w = 128 >> m
                src = (E[0], E[1]) if m % 2 == 0 else (E[2], E[3])
                dst = (E[2], E[3]) if m % 2 == 0 else (E[0], E[1])
                ev = [x_[:, :, 0:2 * w].rearrange("p a (k two) -> p a k two", two=2)[:, :, :, 0] for x_ in src]
                od = [x_[:, :, 0:2 * w].rearrange("p a (k two) -> p a k two", two=2)[:, :, :, 1] for x_ in src]
                dr_, di_ = dst[0][:, :, 0:w], dst[1][:, :, 0:w]
                t_ = Tpre[:, :, 0:w]
                Pr, Pi = pwb(ct, 8 + m, 0, w), pwb(ct, 8 + m, 1, w)
                VTT(t_, ev[0], Pr, ALU.mult, rw, [b_Tpre])
                VTT(dr_, t_, od[0], ALU.add, rw, [bE])
                VTT(t_, ev[1], Pi, ALU.mult, rw, [b_Tpre])
                VTT(dr_, dr_, t_, ALU.subtract, rw, [bE])
                if m < 3:
                    yield
                VTT(t_, ev[1], Pr, ALU.mult, rw, [b_Tpre])
                VTT(di_, t_, od[1], ALU.add, rw, [bE])
                VTT(t_, ev[0], Pi, ALU.mult, rw, [b_Tpre])
                VTT(di_, di_, t_, ALU.add, rw, [bE])
                yield
            R.op(DVE, lambda: nc.vector.tensor_copy(out=Ss5[:, 4 * ct:4 * ct + 4, 0], in_=E[0][:, :, 0]), reads=[bE], writes=[b_Ss5])
            R.op(DVE, lambda: nc.vector.tensor_copy(out=Ss5[:, 4 * ct:4 * ct + 4, 1], in_=E[1][:, :, 0]), reads=[bE], writes=[b_Ss5])

        def chunk(h, c, wv, b_wv):
            ch = slice(c * 128, (c + 1) * 128)
            T = TTp[c % 2]
            for kt in range(8):
                R.op(PE, lambda kt=kt, ch=ch, T=T, wv=wv: nc.tensor.matmul(T["pv"], lhsT=xnT[:, kt, ch], rhs=wv[:, kt, :], start=(kt == 0), stop=(kt == 7)),
                     reads=[b_xnT, b_wv], writes=[T["b_pv"]])
            vsc = zsc[:, 16 * h + c:16 * h + c + 1]
            R.op(ACT, lambda vsc=vsc, T=T: nc.scalar.activation(out=T["vz_bf"], in_=T["pv"], func=AF.Copy, scale=vsc), reads=[T["b_pv"], b_cst], writes=[T["b_v"]])
            for tl in range(2):
                R.op(PE, lambda tl=tl, ch=ch, T=T: nc.tensor.transpose(out=T["ptk"][:, tl, :], in_=krT[:, tl, ch], identity=ident_b), reads=[b_krT, b_ident], writes=[T["b_ptk"]])
            R.op(ACT, lambda T=T: nc.scalar.copy(out=T["ktok"].rearrange("p (a b) -> p a b", b=128), in_=T["ptk"]), reads=[T["b_ptk"]], writes=[T["b_ktok"]])
            for tl in range(2):
                R.op(PE, lambda tl=tl, T=T, c=c: nc.tensor.matmul(plc[:, tl * 256:(tl + 1) * 256], lhsT=T["ktok"][:, tl * 128:(tl + 1) * 128], rhs=T["vz_bf"],
                                                                 start=(c == 0 and tl == 0), stop=(c == NCH - 1 and tl == 1)),
                     reads=[T["b_ktok"], T["b_v"]], writes=[b_plc])
            if c == NCH - 1:
                R.op(DVE, lambda h=h: nc.vector.scalar_tensor_tensor(out=Sret[:, h, :], in0=Sret[:, h, :], scalar=G128[h] ** 16, in1=plc, op0=ALU.mult, op1=ALU.add),
                     reads=[b_Sret, b_plc], writes=[b_Sret])
                R.op(ACT, lambda h=h: nc.scalar.copy(out=Sbf[:, h, :], in_=Sret[:, h, :]), reads=[b_Sret], writes=[b_Sbf])

        for h in range(4):
            wk, b_wk = load_wblock(w_in0, 3072 + h * 256)
            rotary_k(wk, b_wk)
            wv, b_wv = load_wblock(w_in0, 4096 + h * 256)
            for half in range(2):
                gen = tile_steps(2 * h + half)
                for c in range(8 * half, 8 * half + 8):
                    chunk(h, c, wv, b_wv)
                    for _ in range(2):
                        try:
                            next(gen)
                        except StopIteration:
                            break
                for _ in gen:
                    pass

    def layer0(dst_dram):
        own_x = x_seq[3 * SEG:4 * SEG, :]
        load_gain(norm_even)
        s5_precompute()
        R.op(DVE, lambda: nc.vector.memset(Sret, 0.0), reads=[], writes=[b_Sret])
        R.op(DVE, lambda: nc.vector.memset(Sbf, 0.0), reads=[], writes=[b_Sbf])
        R.op(DVE, lambda: nc.vector.memset(Ss5, 0.0), reads=[], writes=[b_Ss5])
        wo0 = carve_at(OFF_X, [128, 8, D], BF16)
        for seg in range(SEG0, NSEG):
            own = seg == NSEG - 1
            norm_transpose(x_seq, seg * SEG)
            R.barrier()
            if not own:
                prefix_segment(seg)
                R.barrier()
                continue
            s5_segment(own)
            if own:
                R.barrier()
                glu()
                R.barrier()
                b_wo0 = k.buf("wo0a")
                out_proj_half(w_out0, 0, own_x, dst_dram, wo0, b_wo0)
            R.barrier()
            ret_segment(seg, own)
            if own:
                R.barrier()
                b_wo0 = k.buf("wo0b")
                out_proj_half(w_out0, 1024, dst_dram, dst_dram, wo0, b_wo0)
            R.barrier()

    def final(src_dram):
        R.barrier()
        apos[0] = 0
        fg = carve([128, D], F32)
        b_fg = k.buf("fg")
        R.dma(lambda: nc.sync.dma_start(out=fg, in_=final_norm.partition_broadcast(128)), b_fg, writes=[b_fg])
        for c in range(NCH):
            i = c % 2
            key = (id(src_dram), c)
            R.dma(lambda i=i, c=c: nc.sync.dma_start(out=xc[i], in_=src_dram[c * 128:(c + 1) * 128, :]), b_xc[i],
                  reads=[b_x[key]] if key in b_x else [], writes=[b_xc[i]])
            R.op(ACT, lambda i=i: nc.scalar.activation(out=junk, in_=xc[i], func=AF.Square, accum_out=stat[i][:, 0:1]),
                 reads=[b_xc[i]], writes=[b_junk, b_stat[i]])
            R.op(DVE, lambda i=i: nc.vector.tensor_scalar(out=stat[i][:, 1:2], in0=stat[i][:, 0:1], scalar1=1.0 / D, scalar2=EPS,
                                                          op0=ALU.mult, op1=ALU.add), reads=[b_stat[i]], writes=[b_stat[i]])
            R.op(ACT, lambda i=i: nc.scalar.activation(out=stat[i][:, 3:4], in_=stat[i][:, 1:2], func=AF.Sqrt), reads=[b_stat[i]], writes=[b_stat[i]])
            R.op(DVE, lambda i=i: nc.vector.reciprocal(out=stat[i][:, 2:3], in_=stat[i][:, 3:4]), reads=[b_stat[i]], writes=[b_stat[i]])
            R.op(DVE, lambda i=i: nc.vector.scalar_tensor_tensor(out=xc[i], in0=xc[i], scalar=stat[i][:, 2:3], in1=fg, op0=ALU.mult, op1=ALU.mult),
                 reads=[b_xc[i], b_stat[i], b_fg], writes=[b_xc[i]])
            R.dma(lambda i=i, c=c: nc.sync.dma_start(out=out[c * 128:(c + 1) * 128, :], in_=xc[i]), b_xc[i], reads=[b_xc[i]], writes=[])

    SEG0 = 0 if mode != "l0own" else 3
    if mode == "l1":
        layer1(x_seq[3 * SEG:4 * SEG, :], x2)
        final(x2)
    elif mode == "pre":
        load_gain(norm_even)
        s5_precompute()
        for c in range(NCH):
            i = c % 2
            R.dma(lambda i=i, c=c: nc.sync.dma_start(out=xc[i], in_=x_seq[3 * SEG + c * 128:3 * SEG + (c + 1) * 128, :]), b_xc[i], writes=[b_xc[i]])
            R.dma(lambda i=i, c=c: nc.sync.dma_start(out=out[c * 128:(c + 1) * 128, :], in_=xc[i]), b_xc[i], reads=[b_xc[i]], writes=[])
    elif mode in ("l0", "l0own"):
        layer0(x1)
        R.barrier()
        for c in range(NCH):
            i = c % 2
            R.dma(lambda i=i, c=c: nc.sync.dma_start(out=xc[i], in_=x1[c * 128:(c + 1) * 128, :]), b_xc[i], reads=[b_x[(id(x1), c)]], writes=[b_xc[i]])
            R.dma(lambda i=i, c=c: nc.sync.dma_start(out=out[c * 128:(c + 1) * 128, :], in_=xc[i]), b_xc[i], reads=[b_xc[i]], writes=[])
    else:
        layer0(x1)
        layer1(x1, x2)
        final(x2)
    rec.emit()
    return nc


_CACHE = {}


def _consts(q):
    tril = np.triu(np.ones((128, 128), np.float32))
    ident = np.eye(128, dtype=np.float32)
    half = 128
    inv = 10000.0 ** (-np.arange(half, dtype=np.float64) / half)
    pos = (q * SEG - (NSEG - 1) * SEG) + np.arange(NSEG * SEG, dtype=np.float64)
    ang = inv[:, None] * pos[None, :]
    rot = np.stack([np.cos(ang), np.sin(ang)], 0)
    rot = rot.reshape(2, 128, NSEG, SEG).transpose(2, 0, 1, 3).astype(np.float32)
    gam = 1.0 - 2.0 ** (-5.0 - np.arange(4, dtype=np.float64))
    idx = np.arange(128, dtype=np.float64)
    diff = idx[None, :] - idx[:, None]
    dtab = np.where(diff[:, None, :] >= 0, gam[None, :, None] ** np.maximum(diff[:, None, :], 0.0), 0.0) * (256.0 ** -0.5)
    xi = gam[None, :] ** (idx[:, None] + 1.0)
    zs = gam[None, :] ** (127.0 - idx[:, None]) * (256.0 ** -0.5)
    cc = np.arange(16, dtype=np.float64)
    zsc = zs[:, :, None] * (gam[None, :, None] ** (128.0 * (15.0 - cc[None, None, :])))
    return {"c_tril": tril, "c_ident": ident, "c_rot": np.ascontiguousarray(rot), "c_dt": dtab.astype(np.float32),
            "c_xizs": np.concatenate([xi, zs], 1).astype(np.float32), "c_zsc": zsc.reshape(128, 64).astype(np.float32)}


def make_maps(inputs):
    x = np.asarray(inputs["x"], np.float32)
    sq = lambda n: np.ascontiguousarray(np.asarray(inputs[n], np.float32)[0])
    shared = {n: sq(n) for n in ["norm_even", "w_in_even", "s5_lam_re", "s5_lam_im", "s5_log_dt", "s5_b_re", "s5_b_im", "s5_c_re", "s5_c_im",
                                 "s5_d", "s5_w_glu", "s5_b_glu", "ret_gn_gain", "w_out_even", "norm_odd", "w_in_odd", "sgu_norm_gain",
                                 "sgu_w_spatial", "sgu_b_spatial", "w_out_odd"]}
    shared["final_norm"] = np.ascontiguousarray(np.asarray(inputs["final_norm"], np.float32))
    maps = []
    for c in range(8):
        b, q = c // 4, c % 4
        xs = np.zeros((NSEG * SEG, D), np.float32)
        lo = q * SEG - (NSEG - 1) * SEG
        src = x[b, max(lo, 0):(q + 1) * SEG]
        xs[NSEG * SEG - src.shape[0]:] = src
        m = dict(shared)
        m["x_seq"] = xs
        m.update(_consts(q))
        maps.append(m)
    return maps


def kernel(**inputs):
    maps = make_maps(inputs)
    nc = bass.Bass("TRN2", target_bir_lowering=False)
    build(nc, "full")
    res = run_bass_kernel_spmd(nc, maps, core_ids=list(range(8)))
    out = np.stack([np.asarray(r["out"], np.float32) for r in res.results]).reshape(2, 4 * SEG, D)
    return out
```

```python
import math
import os
import numpy as np
SKIP = {k_: True for k_ in os.environ.get('KSKIP', '').split(',') if k_}
import ml_dtypes
import concourse.bass as bass
import concourse.mybir as mybir
from concourse.bass_utils import run_bass_kernel_spmd

F32 = mybir.dt.float32
BF16 = mybir.dt.bfloat16
AF = mybir.ActivationFunctionType
ALU = mybir.AluOpType
AX = mybir.AxisListType

D = 1024
SEG = 2048
NCH = SEG // 128
NSEG = 4
EPS = 1e-6
PE, ACT, DVE, POOL, SP = 0, 1, 2, 3, 4


class Buf:
    __slots__ = ("name", "w", "r", "dsem", "dcnt")

    def __init__(self, name):
        self.name = name
        self.w = None
        self.r = []
        self.dsem = None
        self.dcnt = 0


class Rec:
    def __init__(self, nc):
        self.nc = nc
        self.ops = []
        self.engs = [nc.tensor, nc.scalar, nc.vector, nc.gpsimd, nc.sync]

    def op(self, eng, fn, reads=(), writes=(), dma=False):
        idx = len(self.ops)
        deps = set()
        raw = set()
        for b in reads:
            if b.w is not None:
                deps.add(b.w)
                raw.add(b.w)
        for b in writes:
            if b.w is not None:
                deps.add(b.w)
            for r in b.r:
                deps.add(r)
        self.ops.append(dict(eng=eng, fn=fn, deps=deps, raw=raw, dma=dma, sig=False, dbuf=None))
        for b in reads:
            if not dma:
                b.r = [r for r in b.r if self.ops[r]["dma"] or self.ops[r]["eng"] != eng]
            b.r.append(idx)
        for b in writes:
            b.w = idx
            b.r = []
        return idx

    def barrier(self):
        n = len(self.ops)
        lb = getattr(self, "_lb", 0)
        deps = set()
        last = {}
        for j in range(lb, n):
            o = self.ops[j]
            if o["dma"]:
                deps.add(j)
            else:
                last[o["eng"]] = j
        deps.update(last.values())
        for e in range(5):
            eng = self.engs[e]
            self.ops.append(dict(eng=e, fn=(lambda eng=eng: eng.nop()), deps=set(deps), raw=set(), dma=False, sig=False, dbuf=None))
        self._lb = n

    def dma(self, fn, sbuf_side, reads=(), writes=(), eng=SP):
        idx = self.op(eng, fn, reads, writes, dma=True)
        self.ops[idx]["dbuf"] = sbuf_side
        return idx

    def emit(self):
        nc = self.nc
        ops = self.ops
        for i, o in enumerate(ops):
            for d in o["deps"]:
                p = ops[d]
                if p["dma"] or p["eng"] != o["eng"] or o["dma"] or o["eng"] != PE:
                    p["sig"] = True
        esem = [nc.alloc_semaphore("es%d" % i) for i in range(5)]
        ecnt = [0] * 5
        tok = [None] * len(ops)
        waited = [dict() for _ in range(5)]
        final = {}
        for i, o in enumerate(ops):
            e = o["eng"]
            eng = self.engs[e]
            need = {}
            for d in o["deps"]:
                p = ops[d]
                if (not p["dma"]) and p["eng"] == e and not o["dma"] and e == PE:
                    continue
                if (not p["dma"]) and p["eng"] == e and o["dma"] and e == SP:
                    continue
                s, v = tok[d]
                k = id(s)
                if k not in need or need[k][1] < v:
                    need[k] = (s, v)
            for k, (s, v) in need.items():
                if waited[e].get(k, 0) >= v:
                    continue
                eng.wait_ge(s, v)
                waited[e][k] = v
            ins = o["fn"]()
            if o["dma"]:
                b = o["dbuf"]
                if b.dsem is None or b.dcnt >= 800:
                    b.dcnt = 0
                    self._nsem = getattr(self, "_nsem", 0) + 1
                    b.dsem = nc.alloc_semaphore("d%d_%s" % (self._nsem, b.name))
                b.dcnt += 16
                ins.then_inc(b.dsem, 16)
                tok[i] = (b.dsem, b.dcnt)
                final[id(b.dsem)] = (b.dsem, b.dcnt)
            elif o["sig"]:
                if ecnt[e] >= 3000:
                    self._nsem = getattr(self, "_nsem", 0) + 1
                    esem[e] = nc.alloc_semaphore("es%d_%d" % (e, self._nsem))
                    ecnt[e] = 0
                ecnt[e] += 1
                ins.then_inc(esem[e], 1)
                tok[i] = (esem[e], ecnt[e])
            o["fn"] = None
        for k, (s, v) in final.items():
            if waited[SP].get(k, 0) < v:
                nc.sync.wait_ge(s, v)


class K:
    def __init__(self, nc, rec):
        self.nc = nc
        self.rec = rec
        self.nb = 0

    def sb(self, name, shape, dt):
        t = self.nc.alloc_sbuf_tensor(name, list(shape), dt).ap()
        return t

    def ps(self, name, shape, dt=F32):
        return self.nc.alloc_psum_tensor(name, list(shape), dt).ap()

    def buf(self, name=None):
        self.nb += 1
        return Buf(name or ("b%d" % self.nb))


def bcast_rows(ap_1d_dram, n):
    return ap_1d_dram.partition_broadcast(128)


def build(nc, mode="full"):
    rec = Rec(nc)
    k = K(nc, rec)
    dr = lambda name, shape, dt=F32, kind="ExternalInput": nc.dram_tensor(name, list(shape), dt, kind=kind).ap()
    x_seq = dr("x_seq", [NSEG * SEG, D])
    w_in1 = dr("w_in_odd", [D, 6144])
    w_out1 = dr("w_out_odd", [2048, D])
    norm_odd = dr("norm_odd", [D])
    sgu_gain = dr("sgu_norm_gain", [2048])
    sgu_w = dr("sgu_w_spatial", [4, 128, 128])
    sgu_b = dr("sgu_b_spatial", [4, 128])
    final_norm = dr("final_norm", [D])
    tril = dr("c_tril", [128, 128])
    ident_d = dr("c_ident", [128, 128])
    out = dr("out", [SEG, D], F32, kind="ExternalOutput")
    x1 = dr("x1_scratch", [SEG, D], F32, kind="Internal")
    x2 = dr("x2_scratch", [SEG, D], F32, kind="Internal")

    nct = nc
    R = rec

    ident_f = k.sb("ident_f", [128, 128], F32)
    ident_b = k.sb("ident_b", [128, 128], BF16)
    b_ident = k.buf("ident")
    R.dma(lambda: nc.sync.dma_start(out=ident_f, in_=ident_d), b_ident, writes=[b_ident])
    R.op(DVE, lambda: nc.vector.tensor_copy(out=ident_b, in_=ident_f), reads=[b_ident], writes=[b_ident])

    ARENA = 53248
    arena = k.sb("arena", [128, ARENA], BF16)
    apos = [0]

    def carve(shape, dt):
        n = 1
        for d_ in shape[1:]:
            n *= d_
        nb16 = n * (2 if dt == F32 else 1)
        a = apos[0]
        apos[0] += nb16
        assert apos[0] <= ARENA, apos[0]
        v = arena[:, a:a + nb16]
        if dt == F32:
            v = v.bitcast(F32)
        if len(shape) == 3:
            v = v.rearrange("p (a b) -> p a b", b=shape[2])
        return v

    def carve_at(off, shape, dt):
        n = 1
        for d_ in shape[1:]:
            n *= d_
        nb16 = n * (2 if dt == F32 else 1)
        assert off + nb16 <= ARENA, (off, nb16)
        v = arena[:, off:off + nb16]
        if dt == F32:
            v = v.bitcast(F32)
        if len(shape) == 3:
            v = v.rearrange("p (a b) -> p a b", b=shape[2])
        elif len(shape) == 4:
            v = v.rearrange("p (a b c) -> p a b c", b=shape[2], c=shape[3])
        return v

    xnT = k.sb("xnT", [128, 8, SEG], BF16)
    b_xnT = k.buf("xnT")
    pg = k.ps("pg", [128, 4, 512], F32)
    b_pg = [k.buf("pg%d" % i) for i in range(4)]
    yT = k.sb("yT", [128, 8, SEG], BF16)
    b_yT = k.buf("yT")

    xc = [k.sb("xc%d" % i, [128, D], F32) for i in range(2)]
    b_xc = [k.buf("xc%d" % i) for i in range(2)]
    xb = [k.sb("xb0", [128, D], BF16)] * 2
    b_xb = [k.buf("xb0")] * 2
    t1 = k.sb("t1", [128, 1024], F32)
    b_t1 = k.buf("t1")
    junk = t1[:, :D]
    b_junk = b_t1
    stat = [k.sb("stat%d" % i, [128, 8], F32) for i in range(2)]
    b_stat = [k.buf("stat%d" % i) for i in range(2)]
    pT = [k.ps("pT0", [128, 8, 128], BF16)] * 2
    pT2 = k.ps("pT2", [128, 8, 128], BF16)
    b_pT2 = k.buf("pT2")
    b_pT = [k.buf("pT0")] * 2
    b_ptq_fix = [b_pT[0], b_pT2]

    def norm_transpose(src_dram, row0):
        for c in range(NCH):
            i = c % 2
            rows = src_dram[row0 + c * 128: row0 + (c + 1) * 128, :]
            skey = (id(src_dram), c)
            R.dma(lambda i=i, rows=rows: nc.sync.dma_start(out=xc[i], in_=rows), b_xc[i], reads=[b_x[skey]] if (row0 == 0 and skey in b_x) else [], writes=[b_xc[i]])
            R.op(ACT, lambda i=i: nc.scalar.activation(out=junk, in_=xc[i], func=AF.Square, accum_out=stat[i][:, 0:1]),
                 reads=[b_xc[i]], writes=[b_junk, b_stat[i]])
            R.op(DVE, lambda i=i: nc.vector.tensor_scalar(out=stat[i][:, 1:2], in0=stat[i][:, 0:1], scalar1=1.0 / D, scalar2=EPS,
                                                          op0=ALU.mult, op1=ALU.add), reads=[b_stat[i]], writes=[b_stat[i]])
            R.op(ACT, lambda i=i: nc.scalar.activation(out=stat[i][:, 3:4], in_=stat[i][:, 1:2], func=AF.Sqrt), reads=[b_stat[i]], writes=[b_stat[i]])
            R.op(DVE, lambda i=i: nc.vector.reciprocal(out=stat[i][:, 2:3], in_=stat[i][:, 3:4]), reads=[b_stat[i]], writes=[b_stat[i]])
            R.op(DVE, lambda i=i: nc.vector.scalar_tensor_tensor(out=xb[i], in0=xc[i], scalar=stat[i][:, 2:3], in1=gbc, op0=ALU.mult, op1=ALU.mult),
                 reads=[b_xc[i], b_stat[i], b_gbc], writes=[b_xb[i]])
            for kt in range(8):
                R.op(PE, lambda i=i, kt=kt: nc.tensor.transpose(out=pT[i][:, kt, :], in_=xb[i][:, kt * 128:(kt + 1) * 128], identity=ident_b),
                     reads=[b_xb[i], b_ident], writes=[b_pT[i]])
            R.op(DVE, lambda i=i, c=c: nc.vector.tensor_copy(out=xnT[:, :, c * 128:(c + 1) * 128], in_=pT[i]),
                 reads=[b_pT[i]], writes=[b_xnT])

    wbf = [k.sb("wbf%d" % i, [128, 8, 256], BF16) for i in range(3)]
    b_wbf = [k.buf("wbf%d" % i) for i in range(3)]
    gbc = k.sb("gbc", [128, D], F32)
    b_gbc = k.buf("gbc")
    wctr = [0]

    def load_gain(g_dram):
        R.dma(lambda: nc.sync.dma_start(out=gbc, in_=g_dram.partition_broadcast(128)), b_gbc, writes=[b_gbc])

    def load_wblock(w_dram, c0, ncols=256, gain=True):
        i = wctr[0] % 3
        wctr[0] += 1
        src = w_dram.rearrange("(kt p) n -> p kt n", p=128)[:, :, c0:c0 + ncols]
        R.dma(lambda: nc.gpsimd.dma_start(out=wbf[i][:, :, :ncols], in_=src), b_wbf[i], writes=[b_wbf[i]], eng=POOL)
        return wbf[i], b_wbf[i]

    pacc = [k.ps("pacc%d" % i, [128, 512], F32) for i in range(2)]
    b_pacc = [k.buf("pacc%d" % i) for i in range(2)]
    pctr = [0]

    wide = [True]

    def next_pacc():
        if wide[0]:
            i = pctr[0] % 6
            pctr[0] += 1
            if i < 2:
                return pacc[i], b_pacc[i]
            return pg[:, i - 2, :], b_pg[i - 2]
        i = pctr[0] % 2
        pctr[0] += 1
        return pacc[i], b_pacc[i]

    def layer1(src_dram, dst_dram):
        load_gain(norm_odd)
        norm_transpose(src_dram, 0)
        R.barrier()
        apos[0] = 0
        vgain = carve([128, 2048], F32)
        b_vgain = k.buf("vgain")
        R.dma(lambda: nc.sync.dma_start(out=vgain, in_=sgu_gain.partition_broadcast(128)), b_vgain, writes=[b_vgain])
        bsp = k.sb("bsp", [128, 4, 128], F32)
        b_bsp = k.buf("bsp")
        R.dma(lambda: nc.sync.dma_start(out=bsp, in_=sgu_b.partition_broadcast(128)), b_bsp, writes=[b_bsp])
        wraw = k.sb("wraw", [128, 4, 128], F32)
        b_wraw = k.buf("wraw")
        R.dma(lambda: nc.sync.dma_start(out=wraw, in_=sgu_w.rearrange("g t s -> t g s")), b_wraw, writes=[b_wraw])
        trl = k.sb("trl", [128, 128], F32)
        b_trl = k.buf("trl")
        R.dma(lambda: nc.sync.dma_start(out=trl, in_=tril), b_trl, writes=[b_trl])
        wmT = k.sb("wmT", [128, 4, 128], BF16)
        b_wmT = k.buf("wmT")
        ptw = pacc[0].rearrange("p (g t) -> p g t", t=128)
        b_ptw = b_pacc[0]
        for g in range(4):
            R.op(PE, lambda g=g: nc.tensor.transpose(out=ptw[:, g, :], in_=wraw[:, g, :], identity=ident_f),
                 reads=[b_wraw, b_ident], writes=[b_ptw])
        R.op(DVE, lambda: nc.vector.tensor_tensor(out=wmT, in0=ptw, in1=trl.unsqueeze(1).to_broadcast([128, 4, 128]), op=ALU.mult),
             reads=[b_ptw, b_trl], writes=[b_wmT])

        vn = carve([128, NCH, 2048], BF16)
        b_vn = k.buf("vn")
        vf = carve([128, 2048], F32)
        b_vf = k.buf("vf")
        bst = k.sb("bst", [128, 4, 6], F32)
        mv = k.sb("mv", [128, 4], F32)
        b_bst = k.buf("bst")

        def gelu_from(psrc, b_psrc, dst, b_dst, n):
            R.op(ACT, lambda: nc.scalar.activation(out=dst, in_=psrc, func=AF.Gelu_apprx_tanh), reads=[b_psrc], writes=[b_dst])

        uT1 = carve([128, SEG], BF16)
        b_uT1 = k.buf("uT1")
        zT1 = carve([128, SEG], BF16)
        b_zT1 = k.buf("zT1")
        vf2 = arena[:, apos[0] - 4096:apos[0]].bitcast(F32)
        vfs = [vf, vf2]
        b_vfs = [[b_vf], [b_uT1, b_zT1]]
        for blk in range(8):
            src = w_in1.rearrange("(kt p) n -> p kt n", p=128)[:, :, 2048 + blk * 256:2048 + (blk + 1) * 256]
            R.dma(lambda blk=blk, src=src: nc.gpsimd.dma_start(out=yT[:, :, blk * 256:(blk + 1) * 256], in_=src), b_yT, writes=[b_yT], eng=POOL)
        for c in range(NCH):
            vfc, bvf = vfs[c % 2], b_vfs[c % 2]
            for blk in range(8):
                pa, b_pa = next_pacc()
                for kt in range(8):
                    R.op(PE, lambda pa=pa, kt=kt, c=c, blk=blk: nc.tensor.matmul(pa[:, :256], lhsT=xnT[:, kt, c * 128:(c + 1) * 128], rhs=yT[:, kt, blk * 256:(blk + 1) * 256],
                                                                                 start=(kt == 0), stop=(kt == 7)),
                         reads=[b_xnT, b_yT], writes=[b_pa])
                R.op(ACT, lambda pa=pa, vfc=vfc, blk=blk: nc.scalar.activation(out=vfc[:, blk * 256:(blk + 1) * 256], in_=pa[:, :256], func=AF.Gelu_apprx_tanh), reads=[b_pa], writes=bvf)
            for j in range(4):
                R.op(DVE, lambda j=j, vfc=vfc: nc.vector.bn_stats(out=bst[:, j, :], in_=vfc[:, j * 512:(j + 1) * 512]), reads=bvf, writes=[b_bst])
            R.op(DVE, lambda: nc.vector.bn_aggr(out=mv[:, 0:2], in_=bst), reads=[b_bst], writes=[b_bst])
            R.op(DVE, lambda: nc.vector.tensor_scalar(out=mv[:, 3:4], in0=mv[:, 1:2], scalar1=EPS, scalar2=None, op0=ALU.add),
                 reads=[b_bst], writes=[b_bst])
            R.op(ACT, lambda: nc.scalar.activation(out=mv[:, 3:4], in_=mv[:, 3:4], func=AF.Sqrt), reads=[b_bst], writes=[b_bst])
            R.op(DVE, lambda: nc.vector.reciprocal(out=mv[:, 2:3], in_=mv[:, 3:4]), reads=[b_bst], writes=[b_bst])
            R.op(DVE, lambda vfc=vfc: nc.vector.tensor_scalar(out=vfc, in0=vfc, scalar1=mv[:, 0:1], scalar2=mv[:, 2:3], op0=ALU.subtract, op1=ALU.mult),
                 reads=bvf + [b_bst], writes=bvf)
            R.op(DVE, lambda c=c, vfc=vfc: nc.vector.tensor_tensor(out=vn[:, c, :], in0=vfc, in1=vgain, op=ALU.mult), reads=bvf + [b_vgain], writes=[b_vn])

        psT = pg
        b_psT = k.buf("psT")
        wo = carve([128, 8, D], BF16)
        b_wo = k.buf("wo")
        for half in range(2):
            for jt in range(8):
                j = half * 8 + jt
                g = j // 4
                if jt % 2 == 0:
                    wu, b_wu = load_wblock(w_in1, j * 128)
                    wz, b_wz = load_wblock(w_in1, 4096 + j * 128)
                    off = 0
                else:
                    off = 128
                for nb in range(4):
                    pa, b_pa = next_pacc()
                    for kt in range(8):
                        R.op(PE, lambda pa=pa, wu=wu, kt=kt, nb=nb, off=off: nc.tensor.matmul(pa, lhsT=wu[:, kt, off:off + 128], rhs=xnT[:, kt, nb * 512:(nb + 1) * 512],
                                                                                              start=(kt == 0), stop=(kt == 7)),
                             reads=[b_xnT, b_wu], writes=[b_pa])
                    gelu_from(pa, b_pa, uT1[:, nb * 512:(nb + 1) * 512], b_uT1, 512)
                    pz, b_pz = next_pacc()
                    for kt in range(8):
                        R.op(PE, lambda pz=pz, wz=wz, kt=kt, nb=nb, off=off: nc.tensor.matmul(pz, lhsT=wz[:, kt, off:off + 128], rhs=xnT[:, kt, nb * 512:(nb + 1) * 512],
                                                                                              start=(kt == 0), stop=(kt == 7)),
                             reads=[b_xnT, b_wz], writes=[b_pz])
                    R.op(ACT, lambda pz=pz, nb=nb: nc.scalar.activation(out=zT1[:, nb * 512:(nb + 1) * 512], in_=pz, func=AF.Silu),
                         reads=[b_pz], writes=[b_zT1])
                for c in range(NCH):
                    R.op(PE, lambda c=c, j=j, g=g: nc.tensor.matmul(psT[:, c // 4, (c % 4) * 128:(c % 4 + 1) * 128], lhsT=vn[:, c, j * 128:(j + 1) * 128],
                                                                     rhs=wmT[:, g, :], start=True, stop=True),
                         reads=[b_vn, b_wmT], writes=[b_pg[c // 4]])
                R.op(DVE, lambda: nc.vector.tensor_tensor(out=uT1, in0=uT1, in1=zT1, op=ALU.mult), reads=[b_uT1, b_zT1], writes=[b_uT1])
                R.op(DVE, lambda g=g: nc.vector.tensor_tensor(out=zT1.rearrange("p (c t) -> p c t", t=128), in0=psT.rearrange("p a (b t) -> p (a b) t", t=128),
                                                              in1=bsp[:, g:g + 1, :].to_broadcast([128, NCH, 128]), op=ALU.add),
                     reads=b_pg + [b_bsp], writes=[b_zT1])
                R.op(DVE, lambda jt=jt: nc.vector.tensor_tensor(out=yT[:, jt, :], in0=uT1, in1=zT1, op=ALU.mult), reads=[b_uT1, b_zT1], writes=[b_yT])
            out_proj_half(w_out1, half * 1024, src_dram if half == 0 else dst_dram, dst_dram, wo, b_wo)

    b_x = {}

    def out_proj_half(w_dram, row0, srcd, dstd, wo, b_wo):
        for ct2 in range(2):
            r0 = row0 + ct2 * 512
            R.dma(lambda r0=r0, ct2=ct2: nc.gpsimd.dma_start(out=wo[:, 4 * ct2:4 * ct2 + 4, :], in_=w_dram[r0:r0 + 512, :].rearrange("(c p) n -> p c n", p=128)), b_wo, writes=[b_wo], eng=POOL)
        for c in range(NCH):
            i = c % 2
            skey = (id(srcd), c)
            R.dma(lambda i=i, c=c: nc.sync.dma_start(out=xc[i], in_=srcd[c * 128:(c + 1) * 128, :]), b_xc[i],
                  reads=[b_x[skey]] if skey in b_x else [], writes=[b_xc[i]])
            for nb in range(2):
                pa, b_pa = next_pacc()
                for ct in range(8):
                    R.op(PE, lambda pa=pa, ct=ct, c=c, nb=nb: nc.tensor.matmul(pa, lhsT=yT[:, ct, c * 128:(c + 1) * 128], rhs=wo[:, ct, nb * 512:(nb + 1) * 512],
                                                                               start=(ct == 0), stop=(ct == 7)),
                         reads=[b_yT, b_wo], writes=[b_pa])
                R.op(DVE, lambda pa=pa, i=i, nb=nb: nc.vector.tensor_tensor(out=xc[i][:, nb * 512:(nb + 1) * 512], in0=xc[i][:, nb * 512:(nb + 1) * 512], in1=pa, op=ALU.add),
                     reads=[b_pa, b_xc[i]], writes=[b_xc[i]])
            key = (id(dstd), c)
            if key not in b_x:
                b_x[key] = k.buf("xd")
            R.dma(lambda i=i, c=c: nc.gpsimd.dma_start(out=dstd[c * 128:(c + 1) * 128, :], in_=xc[i]), b_xc[i],
                  reads=[b_xc[i]], writes=[b_x[key]], eng=POOL)

    norm_even = dr("norm_even", [D])
    w_in0 = dr("w_in_even", [D, 6144])
    w_out0 = dr("w_out_even", [2048, D])
    lam_re_d = dr("s5_lam_re", [64, 64])
    lam_im_d = dr("s5_lam_im", [64, 64])
    log_dt_d = dr("s5_log_dt", [64])
    b_re_d = dr("s5_b_re", [64, 64, 16])
    b_im_d = dr("s5_b_im", [64, 64, 16])
    c_re_d = dr("s5_c_re", [64, 16, 64])
    c_im_d = dr("s5_c_im", [64, 16, 64])
    s5_d_d = dr("s5_d", [1024])
    w_glu_d = dr("s5_w_glu", [1024, 1024])
    b_glu_d = dr("s5_b_glu", [1024])
    gn_gain_d = dr("ret_gn_gain", [1024])
    rot_d = dr("c_rot", [NSEG, 2, 128, SEG])
    dt_d = dr("c_dt", [128, 4, 128])
    xizs_d = dr("c_xizs", [128, 8])
    zsc_d = dr("c_zsc", [128, 64])
    tabB = dr("tabB", [8, 16, 128, 512], BF16, kind="Internal")
    tabCL = dr("tabCL", [8, 16, 128, 512], BF16, kind="Internal")
    tabK = dr("tabK", [8, 128, 1024], BF16, kind="Internal")
    b_tabK = k.buf("tabK_d")
    b_ptq = [None, None]
    b_tabB = k.buf("tabB_d")
    b_tabC = k.buf("tabC_d")

    OFF_UT, OFF_X, OFF_XB, OFF_E, OFF_PW, OFF_TBC, OFF_SRET, OFF_SBF, OFF_QK = 0, 16384, 24576, 28672, 31744, 34816, 38912, 43008, 45056
    uT = carve_at(OFF_UT, [128, 8, SEG], BF16)
    b_uT = k.buf("uT")
    Xre = carve_at(OFF_X, [128, SEG], F32)
    Xim = carve_at(OFF_X + 4096, [128, SEG], F32)
    b_X = k.buf("X")
    Xbre = carve_at(OFF_XB, [128, SEG], BF16)
    Xbim = carve_at(OFF_XB + 2048, [128, SEG], BF16)
    b_Xb = k.buf("Xb")
    EA = [carve_at(OFF_E + 768 * j, [128, 384], F32) for j in range(2)]
    EB = [carve_at(OFF_E + 768 * (2 + j), [128, 384], F32) for j in range(2)]
    b_E = k.buf("E")
    pw = carve_at(OFF_PW, [128, 32, 16, 3], F32)
    b_pw = k.buf("pw")
    tB = [carve_at(OFF_TBC + 1024 * j, [128, 2, 512], BF16) for j in range(2)]
    b_tB = [k.buf("tB%d" % j) for j in range(2)]
    tC = [carve_at(OFF_TBC + 2048 + 1024 * j, [128, 2, 512], BF16) for j in range(2)]
    b_tC = [k.buf("tC%d" % j) for j in range(2)]
    Sret = carve_at(OFF_SRET, [128, 4, 512], F32)
    b_Sret = k.buf("Sret")
    Sbf = carve_at(OFF_SBF, [128, 4, 512], BF16)
    b_Sbf = k.buf("Sbf")
    qrT = carve_at(OFF_QK, [128, 2, SEG], BF16)
    krT = carve_at(OFF_QK + 4096, [128, 2, SEG], BF16)
    b_qrT = k.buf("qrT")
    b_krT = k.buf("krT")
    Ss5 = k.sb("Ss5", [128, 32, 2], F32)
    b_Ss5 = k.buf("Ss5")
    dcol = k.sb("dcol", [128, 16], F32)
    b_dcol = k.buf("dcol")

    def gelu_sb(src, b_src, dst, b_dst, scr, b_scr):
        R.op(ACT, lambda: nc.scalar.activation(out=scr, in_=src, func=AF.Square), reads=[b_src], writes=[b_scr])
        R.op(DVE, lambda: nc.vector.tensor_scalar(out=scr, in0=scr, scalar1=0.044715 * 1.5957691216, scalar2=1.5957691216,
                                                  op0=ALU.mult, op1=ALU.add), reads=[b_scr], writes=[b_scr])
        R.op(DVE, lambda: nc.vector.tensor_tensor(out=scr, in0=scr, in1=src, op=ALU.mult), reads=[b_scr, b_src], writes=[b_scr])
        R.op(ACT, lambda: nc.scalar.activation(out=scr, in_=scr, func=AF.Sigmoid), reads=[b_scr], writes=[b_scr])
        R.op(DVE, lambda: nc.vector.tensor_tensor(out=dst, in0=scr, in1=src, op=ALU.mult), reads=[b_scr, b_src], writes=[b_dst])

    def s5_precompute():
        R.barrier()
        bp = k.buf("pre")
        pos = [OFF_UT]

        def tmp(shape, dt=F32):
            n = 1
            for d_ in shape[1:]:
                n *= d_
            n16 = n * (2 if dt == F32 else 1)
            v = carve_at(pos[0], shape, dt)
            pos[0] += n16
            assert pos[0] <= OFF_PW, pos[0]
            return v

        def VT(out, a, b, op):
            R.op(DVE, lambda: nc.vector.tensor_tensor(out=out, in0=a, in1=b, op=op), reads=[bp], writes=[bp])

        def VS(out, a, s1, op0, s2=None, op1=None):
            if op1 is None:
                R.op(DVE, lambda: nc.vector.tensor_scalar(out=out, in0=a, scalar1=s1, scalar2=None, op0=op0), reads=[bp], writes=[bp])
            else:
                R.op(DVE, lambda: nc.vector.tensor_scalar(out=out, in0=a, scalar1=s1, scalar2=s2, op0=op0, op1=op1), reads=[bp], writes=[bp])

        def AC(out, a, func):
            R.op(ACT, lambda: nc.scalar.activation(out=out, in_=a, func=func), reads=[bp], writes=[bp])

        def LD(out, src):
            R.dma(lambda: nc.sync.dma_start(out=out, in_=src, allow_slow_non_contiguous=True), bp, writes=[bp])

        S = [128, 32]
        lr, li, ldt, dtv, x1, mag, ang, r, sn, cs, are, aim, den, nre, zre, zim, ta, tb = [tmp(S) for _ in range(18)]
        LD(lr, lam_re_d.rearrange("(i gg) p -> (gg p) i", gg=2))
        LD(li, lam_im_d.rearrange("(i gg) p -> (gg p) i", gg=2))
        for gg in range(2):
            LD(ldt[gg * 64:(gg + 1) * 64, :], log_dt_d.rearrange("(i gg) -> gg i", gg=2)[gg].partition_broadcast(64))
        VS(lr, lr, -1e-4, ALU.min)
        AC(dtv, ldt, AF.Exp)
        VT(x1, lr, dtv, ALU.mult)
        AC(mag, x1, AF.Exp)
        VT(ang, li, dtv, ALU.mult)
        MAGIC = 12582912.0

        def reduce_sin(dst, a_in):
            VS(r, a_in, 1.0 / (2 * math.pi), ALU.mult)
            VS(ta, r, MAGIC, ALU.add)
            VS(ta, ta, -MAGIC, ALU.add)
            VS(tb, ta, -2 * math.pi, ALU.mult)
            VT(r, a_in, tb, ALU.add)
            VS(r, r, 3.14159, ALU.min, -3.14159, ALU.max)
            AC(dst, r, AF.Sin)

        reduce_sin(sn, ang)
        VS(x1, ang, 0.5 * math.pi, ALU.add)
        reduce_sin(cs, x1)
        VT(are, mag, cs, ALU.mult)
        VT(aim, mag, sn, ALU.mult)
        VT(den, lr, lr, ALU.mult)
        VT(ta, li, li, ALU.mult)
        VT(den, den, ta, ALU.add)
        R.op(DVE, lambda: nc.vector.reciprocal(out=den, in_=den), reads=[bp], writes=[bp])
        VS(nre, are, -1.0, ALU.add)
        VT(ta, nre, lr, ALU.mult)
        VT(tb, aim, li, ALU.mult)
        VT(ta, ta, tb, ALU.add)
        VT(zre, ta, den, ALU.mult)
        VT(ta, aim, lr, ALU.mult)
        VT(tb, nre, li, ALU.mult)
        VT(ta, ta, tb, ALU.subtract)
        VT(zim, ta, den, ALU.mult)
        def P(kk, j):
            return pw[:, :, kk, j]

        def setp(kk, re_ap, im_ap):
            R.op(DVE, lambda: nc.vector.tensor_copy(out=P(kk, 0), in_=re_ap), reads=[bp], writes=[bp, b_pw])
            R.op(DVE, lambda: nc.vector.tensor_copy(out=P(kk, 1), in_=im_ap), reads=[bp], writes=[bp, b_pw])
            R.op(DVE, lambda: nc.vector.tensor_scalar(out=P(kk, 2), in0=im_ap, scalar1=-1.0, scalar2=None, op0=ALU.mult), reads=[bp], writes=[bp, b_pw])

        def cmul(ore, oim, a_re, a_im, b_re_, b_im_):
            VT(ta, a_re, b_re_, ALU.mult)
            VT(tb, a_im, b_im_, ALU.mult)
            VT(ore, ta, tb, ALU.subtract)
            VT(ta, a_re, b_im_, ALU.mult)
            VT(tb, a_im, b_re_, ALU.mult)
            VT(oim, ta, tb, ALU.add)

        cr, ci, nr, ni = [tmp(S) for _ in range(4)]
        setp(0, are, aim)
        R.op(DVE, lambda: nc.vector.tensor_copy(out=cr, in_=are), reads=[bp], writes=[bp])
        R.op(DVE, lambda: nc.vector.tensor_copy(out=ci, in_=aim), reads=[bp], writes=[bp])
        for kk in range(1, 8):
            cmul(nr, ni, cr, ci, are, aim)
            R.op(DVE, lambda: nc.vector.tensor_copy(out=cr, in_=nr), reads=[bp], writes=[bp])
            R.op(DVE, lambda: nc.vector.tensor_copy(out=ci, in_=ni), reads=[bp], writes=[bp])
            setp(kk, cr, ci)
        setp(8, cr, ci)
        for m in range(1, 8):
            cmul(nr, ni, cr, ci, cr, ci)
            R.op(DVE, lambda: nc.vector.tensor_copy(out=cr, in_=nr), reads=[bp], writes=[bp])
            R.op(DVE, lambda: nc.vector.tensor_copy(out=ci, in_=ni), reads=[bp], writes=[bp])
            setp(8 + m, cr, ci)
        S3 = [128, 32, 16]
        bre, bim, Bre, Bim, t3a, t3b = [tmp(S3) for _ in range(6)]
        LD(bre, b_re_d.rearrange("(i gg) p h -> (gg p) i h", gg=2))
        LD(bim, b_im_d.rearrange("(i gg) p h -> (gg p) i h", gg=2))
        zre3 = zre.unsqueeze(2).to_broadcast(S3)
        zim3 = zim.unsqueeze(2).to_broadcast(S3)
        VT(t3a, bre, zre3, ALU.mult)
        VT(t3b, bim, zim3, ALU.mult)
        VT(Bre, t3a, t3b, ALU.subtract)
        VT(t3a, bim, zre3, ALU.mult)
        VT(t3b, bre, zim3, ALU.mult)
        VT(Bim, t3a, t3b, ALU.add)
        Blre, Blim = tmp(S3), tmp(S3)
        Wb = [tmp([128, 32, 128], BF16) for _ in range(2)]
        b_wp = [k.buf("wpad%d" % j) for j in range(2)]
        stages = [tmp([128, 8, 128], BF16) for _ in range(2)]
        b_stage = [k.buf("stg%d" % j) for j in range(2)]
        Cpad0 = tmp([128, 8, 2, 512], BF16)
        b_c0 = k.buf("cpad0")
        cin = [tmp([128, 128]) for _ in range(2)]
        b_cin = [k.buf("cin%d" % j) for j in range(2)]
        b_trs = k.buf("trs")
        Kst4 = tmp([128, 512], BF16)
        ptmp = tmp([128, 16])
        b_kst = [k.buf("kst%d" % j) for j in range(2)]
        for j in range(2):
            R.op(DVE, lambda j=j: nc.vector.memset(Wb[j], 0.0), reads=[], writes=[b_wp[j]])
        R.op(DVE, lambda: nc.vector.memset(Cpad0, 0.0), reads=[], writes=[b_c0])
        TrsAll = [tmp([128, 8, 128], BF16) for _ in range(2)]
        for t in range(8):
            for ri, cd in enumerate((c_re_d, c_im_d)):
                src = cd.rearrange("(t gl) h p -> t (gl h) p", gl=8)[t]
                R.dma(lambda ri=ri, src=src: nc.sync.dma_start(out=cin[ri][:, 0:64], in_=src), b_cin[ri], writes=[b_cin[ri]])
                R.dma(lambda ri=ri, src=src: nc.sync.dma_start(out=cin[ri][:, 64:128], in_=src), b_cin[ri], writes=[b_cin[ri]])
                ptc = pacc[ri][:, 0:128]
                R.op(PE, lambda ri=ri, ptc=ptc: nc.tensor.transpose(out=ptc, in_=cin[ri], identity=ident_f), reads=[b_cin[ri], b_ident], writes=[b_pacc[ri]])
                R.op(ACT, lambda ri=ri, ptc=ptc, t=t: nc.scalar.copy(out=TrsAll[ri][:, t, :], in_=ptc), reads=[b_pacc[ri]], writes=[b_trs])
        for kk in range(4):
            for gg in range(2):
                rs = slice(gg * 64, (gg + 1) * 64)
                c0 = 32 * kk + 16 * gg
                R.op(DVE, lambda kk=kk, rs=rs, c0=c0: nc.vector.tensor_copy(out=Cpad0[rs, :, 0, kk * 128 + c0:kk * 128 + c0 + 16], in_=TrsAll[0][rs, :, c0:c0 + 16]), reads=[b_trs], writes=[b_c0])
                R.op(DVE, lambda kk=kk, rs=rs, c0=c0: nc.vector.tensor_scalar(out=Cpad0[rs, :, 1, kk * 128 + c0:kk * 128 + c0 + 16], in0=TrsAll[1][rs, :, c0:c0 + 16], scalar1=-1.0, scalar2=None, op0=ALU.mult),
                     reads=[b_trs], writes=[b_c0])
        nst = [0]
        for lag in range(8):
            if lag == 0:
                srcs = (Bre, Bim)
            else:
                ar3 = pw[:, :, lag - 1, 0].unsqueeze(2).to_broadcast(S3)
                ai3 = pw[:, :, lag - 1, 1].unsqueeze(2).to_broadcast(S3)
                VT(t3a, Bre, ar3, ALU.mult)
                VT(t3b, Bim, ai3, ALU.mult)
                VT(Blre, t3a, t3b, ALU.subtract)
                VT(t3a, Bim, ar3, ALU.mult)
                VT(t3b, Bre, ai3, ALU.mult)
                VT(Blim, t3a, t3b, ALU.add)
                srcs = (Blre, Blim)
            for ri, Bt in enumerate(srcs):
                for kk in range(4):
                    R.op(DVE, lambda Bt=Bt, kk=kk, ri=ri: nc.vector.tensor_copy(out=Wb[ri][0:64, kk::4, 32 * kk:32 * kk + 16], in_=Bt[0:64, kk::4, :]), reads=[bp, b_wp[ri]], writes=[b_wp[ri]])
                    R.op(DVE, lambda Bt=Bt, kk=kk, ri=ri: nc.vector.tensor_copy(out=Wb[ri][64:128, kk::4, 32 * kk + 16:32 * kk + 32], in_=Bt[64:128, kk::4, :]), reads=[bp, b_wp[ri]], writes=[b_wp[ri]])
                for t2 in range(4):
                    j = nst[0] % 2
                    nst[0] += 1
                    ptr = (pT[0] if j == 0 else pT2)
                    for kk in range(8):
                        R.op(PE, lambda kk=kk, t2=t2, ptr=ptr, ri=ri: nc.tensor.transpose(out=ptr[:, kk, :], in_=Wb[ri][:, 8 * t2 + kk, :], identity=ident_b), reads=[b_wp[ri], b_ident], writes=[b_ptq_fix[j]])
                    R.op(ACT, lambda j=j, ptr=ptr: nc.scalar.copy(out=stages[j], in_=ptr), reads=[b_ptq_fix[j]], writes=[b_stage[j]])
                    R.dma(lambda t2=t2, ri=ri, lag=lag, j=j: nc.sync.dma_start(out=tabB[2 * t2:2 * t2 + 2, 2 * lag + ri].rearrange("t p n -> p t n"),
                                                                              in_=stages[j].rearrange("p (t a) b -> p t (a b)", t=2)), b_stage[j], reads=[b_stage[j]], writes=[b_tabB])
            for g4 in range(2):
                j = g4 % 2
                for tq in range(4):
                    t = 4 * g4 + tq
                    pk = pacc[j][:, tq * 128:(tq + 1) * 128]
                    for kk in range(4):
                        for ri in range(2):
                            R.op(PE, lambda pk=pk, kk=kk, ri=ri, t=t: nc.tensor.matmul(pk, lhsT=Wb[ri][:, 4 * t + kk, :], rhs=Cpad0[:, t, ri, kk * 128:(kk + 1) * 128],
                                                                                       start=(kk == 0 and ri == 0), stop=(kk == 3 and ri == 1)), reads=[b_wp[ri], b_c0], writes=[b_pacc[j]])
                R.op(ACT, lambda j=j: nc.scalar.copy(out=Kst4, in_=pacc[j]), reads=[b_pacc[j]], writes=[b_kst[0]])
                R.dma(lambda g4=g4, lag=lag: nc.sync.dma_start(out=tabK[4 * g4:4 * g4 + 4, :, lag * 128:(lag + 1) * 128].rearrange("t p n -> p t n"),
                                                              in_=Kst4.rearrange("p (t n) -> p t n", n=128)), b_kst[0], reads=[b_kst[0]], writes=[b_tabK])
        CpadAll = [Wb[ri].rearrange("p a b -> p (a b)").rearrange("p (t n) -> p t n", n=512) for ri in range(2)]
        tc1, tc2 = tmp([128, 8, 16]), tmp([128, 8, 16])
        for ri in range(2):
            R.op(DVE, lambda ri=ri: nc.vector.memset(Wb[ri], 0.0), reads=[b_wp[ri]], writes=[b_wp[ri]])
        for s_ in range(8):
            for kk in range(4):
                for gg in range(2):
                    rs = slice(gg * 64, (gg + 1) * 64)
                    c0 = 32 * kk + 16 * gg
                    S8 = [64, 8, 16]
                    Ar = pw[rs, kk::4, s_, 0].unsqueeze(2).to_broadcast(S8)
                    Ai = pw[rs, kk::4, s_, 1].unsqueeze(2).to_broadcast(S8)
                    Tr_, Ti_ = TrsAll[0][rs, :, c0:c0 + 16], TrsAll[1][rs, :, c0:c0 + 16]
                    o_re = CpadAll[0][rs, :, kk * 128 + c0:kk * 128 + c0 + 16]
                    o_im = CpadAll[1][rs, :, kk * 128 + c0:kk * 128 + c0 + 16]
                    a_, b_ = tc1[rs], tc2[rs]
                    rd = [b_trs, b_pw, bp]
                    R.op(DVE, lambda a_=a_, Tr_=Tr_, Ar=Ar: nc.vector.tensor_tensor(out=a_, in0=Tr_, in1=Ar, op=ALU.mult), reads=rd, writes=[bp])
                    R.op(DVE, lambda b_=b_, Ti_=Ti_, Ai=Ai: nc.vector.tensor_tensor(out=b_, in0=Ti_, in1=Ai, op=ALU.mult), reads=rd, writes=[bp])
                    R.op(DVE, lambda o_re=o_re, a_=a_, b_=b_: nc.vector.tensor_tensor(out=o_re, in0=a_, in1=b_, op=ALU.subtract), reads=[bp, b_wp[0]], writes=[b_wp[0]])
                    R.op(DVE, lambda a_=a_, Tr_=Tr_, Ai=Ai: nc.vector.tensor_tensor(out=a_, in0=Tr_, in1=Ai, op=ALU.mult), reads=rd, writes=[bp])
                    R.op(DVE, lambda b_=b_, Ti_=Ti_, Ar=Ar: nc.vector.tensor_tensor(out=b_, in0=Ti_, in1=Ar, op=ALU.mult), reads=rd, writes=[bp])
                    R.op(DVE, lambda o_im=o_im, a_=a_, b_=b_: nc.vector.tensor_tensor(out=o_im, in0=a_, in1=b_, op=ALU.add), reads=[bp, b_wp[1]], writes=[b_wp[1]])
            for ri in range(2):
                R.dma(lambda s_=s_, ri=ri: nc.sync.dma_start(out=tabCL[:, 2 * s_ + ri].rearrange("t p n -> p t n"), in_=CpadAll[ri]), b_wp[ri], reads=[b_wp[ri]], writes=[b_tabC])
        R.dma(lambda: nc.sync.dma_start(out=dcol[:, 0:8], in_=s5_d_d.rearrange("(t p) -> p t", p=128), allow_slow_non_contiguous=True), b_dcol, writes=[b_dcol])
        R.dma(lambda: nc.sync.dma_start(out=dcol[:, 8:16], in_=b_glu_d.rearrange("(t p) -> p t", p=128), allow_slow_non_contiguous=True), b_dcol, writes=[b_dcol])
        R.barrier()

    def stt(eng_id, out, in0, scalar, in1, reads, writes):
        e = nc.vector if eng_id == DVE else nc.gpsimd
        R.op(eng_id, lambda: e.scalar_tensor_tensor(out=out, in0=in0, scalar=scalar, in1=in1, op0=ALU.mult, op1=ALU.add), reads=reads, writes=writes)

    tBL = [yT.rearrange("p a b -> p (a b)")[:, j * 8192:(j + 1) * 8192].rearrange("p (l n) -> p l n", n=512) for j in range(2)]
    b_tBL = [k.buf("tBL%d" % j) for j in range(2)]
    tCL = carve_at(OFF_X, [128, 16, 512], BF16)
    b_tCL = k.buf("tCL")
    tK = [carve_at(OFF_TBC + 1024 * j, [128, 8, 128], BF16) for j in range(2)]
    b_tK = [k.buf("tK%d" % j) for j in range(2)]
    carry_b = [carve_at(OFF_XB + 6144, [128, 4, 256], BF16), carve_at(OFF_TBC + 2048, [128, 4, 256], BF16)]
    b_carry = k.buf("carry")
    Epre = [[carve_at(base + 2048 * j, [128, 4, 256], F32) for j in range(4)] for base in (OFF_X, OFF_QK)]
    b_Epre = [k.buf("Epre%d" % j) for j in range(2)]
    Tpre = carve_at(OFF_XB, [128, 4, 128], F32)
    b_Tpre = k.buf("Tpre")
    HA = [carve_at(OFF_QK + 3072 * j, [128, 4, 384], F32) for j in range(2)]
    HB = [carve_at(OFF_XB + 3072 * j, [128, 4, 384], F32) for j in range(2)]
    Ths = carve_at(OFF_QK + 6144, [128, 4, 256], F32)
    b_H = k.buf("H")
    pgv = pg.rearrange("p a b -> p (a b)").rearrange("p (s j) -> p s j", j=256)

    def VTT(out, a, b, op, reads, writes):
        R.op(DVE, lambda: nc.vector.tensor_tensor(out=out, in0=a, in1=b, op=op), reads=reads, writes=writes)

    def pwb(ct, idx, comp, w):
        return pw[:, 4 * ct:4 * ct + 4, idx, comp].unsqueeze(2).to_broadcast([128, 4, w])

    def inject(ct, e_re, e_im, tmp4, reads, writes):
        sre, sim = Ss5[:, 4 * ct:4 * ct + 4, 0], Ss5[:, 4 * ct:4 * ct + 4, 1]
        p8r, p8i = pw[:, 4 * ct:4 * ct + 4, 7, 0], pw[:, 4 * ct:4 * ct + 4, 7, 1]
        rd = reads + [b_pw, b_Ss5]
        VTT(tmp4, sre, p8r, ALU.mult, rd, writes)
        VTT(e_re, e_re, tmp4, ALU.add, rd, writes)
        VTT(tmp4, sim, p8i, ALU.mult, rd, writes)
        VTT(e_re, e_re, tmp4, ALU.subtract, rd, writes)
        VTT(tmp4, sim, p8r, ALU.mult, rd, writes)
        VTT(e_im, e_im, tmp4, ALU.add, rd, writes)
        VTT(tmp4, sre, p8i, ALU.mult, rd, writes)
        VTT(e_im, e_im, tmp4, ALU.add, rd, writes)

    uS3 = t1.bitcast(BF16).rearrange("p (s j) -> p s j", j=256)

    def deinterleave(ct):
        uv_ = uT[:, ct, :].rearrange("p (j s) -> p s j", s=8)
        R.op(POOL, lambda: nc.gpsimd.tensor_copy(out=uS3, in_=uv_), reads=[b_uT], writes=[b_t1])

    def e_matmuls(ct, sl, dst_re, dst_im, bdst, col0):
        uv = uS3
        for kk in range(4):
            for ri, dst in enumerate((dst_re, dst_im)):
                pe_ = pacc[ri][:, 0:256]
                for s_ in range(8):
                    lag = 7 - s_
                    R.op(PE, lambda pe_=pe_, lag=lag, ri=ri, kk=kk, s_=s_: nc.tensor.matmul(pe_, lhsT=tBL[sl][:, 2 * lag + ri, kk * 128:(kk + 1) * 128], rhs=uv[:, s_, :],
                                                                                            start=(s_ == 0), stop=(s_ == 7)), reads=[b_tBL[sl], b_t1], writes=[b_pacc[ri]])
                R.op(ACT, lambda pe_=pe_, dst=dst, kk=kk: nc.scalar.copy(out=dst[:, kk, col0:col0 + 256], in_=pe_), reads=[b_pacc[ri]], writes=[bdst])

    def s5_prefix_tile(ct):
        sl = ct % 2
        R.dma(lambda: nc.sync.dma_start(out=tBL[sl], in_=tabB[ct].rearrange("r p n -> p r n")), b_tBL[sl], reads=[b_tabB], writes=[b_tBL[sl]])
        E = Epre[sl]
        bE = b_Epre[sl]
        deinterleave(ct)
        e_matmuls(ct, sl, E[0], E[1], bE, 0)
        inject(ct, E[0][:, :, 0], E[1][:, :, 0], Tpre[:, :, 0], [bE, b_Tpre], [bE, b_Tpre])
        rw = [bE, b_pw, b_Tpre]
        for m in range(8):
            w = 128 >> m
            src = (E[0], E[1]) if m % 2 == 0 else (E[2], E[3])
            dst = (E[2], E[3]) if m % 2 == 0 else (E[0], E[1])
            ev = [x_[:, :, 0:2 * w].rearrange("p a (k two) -> p a k two", two=2)[:, :, :, 0] for x_ in src]
            od = [x_[:, :, 0:2 * w].rearrange("p a (k two) -> p a k two", two=2)[:, :, :, 1] for x_ in src]
            dr_, di_ = dst[0][:, :, 0:w], dst[1][:, :, 0:w]
            t_ = Tpre[:, :, 0:w]
            Pr, Pi = pwb(ct, 8 + m, 0, w), pwb(ct, 8 + m, 1, w)
            VTT(t_, ev[0], Pr, ALU.mult, rw, [b_Tpre])
            VTT(dr_, t_, od[0], ALU.add, rw, [bE])
            VTT(t_, ev[1], Pi, ALU.mult, rw, [b_Tpre])
            VTT(dr_, dr_, t_, ALU.subtract, rw, [bE])
            VTT(t_, ev[1], Pr, ALU.mult, rw, [b_Tpre])
            VTT(di_, t_, od[1], ALU.add, rw, [bE])
            VTT(t_, ev[0], Pi, ALU.mult, rw, [b_Tpre])
            VTT(di_, di_, t_, ALU.add, rw, [bE])
        R.op(DVE, lambda: nc.vector.tensor_copy(out=Ss5[:, 4 * ct:4 * ct + 4, 0], in_=E[0][:, :, 0]), reads=[bE], writes=[b_Ss5])
        R.op(DVE, lambda: nc.vector.tensor_copy(out=Ss5[:, 4 * ct:4 * ct + 4, 1], in_=E[1][:, :, 0]), reads=[bE], writes=[b_Ss5])

    def s5_own_tile(ct):
        sl = ct % 2
        R.dma(lambda: nc.sync.dma_start(out=tBL[sl], in_=tabB[ct].rearrange("r p n -> p r n")), b_tBL[sl], reads=[b_tabB], writes=[b_tBL[sl]])
        R.dma(lambda: nc.sync.dma_start(out=tCL, in_=tabCL[ct].rearrange("r p n -> p r n")), b_tCL, reads=[b_tabC], writes=[b_tCL])
        R.dma(lambda: nc.sync.dma_start(out=tK[sl], in_=tabK[ct].rearrange("p (l n) -> p l n", n=128)), b_tK[sl], reads=[b_tabK], writes=[b_tK[sl]])
        uv = uT[:, ct, :].rearrange("p (j s) -> p s j", s=8)
        deinterleave(ct)
        for s_ in range(8):
            for lag in range(s_ + 1):
                R.op(PE, lambda s_=s_, lag=lag: nc.tensor.matmul(pgv[:, s_, :], lhsT=tK[sl][:, lag, :], rhs=uS3[:, s_ - lag, :], start=(lag == 0 and s_ % 2 == 0), stop=False),
                     reads=[b_tK[sl], b_t1], writes=[b_pg[s_ // 2]])
        e_matmuls(ct, sl, HA[0], HA[1], b_H, 128)
        inject(ct, HA[0][:, :, 128], HA[1][:, :, 128], Ths[:, :, 0], [b_H], [b_H])
        rw = [b_H, b_pw]

        def cmac(dre, dim_, sre_, sim_, m, w):
            Pr, Pi = pwb(ct, 8 + m, 0, w), pwb(ct, 8 + m, 1, w)
            t_ = Ths[:, :, 0:w]
            VTT(t_, sre_, Pr, ALU.mult, rw, [b_H])
            VTT(dre, dre, t_, ALU.add, rw, [b_H])
            VTT(t_, sim_, Pi, ALU.mult, rw, [b_H])
            VTT(dre, dre, t_, ALU.subtract, rw, [b_H])
            VTT(t_, sim_, Pr, ALU.mult, rw, [b_H])
            VTT(dim_, dim_, t_, ALU.add, rw, [b_H])
            VTT(t_, sre_, Pi, ALU.mult, rw, [b_H])
            VTT(dim_, dim_, t_, ALU.add, rw, [b_H])

        def sview(buf_, first, cnt, step):
            return buf_[:, :, 128 + first:128 + first + (cnt - 1) * step + 1:step]

        for m in range(8):
            d_ = 1 << m
            n_ = 256 // (2 * d_)
            cmac(sview(HA[0], 2 * d_ - 1, n_, 2 * d_), sview(HA[1], 2 * d_ - 1, n_, 2 * d_), sview(HA[0], d_ - 1, n_, 2 * d_), sview(HA[1], d_ - 1, n_, 2 * d_), m, n_)
        for m in range(6, -1, -1):
            d_ = 1 << m
            n_ = 256 // (2 * d_) - 1
            cmac(sview(HA[0], 3 * d_ - 1, n_, 2 * d_), sview(HA[1], 3 * d_ - 1, n_, 2 * d_), sview(HA[0], 2 * d_ - 1, n_, 2 * d_), sview(HA[1], 2 * d_ - 1, n_, 2 * d_), m, n_)
        for ri in range(2):
            R.op(DVE, lambda ri=ri: nc.vector.tensor_copy(out=HA[ri][:, :, 127], in_=Ss5[:, 4 * ct:4 * ct + 4, ri]), reads=[b_Ss5, b_H], writes=[b_H])
        for ri in range(2):
            R.op(DVE, lambda ri=ri: nc.vector.tensor_copy(out=Ss5[:, 4 * ct:4 * ct + 4, ri], in_=HA[ri][:, :, 383]), reads=[b_H], writes=[b_Ss5])
        R.op(ACT, lambda: nc.scalar.copy(out=carry_b[0], in_=HA[0][:, :, 127:383]), reads=[b_H], writes=[b_carry])
        R.op(ACT, lambda: nc.scalar.activation(out=carry_b[1], in_=HA[1][:, :, 127:383], func=AF.Copy, scale=-1.0), reads=[b_H], writes=[b_carry])
        for ri in range(2):
            R.op(DVE, lambda ri=ri: nc.vector.memset(HA[ri][:, :, 127:128], 0.0), reads=[b_carry], writes=[b_H])
        for kk in range(4):
            for s_ in range(8):
                for ri in range(2):
                    R.op(PE, lambda kk=kk, s_=s_, ri=ri: nc.tensor.matmul(pgv[:, s_, :], lhsT=tCL[:, 2 * s_ + ri, kk * 128:(kk + 1) * 128], rhs=carry_b[ri][:, kk, :],
                                                                         start=False, stop=(kk == 3 and ri == 1 and s_ % 2 == 1)), reads=[b_tCL, b_carry], writes=[b_pg[s_ // 2]])
        for nb in range(4):
            ya, sc = t1[:, 0:512], t1[:, 512:1024]
            ya3 = ya.rearrange("p (s j) -> p s j", j=256)
            uvs = uv[:, 2 * nb:2 * nb + 2, :]
            R.op(DVE, lambda nb=nb, ya3=ya3, uvs=uvs: nc.vector.scalar_tensor_tensor(out=ya3, in0=uvs, scalar=dcol[:, ct:ct + 1], in1=pg[:, nb, :].rearrange("p (s j) -> p s j", j=256),
                                                                                    op0=ALU.mult, op1=ALU.add), reads=[b_uT, b_dcol, b_pg[nb]], writes=[b_t1])
            R.op(ACT, lambda uvs=uvs, ya3=ya3: nc.scalar.activation(out=uvs, in_=ya3, func=AF.Gelu_apprx_tanh), reads=[b_t1], writes=[b_uT])

    def s5_segment(own):
        for ct2 in range(4):
            wb, b_wb = load_wblock(w_in0, ct2 * 256)
            for hf in range(2):
                ct = 2 * ct2 + hf
                for nb in range(4):
                    pa, b_pa = next_pacc()
                    for kt in range(8):
                        R.op(PE, lambda pa=pa, wb=wb, kt=kt, nb=nb, hf=hf: nc.tensor.matmul(pa, lhsT=wb[:, kt, hf * 128:(hf + 1) * 128], rhs=xnT[:, kt, nb * 512:(nb + 1) * 512],
                                                                                            start=(kt == 0), stop=(kt == 7)), reads=[b_xnT, b_wb], writes=[b_pa])
                    R.op(ACT, lambda pa=pa, ct=ct, nb=nb: nc.scalar.copy(out=uT[:, ct, nb * 512:(nb + 1) * 512], in_=pa), reads=[b_pa], writes=[b_uT])
        if own:
            for buf_ in HA + HB:
                R.op(DVE, lambda buf_=buf_: nc.vector.memset(buf_[:, :, 0:128], 0.0), reads=[], writes=[b_H])
        for ct in range(8):
            if own:
                s5_own_tile(ct)
            else:
                s5_prefix_tile(ct)

    def glu():
        gsb = carve_at(OFF_X, [128, 512], F32)
        zsb = carve_at(OFF_X + 1024, [128, 512], F32)
        b_g = k.buf("gsb")
        for jt2 in range(4):
            wg, b_wg = load_wblock(w_glu_d, jt2 * 256, gain=False)
            wa, b_wa = load_wblock(w_in0, 1024 + jt2 * 256)
            for hf in range(2):
                jt = 2 * jt2 + hf
                for nb in range(4):
                    pa, b_pa = next_pacc()
                    for kt in range(8):
                        R.op(PE, lambda pa=pa, wg=wg, kt=kt, nb=nb, hf=hf: nc.tensor.matmul(pa, lhsT=wg[:, kt, hf * 128:(hf + 1) * 128], rhs=uT[:, kt, nb * 512:(nb + 1) * 512],
                                                                                            start=(kt == 0), stop=(kt == 7)), reads=[b_uT, b_wg], writes=[b_pa])
                    R.op(ACT, lambda pa=pa, jt=jt: nc.scalar.activation(out=gsb, in_=pa, func=AF.Sigmoid, bias=dcol[:, 8 + jt:9 + jt]), reads=[b_pa, b_dcol], writes=[b_g])
                    pz, b_pz = next_pacc()
                    for kt in range(8):
                        R.op(PE, lambda pz=pz, wa=wa, kt=kt, nb=nb, hf=hf: nc.tensor.matmul(pz, lhsT=wa[:, kt, hf * 128:(hf + 1) * 128], rhs=xnT[:, kt, nb * 512:(nb + 1) * 512],
                                                                                            start=(kt == 0), stop=(kt == 7)), reads=[b_xnT, b_wa], writes=[b_pz])
                    R.op(ACT, lambda pz=pz: nc.scalar.activation(out=zsb, in_=pz, func=AF.Silu), reads=[b_pz], writes=[b_g])
                    R.op(DVE, lambda: nc.vector.tensor_tensor(out=gsb, in0=gsb, in1=zsb, op=ALU.mult), reads=[b_g], writes=[b_g])
                    R.op(DVE, lambda jt=jt, nb=nb: nc.vector.tensor_tensor(out=yT[:, jt, nb * 512:(nb + 1) * 512], in0=uT[:, jt, nb * 512:(nb + 1) * 512], in1=gsb, op=ALU.mult),
                         reads=[b_g, b_uT], writes=[b_yT])

    G128 = [(1.0 - 2.0 ** (-5 - h)) ** 128 for h in range(4)]

    def ret_segment(seg, own):
        cosT = carve_at(OFF_X, [128, SEG], F32)
        sinT = carve_at(OFF_X + 4096, [128, SEG], F32)
        b_rot = k.buf("rot")
        R.dma(lambda: nc.sync.dma_start(out=cosT, in_=rot_d[seg, 0]), b_rot, writes=[b_rot])
        R.dma(lambda: nc.sync.dma_start(out=sinT, in_=rot_d[seg, 1]), b_rot, writes=[b_rot])
        DTt = carve_at(OFF_XB, [128, 4, 128], F32)
        xz = carve_at(OFF_XB + 1024, [128, 8], F32)
        gng = carve_at(OFF_XB + 1040, [128, 1024], F32)
        b_cst = k.buf("rcst")
        R.dma(lambda: nc.sync.dma_start(out=DTt, in_=dt_d), b_cst, writes=[b_cst])
        R.dma(lambda: nc.sync.dma_start(out=xz, in_=xizs_d), b_cst, writes=[b_cst])
        zsc = carve_at(OFF_XB + 1040 + 2048, [128, 64], F32)
        R.dma(lambda: nc.sync.dma_start(out=zsc, in_=zsc_d), b_cst, writes=[b_cst])
        R.dma(lambda: nc.sync.dma_start(out=gng, in_=gn_gain_d.partition_broadcast(128)), b_cst, writes=[b_cst])
        eo = [OFF_UT]

        def et(shape, dt):
            n = 1
            for d_ in shape[1:]:
                n *= d_
            n16 = n * (2 if dt == F32 else 1)
            v = carve_at(eo[0], shape, dt)
            eo[0] += n16
            assert eo[0] <= OFF_UT + 16384
            return v

        TT = []
        for pp in range(2):
            d_ = dict(PT=et([128, 128], BF16), o_sb=et([128, 256], F32), in_sb=et([128, 256], F32), ktok=et([128, 256], BF16), v_bf=et([128, 256], BF16),
                      vz_bf=et([128, 256], BF16), yb=et([128, 256], BF16), sbz=et([128, 256], F32), bstt=et([128, 6], F32), mvv=et([128, 4], F32))
            for nm in ("b_PT", "b_o", "b_in", "b_ktok", "b_v", "b_yb", "b_sbz", "b_bs"):
                d_[nm] = k.buf()
            d_["b_psc"], d_["b_pin"], d_["b_pcr"] = b_pg[0], b_pg[1], b_pg[2]
            tb_, btb_ = (pT[0], b_pT[0]) if pp == 0 else (pT2, b_pT2)
            d_["b_ptk"], d_["b_pty"] = btb_, btb_
            d_["ptk"] = tb_[:, 0:2, :]
            d_["pty"] = tb_[:, 2:4, :]
            d_["psc"] = pg[:, 0, 0:128]
            d_["pin"] = pg[:, 1, 0:256]
            d_["pcr"] = pg[:, 2, 0:256]
            if own:
                d_["b_pv"], d_["b_pz"] = b_pacc[0], b_pacc[1]
                d_["pv"] = pacc[0][:, 0:256]
                d_["pz"] = pacc[1][:, 0:256]
                d_["plc"], d_["b_plc"] = pg[:, 3, :], b_pg[3]
            else:
                d_["b_pv"], d_["b_pz"] = b_pacc[pp], None
                d_["pv"] = pacc[pp][:, 0:256]
                d_["pz"] = None
                d_["plc"], d_["b_plc"] = pg[:, 2 + pp, :], b_pg[2 + pp]
            TT.append(d_)
        hb = [[b_pacc[0]], [b_pacc[1]]]

        rbanks = [((pacc[0], b_pacc[0]), (pacc[1], b_pacc[1])), ((pg[:, 0, :], b_pg[0]), (pg[:, 1, :], b_pg[1])), ((pg[:, 2, :], b_pg[2]), (pg[:, 3, :], b_pg[3]))]
        rctr = [0]

        def rotary_proj(w, b_w, dstT, b_dst):
            for nb in range(4):
                blk = slice(nb * 512, (nb + 1) * 512)
                bk = rbanks[rctr[0] % 3]
                rctr[0] += 1
                for hf in range(2):
                    ps_, bps_ = bk[hf]
                    for kt in range(8):
                        R.op(PE, lambda hf=hf, kt=kt, blk=blk, ps_=ps_: nc.tensor.matmul(ps_, lhsT=w[:, kt, hf * 128:(hf + 1) * 128], rhs=xnT[:, kt, blk], start=(kt == 0), stop=(kt == 7)),
                             reads=[b_xnT, b_w], writes=[bps_])
                (p0, bp0), (p1, bp1) = bk
                ta, tb = t1[:, 0:512], t1[:, 512:1024]
                R.op(DVE, lambda blk=blk, p0=p0: nc.vector.tensor_tensor(out=ta, in0=p0, in1=cosT[:, blk], op=ALU.mult), reads=[bp0, b_rot], writes=[b_t1])
                R.op(DVE, lambda blk=blk, p1=p1: nc.vector.tensor_tensor(out=tb, in0=p1, in1=sinT[:, blk], op=ALU.mult), reads=[bp1, b_rot], writes=[b_t1])
                R.op(DVE, lambda blk=blk: nc.vector.tensor_tensor(out=dstT[:, 0, blk], in0=ta, in1=tb, op=ALU.subtract), reads=[b_t1], writes=[b_dst])
                R.op(DVE, lambda blk=blk, p0=p0: nc.vector.tensor_tensor(out=ta, in0=p0, in1=sinT[:, blk], op=ALU.mult), reads=[bp0, b_rot, b_dst], writes=[b_t1])
                R.op(DVE, lambda blk=blk, p1=p1: nc.vector.tensor_tensor(out=tb, in0=p1, in1=cosT[:, blk], op=ALU.mult), reads=[bp1, b_rot], writes=[b_t1])
                R.op(DVE, lambda blk=blk: nc.vector.tensor_tensor(out=dstT[:, 1, blk], in0=ta, in1=tb, op=ALU.add), reads=[b_t1], writes=[b_dst])

        for h in range(4):
            wk, b_wk = load_wblock(w_in0, 3072 + h * 256)
            rotary_proj(wk, b_wk, krT, b_krT)
            if own:
                wq, b_wq = load_wblock(w_in0, 2048 + h * 256)
                rotary_proj(wq, b_wq, qrT, b_qrT)
            wv, b_wv = load_wblock(w_in0, 4096 + h * 256)
            if own:
                wz, b_wz = load_wblock(w_in0, 5120 + h * 256)
            for c in range(NCH):
                ch = slice(c * 128, (c + 1) * 128)
                T = TT[c % 2]
                for kt in range(8):
                    R.op(PE, lambda kt=kt, ch=ch, T=T, wv=wv: nc.tensor.matmul(T["pv"], lhsT=xnT[:, kt, ch], rhs=wv[:, kt, :], start=(kt == 0), stop=(kt == 7)),
                         reads=[b_xnT, b_wv], writes=[T["b_pv"]])
                R.op(ACT, lambda T=T: nc.scalar.copy(out=T["v_bf"], in_=T["pv"]), reads=[T["b_pv"]], writes=[T["b_v"]])
                vsc = xz[:, 4 + h:5 + h] if own else zsc[:, 16 * h + c:16 * h + c + 1]
                R.op(ACT, lambda vsc=vsc, T=T: nc.scalar.activation(out=T["vz_bf"], in_=T["pv"], func=AF.Copy, scale=vsc), reads=[T["b_pv"], b_cst], writes=[T["b_v"]])
                for tl in range(2):
                    R.op(PE, lambda tl=tl, ch=ch, T=T: nc.tensor.transpose(out=T["ptk"][:, tl, :], in_=krT[:, tl, ch], identity=ident_b), reads=[b_krT, b_ident], writes=[T["b_ptk"]])
                R.op(DVE, lambda T=T: nc.vector.tensor_copy(out=T["ktok"].rearrange("p (a b) -> p a b", b=128), in_=T["ptk"]), reads=[T["b_ptk"]], writes=[T["b_ktok"]])
                if own:
                    for tl in range(2):
                        R.op(PE, lambda tl=tl, ch=ch, T=T: nc.tensor.matmul(T["psc"], lhsT=krT[:, tl, ch], rhs=qrT[:, tl, ch], start=(tl == 0), stop=(tl == 1)),
                             reads=[b_krT, b_qrT], writes=[T["b_psc"]])
                    R.op(DVE, lambda h=h, T=T: nc.vector.tensor_tensor(out=T["PT"], in0=T["psc"], in1=DTt[:, h, :], op=ALU.mult), reads=[T["b_psc"], b_cst], writes=[T["b_PT"]])
                    R.op(PE, lambda T=T: nc.tensor.matmul(T["pin"], lhsT=T["PT"], rhs=T["v_bf"], start=True, stop=True), reads=[T["b_PT"], T["b_v"]], writes=[T["b_pin"]])
                    for tl in range(2):
                        R.op(PE, lambda tl=tl, ch=ch, h=h, T=T: nc.tensor.matmul(T["pcr"], lhsT=qrT[:, tl, ch], rhs=Sbf[:, h, tl * 256:(tl + 1) * 256], start=(tl == 0), stop=(tl == 1)),
                             reads=[b_qrT, b_Sbf], writes=[T["b_pcr"]])
                    R.op(ACT, lambda T=T: nc.scalar.copy(out=T["in_sb"], in_=T["pin"]), reads=[T["b_pin"]], writes=[T["b_in"]])
                    R.op(DVE, lambda h=h, T=T: nc.vector.scalar_tensor_tensor(out=T["o_sb"], in0=T["pcr"], scalar=xz[:, h:h + 1], in1=T["in_sb"], op0=ALU.mult, op1=ALU.add),
                         reads=[T["b_pcr"], b_cst, T["b_in"]], writes=[T["b_o"]])
                    R.op(DVE, lambda T=T: nc.vector.bn_stats(out=T["bstt"], in_=T["o_sb"]), reads=[T["b_o"]], writes=[T["b_bs"]])
                    R.op(DVE, lambda T=T: nc.vector.bn_aggr(out=T["mvv"][:, 0:2], in_=T["bstt"]), reads=[T["b_bs"]], writes=[T["b_bs"]])
                    R.op(DVE, lambda T=T: nc.vector.tensor_scalar(out=T["mvv"][:, 3:4], in0=T["mvv"][:, 1:2], scalar1=EPS, scalar2=None, op0=ALU.add), reads=[T["b_bs"]], writes=[T["b_bs"]])
                    R.op(ACT, lambda T=T: nc.scalar.activation(out=T["mvv"][:, 3:4], in_=T["mvv"][:, 3:4], func=AF.Sqrt), reads=[T["b_bs"]], writes=[T["b_bs"]])
                    R.op(DVE, lambda T=T: nc.vector.reciprocal(out=T["mvv"][:, 2:3], in_=T["mvv"][:, 3:4]), reads=[T["b_bs"]], writes=[T["b_bs"]])
                    R.op(DVE, lambda T=T: nc.vector.tensor_scalar(out=T["o_sb"], in0=T["o_sb"], scalar1=T["mvv"][:, 0:1], scalar2=T["mvv"][:, 2:3], op0=ALU.subtract, op1=ALU.mult),
                         reads=[T["b_o"], T["b_bs"]], writes=[T["b_o"]])
                    for kt in range(8):
                        R.op(PE, lambda kt=kt, ch=ch, T=T, wz=wz: nc.tensor.matmul(T["pz"], lhsT=xnT[:, kt, ch], rhs=wz[:, kt, :], start=(kt == 0), stop=(kt == 7)),
                             reads=[b_xnT, b_wz], writes=[T["b_pz"]])
                    R.op(ACT, lambda T=T: nc.scalar.activation(out=T["sbz"], in_=T["pz"], func=AF.Silu), reads=[T["b_pz"]], writes=[T["b_sbz"]])
                    R.op(DVE, lambda h=h, T=T: nc.vector.tensor_tensor(out=T["o_sb"], in0=T["o_sb"], in1=gng[:, h * 256:(h + 1) * 256], op=ALU.mult), reads=[T["b_o"], b_cst], writes=[T["b_o"]])
                    R.op(DVE, lambda T=T: nc.vector.tensor_tensor(out=T["yb"], in0=T["o_sb"], in1=T["sbz"], op=ALU.mult), reads=[T["b_o"], T["b_sbz"]], writes=[T["b_yb"]])
                    for tl in range(2):
                        R.op(PE, lambda tl=tl, T=T: nc.tensor.transpose(out=T["pty"][:, tl, :], in_=T["yb"][:, tl * 128:(tl + 1) * 128], identity=ident_b), reads=[T["b_yb"], b_ident], writes=[T["b_pty"]])
                    R.op(DVE, lambda h=h, ch=ch, T=T: nc.vector.tensor_copy(out=yT[:, 2 * h:2 * h + 2, ch], in_=T["pty"]), reads=[T["b_pty"]], writes=[b_yT])
                if own:
                    plc, b_plc = T["plc"], T["b_plc"]
                    for tl in range(2):
                        R.op(PE, lambda tl=tl, T=T, plc=plc: nc.tensor.matmul(plc[:, tl * 256:(tl + 1) * 256], lhsT=T["ktok"][:, tl * 128:(tl + 1) * 128], rhs=T["vz_bf"], start=(tl == 0), stop=(tl == 1)),
                             reads=[T["b_ktok"], T["b_v"]], writes=[b_plc])
                    R.op(DVE, lambda h=h, plc=plc: nc.vector.scalar_tensor_tensor(out=Sret[:, h, :], in0=Sret[:, h, :], scalar=G128[h], in1=plc, op0=ALU.mult, op1=ALU.add),
                         reads=[b_Sret, b_plc], writes=[b_Sret])
                    R.op(ACT, lambda h=h: nc.scalar.copy(out=Sbf[:, h, :], in_=Sret[:, h, :]), reads=[b_Sret], writes=[b_Sbf])
                else:
                    plc, b_plc = pg[:, 3, :], b_pg[3]
                    for tl in range(2):
                        R.op(PE, lambda tl=tl, T=T, plc=plc, c=c: nc.tensor.matmul(plc[:, tl * 256:(tl + 1) * 256], lhsT=T["ktok"][:, tl * 128:(tl + 1) * 128], rhs=T["vz_bf"],
                                                                                  start=(c == 0 and tl == 0), stop=(c == NCH - 1 and tl == 1)),
                             reads=[T["b_ktok"], T["b_v"]], writes=[b_plc])
                    if c == NCH - 1:
                        R.op(DVE, lambda h=h, plc=plc: nc.vector.scalar_tensor_tensor(out=Sret[:, h, :], in0=Sret[:, h, :], scalar=G128[h] ** 16, in1=plc, op0=ALU.mult, op1=ALU.add),
                             reads=[b_Sret, b_plc], writes=[b_Sret])
                        R.op(ACT, lambda h=h: nc.scalar.copy(out=Sbf[:, h, :], in_=Sret[:, h, :]), reads=[b_Sret], writes=[b_Sbf])

    def prefix_segment(seg):
        for ct2 in range(4):
            wb, b_wb = load_wblock(w_in0, ct2 * 256)
            for hf in range(2):
                ct = 2 * ct2 + hf
                for nb in range(4):
                    pa, b_pa = next_pacc()
                    for kt in range(8):
                        R.op(PE, lambda pa=pa, wb=wb, kt=kt, nb=nb, hf=hf: nc.tensor.matmul(pa, lhsT=wb[:, kt, hf * 128:(hf + 1) * 128], rhs=xnT[:, kt, nb * 512:(nb + 1) * 512],
                                                                                            start=(kt == 0), stop=(kt == 7)), reads=[b_xnT, b_wb], writes=[b_pa])
                    R.op(ACT, lambda pa=pa, ct=ct, nb=nb: nc.scalar.copy(out=uT[:, ct, nb * 512:(nb + 1) * 512], in_=pa), reads=[b_pa], writes=[b_uT])
        yTf = yT.rearrange("p a b -> p (a b)")
        tbl = yTf[:, 0:8192].rearrange("p (l n) -> p l n", n=512)
        b_tbl = k.buf("ptbl")
        cosT = yTf[:, 8192:12288].bitcast(F32)
        sinT = yTf[:, 12288:16384].bitcast(F32)
        b_rot = k.buf("prot")
        R.dma(lambda: nc.sync.dma_start(out=cosT, in_=rot_d[seg, 0]), b_rot, writes=[b_rot])
        R.dma(lambda: nc.sync.dma_start(out=sinT, in_=rot_d[seg, 1]), b_rot, writes=[b_rot])
        zsc = carve_at(OFF_XB + 1040 + 2048, [128, 64], F32)
        b_cst = k.buf("pcst")
        R.dma(lambda: nc.sync.dma_start(out=zsc, in_=zsc_d), b_cst, writes=[b_cst])
        E = [carve_at(OFF_X + 2048 * j, [128, 4, 256], F32) for j in range(4)]
        bE = k.buf("pE")
        TTp = []
        for pp in range(2):
            base = OFF_E + pp * 768
            d_ = dict(ktok=carve_at(base, [128, 256], BF16), v_bf=carve_at(base + 256, [128, 256], BF16), vz_bf=carve_at(base + 512, [128, 256], BF16),
                      b_ktok=k.buf(), b_v=k.buf())
            d_["pv"], d_["b_pv"] = pacc[pp][:, 0:256], b_pacc[pp]
            tb_, btb_ = (pT[0], b_pT[0]) if pp == 0 else (pT2, b_pT2)
            d_["ptk"], d_["b_ptk"] = tb_[:, 0:2, :], btb_
            TTp.append(d_)
        plc, b_plc = pg[:, 3, :], b_pg[3]
        pE = [pg[:, 0, 0:256], pg[:, 1, 0:256]]
        bpE = [b_pg[0], b_pg[1]]
        rb = [((pacc[0], b_pacc[0]), (pacc[1], b_pacc[1])), ((pg[:, 2, :], b_pg[2]), (pg[:, 3, :], b_pg[3]))]
        rc = [0]

        def rotary_k(w, b_w):
            for nb in range(4):
                blk = slice(nb * 512, (nb + 1) * 512)
                bk = rb[rc[0] % 2]
                rc[0] += 1
                for hf in range(2):
                    ps_, bps_ = bk[hf]
                    for kt in range(8):
                        R.op(PE, lambda hf=hf, kt=kt, blk=blk, ps_=ps_: nc.tensor.matmul(ps_, lhsT=w[:, kt, hf * 128:(hf + 1) * 128], rhs=xnT[:, kt, blk], start=(kt == 0), stop=(kt == 7)),
                             reads=[b_xnT, b_w], writes=[bps_])
                (p0, bp0), (p1, bp1) = bk
                ta, tb = Tpre.rearrange("p a b -> p (a b)"), carve_at(OFF_XB + 3216, [128, 512], F32) if False else t1[:, 512:1024]
                ta = t1[:, 0:512]
                R.op(DVE, lambda blk=blk, p0=p0: nc.vector.tensor_tensor(out=ta, in0=p0, in1=cosT[:, blk], op=ALU.mult), reads=[bp0, b_rot], writes=[b_t1])
                R.op(DVE, lambda blk=blk, p1=p1: nc.vector.tensor_tensor(out=tb, in0=p1, in1=sinT[:, blk], op=ALU.mult), reads=[bp1, b_rot], writes=[b_t1])
                R.op(DVE, lambda blk=blk: nc.vector.tensor_tensor(out=krT[:, 0, blk], in0=ta, in1=tb, op=ALU.subtract), reads=[b_t1], writes=[b_krT])
                R.op(DVE, lambda blk=blk, p0=p0: nc.vector.tensor_tensor(out=ta, in0=p0, in1=sinT[:, blk], op=ALU.mult), reads=[bp0, b_rot, b_krT], writes=[b_t1])
                R.op(DVE, lambda blk=blk, p1=p1: nc.vector.tensor_tensor(out=tb, in0=p1, in1=cosT[:, blk], op=ALU.mult), reads=[bp1, b_rot], writes=[b_t1])
                R.op(DVE, lambda blk=blk: nc.vector.tensor_tensor(out=krT[:, 1, blk], in0=ta, in1=tb, op=ALU.add), reads=[b_t1], writes=[b_krT])

        def tile_steps(ct):
            R.dma(lambda: nc.sync.dma_start(out=tbl, in_=tabB[ct].rearrange("r p n -> p r n")), b_tbl, reads=[b_tabB], writes=[b_tbl])
            deinterleave(ct)
            for kk in range(4):
                for ri in range(2):
                    for s_ in range(8):
                        lag = 7 - s_
                        R.op(PE, lambda lag=lag, ri=ri, kk=kk, s_=s_: nc.tensor.matmul(pE[ri], lhsT=tbl[:, 2 * lag + ri, kk * 128:(kk + 1) * 128], rhs=uS3[:, s_, :],
                                                                                       start=(s_ == 0), stop=(s_ == 7)), reads=[b_tbl, b_t1], writes=[bpE[ri]])
                    R.op(ACT, lambda ri=ri, kk=kk: nc.scalar.copy(out=E[ri][:, kk, :], in_=pE[ri]), reads=[bpE[ri]], writes=[bE])
                if kk % 2 == 1:
                    yield
            inject(ct, E[0][:, :, 0], E[1][:, :, 0], Tpre[:, :, 0], [bE, b_Tpre], [bE, b_Tpre])
            yield
            rw = [bE, b_pw, b_Tpre]
            for m in range(8):
                w = 128 >> m
                src = (E[0], E[1]) if m % 2 == 0 else (E[2], E[3])
                dst = (E[2], E[3]) if m % 2 == 0 else (E[0], E[1])
                ev = [x_[:, :, 0:2 * w].rearrange("p a (k two) -> p a k two", two=2)[:, :, :, 0] for x_ in src]
                od = [x_[:, :, 0:2 * w].rearrange("p a (k two) -> p a k two", two=2)[:, :, :, 1] for x_ in src]
                dr_, di_ = dst[0][:, :, 0:w], dst[1][:, :, 0:w]
                t_ = Tpre[:, :, 0:w]
                Pr, Pi = pwb(ct, 8 + m, 0, w), pwb(ct, 8 + m, 1, w)
                VTT(t_, ev[0], Pr, ALU.mult, rw, [b_Tpre])
                VTT(dr_, t_, od[0], ALU.add, rw, [bE])
                VTT(t_, ev[1], Pi, ALU.mult, rw, [b_Tpre])
                VTT(dr_, dr_, t_, ALU.subtract, rw, [bE])
                if m < 3:
                    yield
                VTT(t_, ev[1], Pr, ALU.mult, rw, [b_Tpre])
                VTT(di_, t_, od[1], ALU.add, rw, [bE])
                VTT(t_, ev[0], Pi, ALU.mult, rw, [b_Tpre])
                VTT(di_, di_, t_, ALU.add, rw, [bE])
                yield
            R.op(DVE, lambda: nc.vector.tensor_copy(out=Ss5[:, 4 * ct:4 * ct + 4, 0], in_=E[0][:, :, 0]), reads=[bE], writes=[b_Ss5])
            R.op(DVE, lambda: nc.vector.tensor_copy(out=Ss5[:, 4 * ct:4 * ct + 4, 1], in_=E[1][:, :, 0]), reads=[bE], writes=[b_Ss5])

        def chunk(h, c, wv, b_wv):
            ch = slice(c * 128, (c + 1) * 128)
            T = TTp[c % 2]
            for kt in range(8):
                R.op(PE, lambda kt=kt, ch=ch, T=T, wv=wv: nc.tensor.matmul(T["pv"], lhsT=xnT[:, kt, ch], rhs=wv[:, kt, :], start=(kt == 0), stop=(kt == 7)),
                     reads=[b_xnT, b_wv], writes=[T["b_pv"]])
            vsc = zsc[:, 16 * h + c:16 * h + c + 1]
            R.op(ACT, lambda vsc=vsc, T=T: nc.scalar.activation(out=T["vz_bf"], in_=T["pv"], func=AF.Copy, scale=vsc), reads=[T["b_pv"], b_cst], writes=[T["b_v"]])
            for tl in range(2):
                R.op(PE, lambda tl=tl, ch=ch, T=T: nc.tensor.transpose(out=T["ptk"][:, tl, :], in_=krT[:, tl, ch], identity=ident_b), reads=[b_krT, b_ident], writes=[T["b_ptk"]])
            R.op(ACT, lambda T=T: nc.scalar.copy(out=T["ktok"].rearrange("p (a b) -> p a b", b=128), in_=T["ptk"]), reads=[T["b_ptk"]], writes=[T["b_ktok"]])
            for tl in range(2):
                R.op(PE, lambda tl=tl, T=T, c=c: nc.tensor.matmul(plc[:, tl * 256:(tl + 1) * 256], lhsT=T["ktok"][:, tl * 128:(tl + 1) * 128], rhs=T["vz_bf"],
                                                                 start=(c == 0 and tl == 0), stop=(c == NCH - 1 and tl == 1)),
                     reads=[T["b_ktok"], T["b_v"]], writes=[b_plc])
            if c == NCH - 1:
                R.op(DVE, lambda h=h: nc.vector.scalar_tensor_tensor(out=Sret[:, h, :], in0=Sret[:, h, :], scalar=G128[h] ** 16, in1=plc, op0=ALU.mult, op1=ALU.add),
                     reads=[b_Sret, b_plc], writes=[b_Sret])
                R.op(ACT, lambda h=h: nc.scalar.copy(out=Sbf[:, h, :], in_=Sret[:, h, :]), reads=[b_Sret], writes=[b_Sbf])

        for h in range(4):
            wk, b_wk = load_wblock(w_in0, 3072 + h * 256)
            rotary_k(wk, b_wk)
            wv, b_wv = load_wblock(w_in0, 4096 + h * 256)
            for half in range(2):
                gen = tile_steps(2 * h + half)
                for c in range(8 * half, 8 * half + 8):
                    chunk(h, c, wv, b_wv)
                    for _ in range(2):
                        try:
                            next(gen)
                        except StopIteration:
                            break
                for _ in gen:
                    pass

    def layer0(dst_dram):
        own_x = x_seq[3 * SEG:4 * SEG, :]
        load_gain(norm_even)
        s5_precompute()
        R.op(DVE, lambda: nc.vector.memset(Sret, 0.0), reads=[], writes=[b_Sret])
        R.op(DVE, lambda: nc.vector.memset(Sbf, 0.0), reads=[], writes=[b_Sbf])
        R.op(DVE, lambda: nc.vector.memset(Ss5, 0.0), reads=[], writes=[b_Ss5])
        wo0 = carve_at(OFF_X, [128, 8, D], BF16)
        for seg in range(SEG0, NSEG):
            own = seg == NSEG - 1
            norm_transpose(x_seq, seg * SEG)
            R.barrier()
            if not own:
                prefix_segment(seg)
                R.barrier()
                continue
            s5_segment(own)
            if own:
                R.barrier()
                glu()
                R.barrier()
                b_wo0 = k.buf("wo0a")
                out_proj_half(w_out0, 0, own_x, dst_dram, wo0, b_wo0)
            R.barrier()
            ret_segment(seg, own)
            if own:
                R.barrier()
                b_wo0 = k.buf("wo0b")
                out_proj_half(w_out0, 1024, dst_dram, dst_dram, wo0, b_wo0)
            R.barrier()

    def final(src_dram):
        R.barrier()
        apos[0] = 0
        fg = carve([128, D], F32)
        b_fg = k.buf("fg")
        R.dma(lambda: nc.sync.dma_start(out=fg, in_=final_norm.partition_broadcast(128)), b_fg, writes=[b_fg])
        for c in range(NCH):
            i = c % 2
            key = (id(src_dram), c)
            R.dma(lambda i=i, c=c: nc.sync.dma_start(out=xc[i], in_=src_dram[c * 128:(c + 1) * 128, :]), b_xc[i],
                  reads=[b_x[key]] if key in b_x else [], writes=[b_xc[i]])
            R.op(ACT, lambda i=i: nc.scalar.activation(out=junk, in_=xc[i], func=AF.Square, accum_out=stat[i][:, 0:1]),
                 reads=[b_xc[i]], writes=[b_junk, b_stat[i]])
            R.op(DVE, lambda i=i: nc.vector.tensor_scalar(out=stat[i][:, 1:2], in0=stat[i][:, 0:1], scalar1=1.0 / D, scalar2=EPS,
                                                          op0=ALU.mult, op1=ALU.add), reads=[b_stat[i]], writes=[b_stat[i]])
            R.op(ACT, lambda i=i: nc.scalar.activation(out=stat[i][:, 3:4], in_=stat[i][:, 1:2], func=AF.Sqrt), reads=[b_stat[i]], writes=[b_stat[i]])
            R.op(DVE, lambda i=i: nc.vector.reciprocal(out=stat[i][:, 2:3], in_=stat[i][:, 3:4]), reads=[b_stat[i]], writes=[b_stat[i]])
            R.op(DVE, lambda i=i: nc.vector.scalar_tensor_tensor(out=xc[i], in0=xc[i], scalar=stat[i][:, 2:3], in1=fg, op0=ALU.mult, op1=ALU.mult),
                 reads=[b_xc[i], b_stat[i], b_fg], writes=[b_xc[i]])
            R.dma(lambda i=i, c=c: nc.sync.dma_start(out=out[c * 128:(c + 1) * 128, :], in_=xc[i]), b_xc[i], reads=[b_xc[i]], writes=[])

    SEG0 = 0 if mode != "l0own" else 3
    if mode == "l1":
        layer1(x_seq[3 * SEG:4 * SEG, :], x2)
        final(x2)
    elif mode == "pre":
        load_gain(norm_even)
        s5_precompute()
        for c in range(NCH):
            i = c % 2
            R.dma(lambda i=i, c=c: nc.sync.dma_start(out=xc[i], in_=x_seq[3 * SEG + c * 128:3 * SEG + (c + 1) * 128, :]), b_xc[i], writes=[b_xc[i]])
            R.dma(lambda i=i, c=c: nc.sync.dma_start(out=out[c * 128:(c + 1) * 128, :], in_=xc[i]), b_xc[i], reads=[b_xc[i]], writes=[])
    elif mode in ("l0", "l0own"):
        layer0(x1)
        R.barrier()
        for c in range(NCH):
            i = c % 2
            R.dma(lambda i=i, c=c: nc.sync.dma_start(out=xc[i], in_=x1[c * 128:(c + 1) * 128, :]), b_xc[i], reads=[b_x[(id(x1), c)]], writes=[b_xc[i]])
            R.dma(lambda i=i, c=c: nc.sync.dma_start(out=out[c * 128:(c + 1) * 128, :], in_=xc[i]), b_xc[i], reads=[b_xc[i]], writes=[])
    else:
        layer0(x1)
        layer1(x1, x2)
        final(x2)
    rec.emit()
    return nc


_CACHE = {}


def _consts(q):
    tril = np.triu(np.ones((128, 128), np.float32))
    ident = np.eye(128, dtype=np.float32)
    half = 128
    inv = 10000.0 ** (-np.arange(half, dtype=np.float64) / half)
    pos = (q * SEG - (NSEG - 1) * SEG) + np.arange(NSEG * SEG, dtype=np.float64)
    ang = inv[:, None] * pos[None, :]
    rot = np.stack([np.cos(ang), np.sin(ang)], 0)
    rot = rot.reshape(2, 128, NSEG, SEG).transpose(2, 0, 1, 3).astype(np.float32)
    gam = 1.0 - 2.0 ** (-5.0 - np.arange(4, dtype=np.float64))
    idx = np.arange(128, dtype=np.float64)
    diff = idx[None, :] - idx[:, None]
    dtab = np.where(diff[:, None, :] >= 0, gam[None, :, None] ** np.maximum(diff[:, None, :], 0.0), 0.0) * (256.0 ** -0.5)
    xi = gam[None, :] ** (idx[:, None] + 1.0)
    zs = gam[None, :] ** (127.0 - idx[:, None]) * (256.0 ** -0.5)
    cc = np.arange(16, dtype=np.float64)
    zsc = zs[:, :, None] * (gam[None, :, None] ** (128.0 * (15.0 - cc[None, None, :])))
    return {"c_tril": tril, "c_ident": ident, "c_rot": np.ascontiguousarray(rot), "c_dt": dtab.astype(np.float32),
            "c_xizs": np.concatenate([xi, zs], 1).astype(np.float32), "c_zsc": zsc.reshape(128, 64).astype(np.float32)}


def make_maps(inputs):
    x = np.asarray(inputs["x"], np.float32)
    sq = lambda n: np.ascontiguousarray(np.asarray(inputs[n], np.float32)[0])
    shared = {n: sq(n) for n in ["norm_even", "w_in_even", "s5_lam_re", "s5_lam_im", "s5_log_dt", "s5_b_re", "s5_b_im", "s5_c_re", "s5_c_im",
                                 "s5_d", "s5_w_glu", "s5_b_glu", "ret_gn_gain", "w_out_even", "norm_odd", "w_in_odd", "sgu_norm_gain",
                                 "sgu_w_spatial", "sgu_b_spatial", "w_out_odd"]}
    shared["final_norm"] = np.ascontiguousarray(np.asarray(inputs["final_norm"], np.float32))
    maps = []
    for c in range(8):
        b, q = c // 4, c % 4
        xs = np.zeros((NSEG * SEG, D), np.float32)
        lo = q * SEG - (NSEG - 1) * SEG
        src = x[b, max(lo, 0):(q + 1) * SEG]
        xs[NSEG * SEG - src.shape[0]:] = src
        m = dict(shared)
        m["x_seq"] = xs
        m.update(_consts(q))
        maps.append(m)
    return maps


def kernel(**inputs):
    maps = make_maps(inputs)
    nc = bass.Bass("TRN2", target_bir_lowering=False)
    build(nc, "full")
    res = run_bass_kernel_spmd(nc, maps, core_ids=list(range(8)))
    out = np.stack([np.asarray(r["out"], np.float32) for r in res.results]).reshape(2, 4 * SEG, D)
    return out
```
